# Optimizing a Trainium2 kernel written in Bass

```python
import math
import jax, jax.numpy as jnp
from jax import lax
import numpy as np

D_MODEL = 1024
BATCH = 16
SEQ = 2048
DEPTH = 1

N_HEADS = 8
HEAD_DIM = 64
QK_WIDTH = N_HEADS * 2 * HEAD_DIM
V_HEAD_DIM = 2 * HEAD_DIM
V_WIDTH = N_HEADS * V_HEAD_DIM
ROT_DIM = HEAD_DIM // 4
ROPE_THETA = 500000.0
Q_BLOCK = 128

CONV_C = D_MODEL
CONV_W = 31

N_BRANCH = 2
D_FF = int(math.ceil(8 * D_MODEL / 3 / 256) * 256)

IN_COLS = 2 * CONV_C + 2 * QK_WIDTH + V_WIDTH + N_BRANCH * D_MODEL

kernel_name = "hybrid_conformer_diffattn_gated_block"


def rmsnorm(x, g, eps=1e-6):
    xf = x.astype(jnp.float32)
    y = xf * lax.rsqrt(jnp.mean(xf * xf, axis=-1, keepdims=True) + eps)
    return (y * g.astype(jnp.float32)).astype(x.dtype)


def layernorm(x, g, b, eps=1e-5):
    xf = x.astype(jnp.float32)
    mu = jnp.mean(xf, axis=-1, keepdims=True)
    xc = xf - mu
    var = jnp.mean(xc * xc, axis=-1, keepdims=True)
    y = xc * lax.rsqrt(var + eps) * g.astype(jnp.float32) + b.astype(jnp.float32)
    return y.astype(x.dtype)


def rope_tables(seq_len):
    inv_freq = ROPE_THETA ** (-jnp.arange(0, ROT_DIM, 2, dtype=jnp.float32) / ROT_DIM)
    pos = jnp.arange(seq_len, dtype=jnp.float32)
    ang = pos[:, None] * inv_freq[None, :]
    return jnp.cos(ang)[:, None, None, :], jnp.sin(ang)[:, None, None, :]


def apply_partial_rope(t, cos, sin):
    cos = cos.astype(t.dtype)
    sin = sin.astype(t.dtype)
    tr, tp = t[..., :ROT_DIM], t[..., ROT_DIM:]
    t1, t2 = tr[..., : ROT_DIM // 2], tr[..., ROT_DIM // 2:]
    rot = jnp.concatenate([t1 * cos - t2 * sin, t2 * cos + t1 * sin], axis=-1)
    return jnp.concatenate([rot, tp], axis=-1)


def conformer_conv_branch(u_glu, w_dw, b_dw, g_ln, b_ln, w_pw):
    a, gate = jnp.split(u_glu, 2, axis=-1)
    u = a * jax.nn.sigmoid(gate)
    u = lax.conv_general_dilated(
        u, w_dw[:, None, :].astype(u.dtype), window_strides=(1,),
        padding=[(CONV_W - 1, 0)],
        dimension_numbers=("NWC", "WIO", "NWC"),
        feature_group_count=CONV_C) + b_dw
    u = layernorm(u, g_ln, b_ln)
    u = jax.nn.silu(u)
    return u @ w_pw


def diff_attention(q, k, v, lam):
    seq_len = q.shape[1]
    scale = HEAD_DIM ** -0.5
    neg = jnp.finfo(jnp.float32).min
    outs = []
    for i in range(seq_len // Q_BLOCK):
        kv_len = (i + 1) * Q_BLOCK
        qs = q[:, i * Q_BLOCK:(i + 1) * Q_BLOCK]
        ks = k[:, :kv_len]
        vs = v[:, :kv_len]
        s = jnp.einsum("bqhcd,bkhcd->bhcqk", qs, ks).astype(jnp.float32) * scale
        qpos = i * Q_BLOCK + jnp.arange(Q_BLOCK)
        kpos = jnp.arange(kv_len)
        mask = kpos[None, :] <= qpos[:, None]
        s = jnp.where(mask, s, neg)
        p = jax.nn.softmax(s, axis=-1)
        a = p[:, :, 0] - lam * p[:, :, 1]
        outs.append(jnp.einsum("bhqk,bkhe->bqhe", a.astype(v.dtype), vs))
    return jnp.concatenate(outs, axis=1)


def setup_inputs(seed: int = 0) -> dict:
    key = jax.random.key(seed)
    ks = jax.random.split(key, 24)
    f32 = jnp.float32
    D, L = D_MODEL, DEPTH

    def nrm(k, shape, scale):
        return jax.random.normal(k, shape, f32) * scale

    def gain(k, shape):
        return 1.0 + 0.02 * jax.random.normal(k, shape, f32)

    return {
        "x": jax.random.normal(ks[0], (BATCH, SEQ, D), f32),
        "g_mix": gain(ks[1], (L, D)),
        "w_in": nrm(ks[2], (L, D, IN_COLS), D ** -0.5),
        "b_gate": nrm(ks[3], (L, N_BRANCH * D), 0.02),
        "w_dw": nrm(ks[4], (L, CONV_W, CONV_C), CONV_W ** -0.5),
        "b_dw": nrm(ks[5], (L, CONV_C), 0.02),
        "g_conv_ln": gain(ks[6], (L, CONV_C)),
        "b_conv_ln": nrm(ks[7], (L, CONV_C), 0.02),
        "w_conv_pw": nrm(ks[8], (L, CONV_C, D), CONV_C ** -0.5),
        "lambda_q1": nrm(ks[9], (L, HEAD_DIM), 0.1),
        "lambda_k1": nrm(ks[10], (L, HEAD_DIM), 0.1),
        "lambda_q2": nrm(ks[11], (L, HEAD_DIM), 0.1),
        "lambda_k2": nrm(ks[12], (L, HEAD_DIM), 0.1),
        "g_subln": gain(ks[13], (L, V_HEAD_DIM)),
        "w_attn_pw": nrm(ks[14], (L, V_WIDTH, D), V_WIDTH ** -0.5),
        "w_out": nrm(ks[15], (L, D, D), D ** -0.5),
        "g_ffn": gain(ks[16], (L, D)),
        "w_ffn_gate": nrm(ks[17], (L, D, D_FF), D ** -0.5),
        "w_ffn_up": nrm(ks[18], (L, D, D_FF), D ** -0.5),
        "w_ffn_down": nrm(ks[19], (L, D_FF, D), D_FF ** -0.5),
        "g_final": gain(ks[20], (D,)),
    }


def reference(x, g_mix, w_in, b_gate, w_dw, b_dw, g_conv_ln, b_conv_ln, w_conv_pw,
              lambda_q1, lambda_k1, lambda_q2, lambda_k2, g_subln, w_attn_pw, w_out,
              g_ffn, w_ffn_gate, w_ffn_up, w_ffn_down, g_final):
    bsz, seq_len, _ = x.shape
    cos, sin = rope_tables(seq_len)
    c0 = 2 * CONV_C
    c1 = c0 + QK_WIDTH
    c2 = c1 + QK_WIDTH
    c3 = c2 + V_WIDTH
    for l in range(DEPTH):
        lambda_init = 0.8 - 0.6 * math.exp(-0.3 * l)
        h = rmsnorm(x, g_mix[l])
        z = h @ w_in[l]
        z_conv = z[..., :c0]
        q = z[..., c0:c1].reshape(bsz, seq_len, N_HEADS, 2, HEAD_DIM)
        k = z[..., c1:c2].reshape(bsz, seq_len, N_HEADS, 2, HEAD_DIM)
        v = z[..., c2:c3].reshape(bsz, seq_len, N_HEADS, V_HEAD_DIM)
        gates = jax.nn.sigmoid(z[..., c3:] + b_gate[l])
        gate_conv, gate_attn = jnp.split(gates, N_BRANCH, axis=-1)

        y_conv = conformer_conv_branch(z_conv, w_dw[l], b_dw[l], g_conv_ln[l],
                                       b_conv_ln[l], w_conv_pw[l])

        q = apply_partial_rope(q, cos, sin)
        k = apply_partial_rope(k, cos, sin)
        lam = (jnp.exp(jnp.sum(lambda_q1[l].astype(jnp.float32) * lambda_k1[l].astype(jnp.float32)))
               - jnp.exp(jnp.sum(lambda_q2[l].astype(jnp.float32) * lambda_k2[l].astype(jnp.float32)))
               + lambda_init)
        o = diff_attention(q, k, v, lam)
        o = rmsnorm(o, g_subln[l]) * (1.0 - lambda_init)
        y_attn = o.reshape(bsz, seq_len, V_WIDTH) @ w_attn_pw[l]

        merged = gate_conv * y_conv + gate_attn * y_attn
        x = x + merged @ w_out[l]

        h2 = rmsnorm(x, g_ffn[l])
        ff = jax.nn.silu(h2 @ w_ffn_gate[l]) * (h2 @ w_ffn_up[l])
        x = x + ff @ w_ffn_down[l]
    return rmsnorm(x, g_final)
```

```python
import math
import numpy as np
import ml_dtypes
import concourse.bass as bass
import concourse.mybir as mybir
from concourse.bass_utils import run_bass_kernel_spmd

F32 = mybir.dt.float32
BF16 = mybir.dt.bfloat16
AF = mybir.ActivationFunctionType
ALU = mybir.AluOpType

D = 1024
KC = 8
NH = 8
DFF = 2816
FC = 22
CW = 31
IN_COLS = 7168
C0, C1, C2, C3 = 2048, 3072, 4096, 5120
LAMBDA_INIT = 0.8 - 0.6 * math.exp(-0.3 * 0)
NCF = 320
TT = 512
NDUM_FIN = 0
NPE_TAPS = 26
NDUM_DIAG = 0
DBG_TRI = False

ENGS = ("pe", "act", "dve", "pool", "sp")
SEM_LIMIT = 30000
NDMA_SEM = 8


class Op:
    __slots__ = ("eng", "fn", "reads", "writes", "dma", "deps", "ord", "gen", "slot", "target", "need_inc", "bar")

    def __init__(self, eng, fn, reads, writes, dma):
        self.eng = eng
        self.fn = fn
        self.reads = reads
        self.writes = writes
        self.dma = dma
        self.deps = ()
        self.ord = 0
        self.gen = 0
        self.slot = 0
        self.target = 0
        self.need_inc = False
        self.bar = False


class Prog:
    def __init__(self, nc):
        self.nc = nc
        self.ops = []

    def add(self, eng, fn, reads=(), writes=(), dma=False):
        self.ops.append(Op(eng, fn, tuple(reads), tuple(writes), dma))

    def barrier(self):
        o = Op("bar", None, (), (), False)
        o.bar = True
        self.ops.append(o)

    def analyze(self):
        ops = self.ops
        tok_w, tok_r = {}, {}
        last_c = {}
        last_d = {}
        bar_deps = {}
        dma_cnt = {"sp": 0, "pool": 0}
        for i, op in enumerate(ops):
            if op.bar:
                deps = set(last_c.values())
                for (q, s), j in last_d.items():
                    if q != "pool":
                        deps.add(j)
                for e in ENGS:
                    if e != "pool":
                        bar_deps.setdefault(e, set()).update(deps)
                continue
            deps = set()
            for t in op.reads:
                deps.update(tok_w.get(t, {}).values())
            for t in op.writes:
                deps.update(tok_w.get(t, {}).values())
                deps.update(tok_r.get(t, {}).values())
            if op.eng in bar_deps:
                deps.update(bar_deps.pop(op.eng))
            if op.dma:
                n = dma_cnt[op.eng]
                dma_cnt[op.eng] = n + 1
                op.slot = n % NDMA_SEM
                prev = last_d.get((op.eng, op.slot))
                if prev is not None:
                    deps.add(prev)
                    op.target = ops[prev].target + 16
                else:
                    op.target = 16
                last_d[(op.eng, op.slot)] = i
                key = ("d", i)
            else:
                last_c[op.eng] = i
                key = op.eng
            for t in op.reads:
                tok_r.setdefault(t, {})[key] = i
            for t in op.writes:
                if tok_r.get(t):
                    tok_w[t] = {key: i}
                    tok_r[t] = {}
                else:
                    tok_w.setdefault(t, {})[key] = i
            deps.discard(i)
            best = {}
            out = []
            for d in deps:
                dop = ops[d]
                if dop.dma:
                    out.append(d)
                else:
                    if dop.eng == "pe" and op.eng == "pe" and not op.dma:
                        continue
                    if d > best.get(dop.eng, -1):
                        best[dop.eng] = d
            out.extend(best.values())
            op.deps = tuple(sorted(out))
            for d in op.deps:
                ops[d].need_inc = True
        cnt = {e: 0 for e in ENGS}
        gen = {e: 0 for e in ENGS}
        for op in ops:
            if op.bar or op.dma or not op.need_inc:
                continue
            if cnt[op.eng] >= SEM_LIMIT:
                cnt[op.eng] = 0
                gen[op.eng] += 1
            cnt[op.eng] += 1
            op.ord = cnt[op.eng]
            op.gen = gen[op.eng]
        self.ngen = {e: gen[e] + 1 for e in ENGS}

    def emit(self):
        nc = self.nc
        engobj = {"pe": nc.tensor, "act": nc.scalar, "dve": nc.vector, "pool": nc.gpsimd, "sp": nc.sync}
        csem = {e: [nc.alloc_semaphore(f"c_{e}_{g}") for g in range(self.ngen[e])] for e in ENGS}
        dsem = {q: [nc.alloc_semaphore(f"d_{q}_{s}") for s in range(NDMA_SEM)] for q in ("sp", "pool")}
        waited = {e: {} for e in ENGS}
        ops = self.ops
        final_d = {}
        nwait = 0
        for op in ops:
            if op.bar:
                continue
            E = engobj[op.eng]
            w = waited[op.eng]
            for d in op.deps:
                dop = ops[d]
                if dop.dma:
                    key = ("d", dop.eng, dop.slot)
                    val = dop.target
                    sem = dsem[dop.eng][dop.slot]
                else:
                    key = ("c", dop.eng, dop.gen)
                    val = dop.ord
                    sem = csem[dop.eng][dop.gen]
                if w.get(key, 0) >= val:
                    continue
                E.wait_ge(sem, val)
                nwait += 1
                w[key] = val
            ins = op.fn(E)
            if op.dma:
                ins.then_inc(dsem[op.eng][op.slot], 16)
                final_d[(op.eng, op.slot)] = op.target
            elif op.need_inc:
                ins.then_inc(csem[op.eng][op.gen], 1)
        for (q, s), tgt in sorted(final_d.items()):
            nc.sync.wait_ge(dsem[q][s], tgt)
        self.nwait = nwait


def build_nc(S=2048, NSEQ=2, dbg=()):
    NT = S // TT
    TG = min(2, NT)
    NTOK = S * NSEQ
    nc = bass.Bass("TRN2", target_bir_lowering=False)
    P = Prog(nc)

    def din(name, shape, dt=F32):
        return nc.dram_tensor(name, shape, dt, kind="ExternalInput").ap()

    xT = din("xT", [D, NTOK])
    w_in = din("w_in", [D, IN_COLS])
    w_pw = din("w_conv_pw", [D, D])
    w_ap = din("w_attn_pw", [D, D])
    w_out = din("w_out", [D, D])
    w_fg = din("w_ffn_gate", [D, DFF])
    w_fu = din("w_ffn_up", [D, DFF])
    w_fd = din("w_ffn_down", [DFF, D])
    cf_d = din("cf32", [128, NCF])
    cb_d = din("cb16", [128, 1024], BF16)
    rope_d = din("rope", [128, 2, S])
    outT = nc.dram_tensor("outT", [D, NTOK], F32, kind="ExternalOutput").ap()
    dbg_out = {}

    w_in_v = w_in.rearrange("(k p) c -> p k c", p=128)
    w_pw_v = w_pw.rearrange("(k p) c -> p k c", p=128)
    w_ap_v = w_ap.rearrange("(k p) c -> p k c", p=128)
    w_out_v = w_out.rearrange("(k p) c -> p k c", p=128)
    w_fg_v = w_fg.rearrange("(k p) c -> p k c", p=128)
    w_fu_v = w_fu.rearrange("(k p) c -> p k c", p=128)
    w_fd_v = w_fd.rearrange("(f p) c -> p f c", p=128)
    xT_v = xT.rearrange("(k p) t -> p k t", p=128)
    outT_v = outT.rearrange("(k p) t -> p k t", p=128)

    def sb(name, shape, dt, off):
        return nc.alloc_sbuf_tensor_at(name, shape, dt, offset=off)

    BASE = 24576 - 256
    SB2 = S * 2 * 8
    o_R1 = BASE
    o_R2 = o_R1 + SB2
    o_A = o_R2 + SB2
    A_SZ = 49152
    o_R3 = o_A + A_SZ
    o_W = o_R3 + SB2
    NW = 3
    o_ROPE = o_W + NW * 8192
    o_T = o_ROPE + 2 * S * 4
    assert o_T + 16384 <= 229376 - 256, o_T
    R1 = sb("R1", [128, 8, S], BF16, o_R1)
    R2 = sb("R2", [128, 8, S], BF16, o_R2)
    R3 = sb("R3", [128, 8, S], BF16, o_R3)
    X1 = sb("X1", [128, 8, S], F32, o_R2)
    assert 8 * S * 4 <= SB2 + 32768
    FF = sb("FF", [128, FC, TG * TT], BF16, o_A + 32768)
    assert FC * TG * TT * 2 <= 16384 + SB2 or S < 2048
    QT = sb("QT", [128, 4, S], BF16, o_A)
    KT = sb("KT", [128, 4, S], BF16, o_A + 16384)
    VV = sb("VV", [128, S // 128, 512], BF16, o_A + 32768)
    XS = [sb(f"XS{i}", [128, 8, TT], F32, o_A + i * 16384) for i in range(2)]
    UL = 30 + S
    ub = ((UL * 2 + 31) // 32) * 32
    U16 = [sb(f"U16_{i}", [128, UL], BF16, o_A + i * ub) for i in range(2)]
    DG = [sb(f"DG{i}", [128, CW, 128], BF16, o_A + 2 * ub + i * CW * 256) for i in range(2)]
    assert 2 * ub + 2 * CW * 256 <= A_SZ
    WS = [sb(f"WS{j}", [128, 8, 512], BF16, o_W + j * 8192) for j in range(NW)]
    WD = [sb(f"WD{j}", [128, FC, 128], BF16, o_W + j * 8192) for j in range(NW)]
    RC = sb("RC", [128, S], F32, o_ROPE)
    RS = sb("RS", [128, S], F32, o_ROPE + S * 4)
    TF = [sb(f"TF{i}", [128, TT], F32, o_T + i * 2048) for i in range(8)]
    TH = [[sb(f"TH{i}_{h}", [128, TT], BF16, o_T + i * 2048 + h * 1024) for h in range(2)] for i in range(8)]

    PT2 = [sb(f"PT2_{i}", [128, 2, TT], BF16, o_T + i * 2048) for i in range(2)]
    LT = sb("LT", [128, 2, TT], F32, o_T + 2 * 2048)

    def tf(i):
        return TF[i], ("T", i, 0), ("T", i, 1)

    AUX = 16768
    CF = sb("CF", [128, NCF], F32, AUX)
    CB = sb("CB", [128, 1024], BF16, AUX + 1280)
    LP = sb("LP", [128, 8], F32, AUX + 1280 + 2048)
    RSB = sb("RSB", [128, TT], F32, AUX + 1280 + 2048 + 64)
    PT2.append(sb("PT2_2", [128, 2, TT], BF16, AUX + 1280 + 2048 + 64 + 2048))
    assert AUX + 1280 + 2048 + 64 + 4096 <= BASE
    ONESF = sb("ONESF", [128, 128], F32, o_T)
    ones_b = CB[:, 0:128]
    pm_b = CB[:, 128:256]
    ident_b = CB[:, 256:384]
    zeros_b = CB[:, 384:512]
    LM = lambda h, a: CB[64 * h:64 * h + 64, 512 + a * 128:512 + (a + 1) * 128]
    RM = lambda h, a: CB[64 * h:64 * h + 64, 768 + a * 128:768 + (a + 1) * 128]
    g_mix = lambda k: CF[:, k:k + 1]
    b_gate = lambda c: CF[:, 8 + c:9 + c]
    w_dw = lambda c, j: CF[:, 24 + c * CW + j:25 + c * CW + j]
    b_dw = lambda c: CF[:, 272 + c:273 + c]
    g_ln = lambda c: CF[:, 280 + c:281 + c]
    b_ln = lambda c: CF[:, 288 + c:289 + c]
    g_ffn = lambda k: CF[:, 296 + k:297 + k]
    g_fin = lambda k: CF[:, 304 + k:305 + k]
    neglam = LP[:, 4:5]
    gsubs = LP[:, 5:6]

    PSA = nc.alloc_psum_tensor("psa", [128, 8, TT], F32)

    class _B:
        def __init__(self, b):
            self.b = b

        def __getitem__(self, idx):
            if not isinstance(idx, tuple):
                idx = (idx,)
            return PSA[(idx[0], self.b) + tuple(idx[1:])]
    PS = [_B(i) for i in range(8)]
    bank_ctr = [0]

    bank_reserved = set()

    def nbank():
        while True:
            b = bank_ctr[0] % 8
            bank_ctr[0] += 1
            if b not in bank_reserved:
                return b

    def mm(bank, pairs, reads, ncols=None, c0=0):
        out_ap = PS[bank][:, c0:(c0 + ncols)] if ncols is not None else PS[bank][:]

        def fn(pe, pairs=pairs, out_ap=out_ap):
            n = len(pairs)
            ins = None
            for i, (l, r) in enumerate(pairs):
                ins = pe.matmul(out_ap, lhsT=l, rhs=r, start=(i == 0), stop=(i == n - 1))
            return ins
        P.add("pe", fn, reads=reads, writes=[("ps", bank)])

    def act(out, in_, func, reads, writes, bias=None, scale=None):
        kw = {}
        if bias is not None:
            kw["bias"] = bias
        if scale is not None:
            kw["scale"] = scale
        P.add("act", lambda e, kw=kw: e.activation(out=out, in_=in_, func=func, **kw), reads=reads, writes=writes)

    def tt(out, in0, in1, op, reads, writes, eng="dve"):
        P.add(eng, lambda e: e.tensor_tensor(out=out, in0=in0, in1=in1, op=op), reads=reads, writes=writes)

    def stt(out, in0, scalar, in1, op0, op1, reads, writes):
        P.add("dve", lambda e: e.scalar_tensor_tensor(out=out, in0=in0, scalar=scalar, in1=in1, op0=op0, op1=op1),
              reads=reads, writes=writes)

    def ts(out, in0, s1, s2, op0, op1, reads, writes):
        if s2 is None:
            P.add("dve", lambda e: e.tensor_scalar(out=out, in0=in0, scalar1=s1, scalar2=None, op0=op0),
                  reads=reads, writes=writes)
        else:
            P.add("dve", lambda e: e.tensor_scalar(out=out, in0=in0, scalar1=s1, scalar2=s2, op0=op0, op1=op1),
                  reads=reads, writes=writes)

    def recip(out, in_, reads, writes):
        P.add("dve", lambda e: e.reciprocal(out=out, in_=in_), reads=reads, writes=writes)

    def dma(q, out, in_, reads, writes):
        P.add(q, lambda e: e.dma_start(out=out, in_=in_), reads=reads, writes=writes, dma=True)

    def dump(name, ap, shape, dt, reads):
        if name not in dbg:
            return
        t = nc.dram_tensor("dbg_" + name, shape, dt, kind="ExternalOutput").ap()
        dbg_out[name] = t
        dma("sp", t, ap, reads, [("dbg", name)])

    def rstd_tile(bank, scale, eps, slot, mode):
        t, ta, tb = tf(slot)
        act(t[:], PS[bank][:], AF.Ln, [("ps", bank), ("LPe",)], [ta, tb], bias=eps_ap(eps), scale=scale)
        act(t[:], t[:], AF.Exp, [ta, tb], [ta, tb], scale=-0.5)
        return t, ta, tb

    eps_cols = {1e-6: 6, 1e-5: 7}

    def eps_ap(eps):
        c = eps_cols[eps]
        return LP[:, c:c + 1]

    dma("sp", CF[:], cf_d, [], [("CF",)])
    dma("sp", CB[:], cb_d, [], [("CB",)])
    dma("sp", RC[:], rope_d[:, 0, :], [], [("ROPE",)])
    dma("sp", RS[:], rope_d[:, 1, :], [], [("ROPE",)])
    P.add("dve", lambda e: e.memset(ONESF[:], 1.0), writes=[("T", 0, 0)])
    P.add("dve", lambda e: e.memset(LP[:, 6:7], 1e-6), writes=[("LPe",)])
    P.add("dve", lambda e: e.memset(LP[:, 7:8], 1e-5), writes=[("LPe",)])
    tt(LP[:, 0:1], CF[:, 313:314], CF[:, 314:315], ALU.mult, [("CF",)], [("LP0",)])
    tt(LP[:, 1:2], CF[:, 315:316], CF[:, 316:317], ALU.mult, [("CF",)], [("LP0",)])
    b0 = nbank()
    P.add("pe", lambda pe: pe.matmul(PS[b0][:, 0:2], lhsT=ONESF[:], rhs=LP[:, 0:2], start=True, stop=True),
          reads=[("T", 0, 0), ("LP0",)], writes=[("ps", b0)])
    act(LP[:, 2:4], PS[b0][:, 0:2], AF.Exp, [("ps", b0)], [("LP1",)])
    tt(LP[:, 4:5], LP[:, 3:4], LP[:, 2:3], ALU.subtract, [("LP1",)], [("LP2",)])
    ts(LP[:, 4:5], LP[:, 4:5], -LAMBDA_INIT, None, ALU.add, None, [("LP2",)], [("LP2",)])
    ts(LP[:, 5:6], CF[:, 312:313], 1.0 - LAMBDA_INIT, None, ALU.mult, None, [("CF",)], [("LP3",)])
    CONST_R = [("CF",), ("CB",), ("ROPE",), ("LP2",), ("LP3",), ("LPe",)]

    wstate = {"n": 0}

    def fill(spec):
        j = wstate["n"] % NW
        wstate["n"] += 1
        for (dstf, src) in spec:
            dma("pool", dstf(j), src, [], [("W", j)])
        return j

    def spec_cols(view, ranges):
        sp = []
        o = 0
        for (c0, n) in ranges:
            sp.append((lambda j, o=o, n=n: WS[j][:, :, o:o + n], view[:, :, c0:c0 + n]))
            o += n
        return sp

    def spec_down(c):
        return [(lambda j: WD[j][:], w_fd_v[:, :, c * 128:(c + 1) * 128])]

    def r1tok(t):
        return [("R1", k, t) for k in range(8)]

    def r2tok(t):
        return [("R2", k, t) for k in range(8)]

    def r3tok(t):
        return [("R3", k, t) for k in range(8)]

    def seq_stages(s):
        stages = []
        tok0 = s * S

        def p1():
            P.barrier()
            for t in range(NT):
                xs = XS[t % 2]
                xtok = ("XS", t % 2)
                dma("sp", xs[:], xT_v[:, :, tok0 + t * TT: tok0 + (t + 1) * TT], [], [xtok])
                b = nbank()
                for k in range(8):
                    sq = TH[0][k % 2]
                    sqt = ("T", 0, k % 2)
                    act(sq[:], xs[:, k, :], AF.Square, [xtok], [sqt])
                    P.add("pe", lambda pe, k=k, sq=sq, b=b: pe.matmul(PS[b][:], lhsT=ones_b, rhs=sq[:], start=(k == 0), stop=(k == 7)),
                          reads=[sqt, ("CB",)], writes=[("ps", b)])
                rt, ra, rb = rstd_tile(b, 1.0 / D, 1e-6, 1, "sqrt")
                for k in range(8):
                    stt(R1[:, k, t * TT:(t + 1) * TT], xs[:, k, :], g_mix(k), rt[:], ALU.mult, ALU.mult,
                        [xtok, ra, rb, ("CF",)], [("R1", k, t)])
            dump("hT", R1[:], [128, 8, S], BF16, [x for t in range(NT) for x in r1tok(t)])
        stages.append((None, p1))

        for j2 in range(4):
            spec = spec_cols(w_in_v, [(j2 * 256, 256), (1024 + j2 * 256, 256)])

            def p2a(slot, j2=j2):
                if j2 == 0:
                    P.barrier()
                    for i in range(2):
                        P.add("dve", lambda e, i=i: e.memset(U16[i][:, 0:30], 0.0), writes=[("U16pad", i)])
                for cc in range(2):
                    c = j2 * 2 + cc
                    ui = c % 2
                    for j in range(NPE_TAPS):
                        P.add("dve", lambda e, j=j, c=c, ui=ui: e.tensor_scalar(out=DG[ui][:, j, :], in0=ident_b, scalar1=w_dw(c, j),
                                                                               scalar2=None, op0=ALU.mult),
                              reads=[("CB",), ("CF",)], writes=[("DG", ui)])
                    for t in range(NT):
                        ba, bg = nbank(), nbank()
                        mm(ba, [(WS[slot][:, k, cc * 128:(cc + 1) * 128], R1[:, k, t * TT:(t + 1) * TT]) for k in range(8)],
                           [("W", slot)] + r1tok(t))
                        mm(bg, [(WS[slot][:, k, 256 + cc * 128:256 + (cc + 1) * 128], R1[:, k, t * TT:(t + 1) * TT]) for k in range(8)],
                           [("W", slot)] + r1tok(t))
                        sg, sa, sb_ = tf(t % 2)
                        act(sg[:], PS[bg][:], AF.Sigmoid, [("ps", bg)], [sa, sb_])
                        tt(U16[ui][:, 30 + t * TT:30 + (t + 1) * TT], PS[ba][:], sg[:], ALU.mult,
                           [("ps", ba), sa, sb_], [("U16", ui, t)])
                    for t in range(NT):
                        bc = nbank()
                        rd = [("DG", ui), ("U16", ui, t), ("U16pad", ui)] + ([("U16", ui, t - 1)] if t > 0 else [])
                        mm(bc, [(DG[ui][:, j, :], U16[ui][:, t * TT + j:t * TT + j + TT]) for j in range(NPE_TAPS)], rd)
                        ca, caa, cab = tf(4 + (t % 2))
                        urd = [x for x in rd if x[0] != "DG"] + [("CF",)]
                        ts(ca[:], U16[ui][:, t * TT + NPE_TAPS:t * TT + NPE_TAPS + TT], w_dw(c, NPE_TAPS), None, ALU.mult, None,
                           urd, [caa, cab])
                        for j in range(NPE_TAPS + 1, CW):
                            stt(ca[:], U16[ui][:, t * TT + j:t * TT + j + TT], w_dw(c, j), ca[:], ALU.mult, ALU.add,
                                urd + [caa, cab], [caa, cab])
                        stt(R2[:, c, t * TT:(t + 1) * TT], PS[bc][:], b_dw(c), ca[:], ALU.add, ALU.add,
                            [("ps", bc), ("CF",), caa, cab], [("R2", c, t)])
            stages.append((spec, p2a))

        LNT = [(RSB, [("RSB", 0), ("RSB", 1)]), (sb(f"LNT1_{s}", [128, TT], F32, AUX + 1280 + 2048 + 64 + 2048), [("PT2x",)])]
        ln_state = {"gen": None}

        def ln_steps(tiles):
            for t in tiles:
                bs, bq = nbank(), nbank()
                bank_reserved.update((bs, bq))
                LAG = 1

                def stat_mm(c):
                    cbv = R2[:, c, t * TT:(t + 1) * TT]
                    sq = TH[7][c % 2]
                    sqt = ("T", 7, c % 2)
                    P.add("pe", lambda pe, c=c, cbv=cbv, bs=bs: pe.matmul(PS[bs][:], lhsT=ones_b, rhs=cbv, start=(c == 0), stop=(c == 7)),
                          reads=[("R2", c, t), ("CB",)], writes=[("ps", bs)])
                    P.add("pe", lambda pe, c=c, sq=sq, bq=bq: pe.matmul(PS[bq][:], lhsT=ones_b, rhs=sq[:], start=(c == 0), stop=(c == 7)),
                          reads=[sqt, ("CB",)], writes=[("ps", bq)])
                for c in range(8 + LAG):
                    if c < 8:
                        act(TH[7][c % 2][:], R2[:, c, t * TT:(t + 1) * TT], AF.Square, [("R2", c, t)], [("T", 7, c % 2)])
                    if c >= LAG:
                        stat_mm(c - LAG)
                    yield
                m, ma, mb = tf(5)
                q2, qa, qb = tf(6)
                ts(m[:], PS[bs][:], 1.0 / D, None, ALU.mult, None, [("ps", bs)], [ma, mb])
                tt(q2[:], m[:], m[:], ALU.mult, [ma, mb], [qa, qb])
                stt(q2[:], PS[bq][:], 1.0 / D, q2[:], ALU.mult, ALU.subtract, [("ps", bq), qa, qb], [qa, qb])
                act(q2[:], q2[:], AF.Ln, [qa, qb, ("LPe",)], [qa, qb], bias=eps_ap(1e-5), scale=1.0)
                act(q2[:], q2[:], AF.Exp, [qa, qb], [qa, qb], scale=-0.5)
                tt(m[:], m[:], q2[:], ALU.mult, [ma, mb, qa, qb], [ma, mb])
                bank_reserved.difference_update((bs, bq))
                yield
                for c in range(8):
                    cbv = R2[:, c, t * TT:(t + 1) * TT]
                    t1, t1t = LNT[c % 2]
                    tt(t1[:], cbv, q2[:], ALU.mult, [("R2", c, t), qa, qb], t1t)
                    tt(t1[:], t1[:], m[:], ALU.subtract, t1t + [ma, mb], t1t)
                    act(cbv, t1[:], AF.Silu, t1t + [("CF",)], [("R2", c, t)], bias=b_ln(c), scale=g_ln(c))
                    yield

        def pump(k):
            for _ in range(k):
                if ln_state["gen"] is None:
                    return
                try:
                    next(ln_state["gen"])
                except StopIteration:
                    ln_state["gen"] = None

        halves = [list(range(0, NT // 2)), list(range(NT // 2, NT))] if NT >= 2 else [list(range(NT))]

        def ln_first():
            dump("cb", R2[:], [128, 8, S], BF16, [x for t in range(NT) for x in r2tok(t)])
            for _ in ln_steps(halves[0]):
                pass
            if len(halves) > 1:
                ln_state["gen"] = ln_steps(halves[1])
        stages.append((None, ln_first))

        for hi, tiles in enumerate(halves):
            for j2 in range(4):
                spec = spec_cols(w_pw_v, [(j2 * 256, 256)]) + [
                    (lambda j: WS[j][:, :, 256:512], w_in_v[:, :, C3 + j2 * 256: C3 + (j2 + 1) * 256])]

                def p2b(slot, j2=j2, tiles=tiles, hi=hi):
                    for cc in range(2):
                        c = j2 * 2 + cc
                        for t in tiles:
                            by, bg = nbank(), nbank()
                            mm(by, [(WS[slot][:, k, cc * 128:(cc + 1) * 128], R2[:, k, t * TT:(t + 1) * TT]) for k in range(8)],
                               [("W", slot)] + r2tok(t))
                            mm(bg, [(WS[slot][:, k, 256 + cc * 128:256 + (cc + 1) * 128], R1[:, k, t * TT:(t + 1) * TT]) for k in range(8)],
                               [("W", slot)] + r1tok(t))
                            sg, sa, sb_ = tf((c * NT + t) % 2)
                            act(sg[:], PS[bg][:], AF.Sigmoid, [("ps", bg), ("CF",)], [sa, sb_], bias=b_gate(c))
                            tt(R3[:, c, t * TT:(t + 1) * TT], PS[by][:], sg[:], ALU.mult, [("ps", by), sa, sb_], [("R3", c, t)])
                            if hi == 0:
                                pump(3)
                    if hi == 0 and j2 == 3:
                        pump(10 ** 6)
                        dump("cs", R2[:], [128, 8, S], BF16, [x for t in range(NT) for x in r2tok(t)])
                stages.append((spec, p2b))

        def dump_m1():
            dump("m1", R3[:], [128, 8, S], BF16, [x for t in range(NT) for x in r3tok(t)])
        stages.append((None, dump_m1))

        for hg in range(2):
            for which in range(2):
                spec = spec_cols(w_in_v, [((C0 if which == 0 else C1) + hg * 512, 512)])

                def p3qk(slot, hg=hg, which=which):
                    if hg == 0 and which == 0:
                        P.barrier()
                    dst = QT if which == 0 else KT
                    dn = "QT" if which == 0 else "KT"
                    for hh in range(4):
                        for t in range(NT):
                            bz = nbank()
                            mm(bz, [(WS[slot][:, k, hh * 128:(hh + 1) * 128], R1[:, k, t * TT:(t + 1) * TT]) for k in range(8)],
                               [("W", slot)] + r1tok(t))
                            i2 = (hh * NT + t) % 2
                            qb_ = TH[0][i2]
                            qbt = ("T", 0, i2)
                            act(qb_[:], PS[bz][:], AF.Copy, [("ps", bz)], [qbt])
                            bp = nbank()
                            P.add("pe", lambda pe, bp=bp, qb_=qb_: pe.matmul(PS[bp][:], lhsT=pm_b, rhs=qb_[:], start=True, stop=True),
                                  reads=[qbt, ("CB",)], writes=[("ps", bp)])
                            tS, tSa, tSb = tf(1 + i2)
                            qf, qfa, qfb = tf(3 + i2)
                            tt(tS[:], PS[bp][:], RS[:, t * TT:(t + 1) * TT], ALU.mult, [("ps", bp), ("ROPE",)], [tSa, tSb])
                            tt(qf[:], PS[bz][:], RC[:, t * TT:(t + 1) * TT], ALU.mult, [("ps", bz), ("ROPE",)], [qfa, qfb])
                            tt(dst[:, hh, t * TT:(t + 1) * TT], qf[:], tS[:], ALU.add, [tSa, tSb, qfa, qfb], [(dn, hh, t)])
                stages.append((spec, p3qk))
            spec = spec_cols(w_in_v, [(C2 + hg * 512, 512)])

            def p3v(slot, hg=hg):
                for tq in range(S // 128):
                    bv = nbank()
                    mm(bv, [(R1[:, k, tq * 128:(tq + 1) * 128], WS[slot][:, k, :]) for k in range(8)],
                       [("W", slot)] + r1tok(tq // 4))
                    P.add("act", lambda e, tq=tq, bv=bv: e.activation(out=VV[:, tq, :], in_=PS[bv][:], func=AF.Copy),
                          reads=[("ps", bv)], writes=[("VV", tq)])
                if hg == 0:
                    dump("qT", QT[:], [128, 4, S], BF16, [("QT", hh, t) for hh in range(4) for t in range(NT)])
                    dump("kT", KT[:], [128, 4, S], BF16, [("KT", hh, t) for hh in range(4) for t in range(NT)])
                    dump("vv", VV[:], [128, S // 128, 512], BF16, [("VV", tq) for tq in range(S // 128)])
            stages.append((spec, p3v))

            def p4(hg=hg):
                its = [(hh, i, j) for hh in range(4) for i in range(NT) for j in range(4 * (i + 1))]
                BO1, BO2, BD1, BD2 = 4, 5, 6, 7

                def geom(n):
                    hh, i, j = its[n]
                    jd = j - 4 * i
                    q0 = jd * 128 if jd > 0 else 0
                    return hh, i, j, jd, q0

                def emit_S(n):
                    hh, i, j, jd, q0 = geom(n)
                    b = 2 * (n % 2)
                    qs = slice(i * TT + q0, (i + 1) * TT)

                    def fn_s(pe, j=j, hh=hh, qs=qs, b=b, q0=q0, diag=(jd >= 0)):
                        pe.matmul(PS[b][:, q0:TT], lhsT=KT[0:64, hh, j * 128:(j + 1) * 128], rhs=QT[0:64, hh, qs],
                                  start=True, stop=not diag)
                        ins = pe.matmul(PS[b + 1][:, q0:TT], lhsT=KT[64:128, hh, j * 128:(j + 1) * 128], rhs=QT[64:128, hh, qs],
                                        start=True, stop=not diag)
                        if diag:
                            for a in range(2):
                                pe.matmul(PS[b][:, q0:q0 + 128], lhsT=LM(0, a), rhs=RM(0, a), start=False, stop=(a == 1))
                                ins = pe.matmul(PS[b + 1][:, q0:q0 + 128], lhsT=LM(1, a), rhs=RM(1, a), start=False, stop=(a == 1))
                        return ins
                    P.add("pe", fn_s, reads=[("KT", hh, j // 4), ("QT", hh, i), ("CB",)], writes=[("ps", b), ("ps", b + 1)])

                def emit_exp(n):
                    hh, i, j, jd, q0 = geom(n)
                    pi = n % 3
                    b = 2 * (n % 2)
                    ptok = [("T", pi, 0), ("T", pi, 1)] if pi < 2 else [("PT2x",)]
                    act(PT2[pi][:, :, q0:TT], PSA[:, b:b + 2, q0:TT], AF.Exp, [("ps", b), ("ps", b + 1)], ptok, scale=0.125)

                def emit_PV(n):
                    hh, i, j, jd, q0 = geom(n)
                    pi = n % 3
                    ptok = [("T", pi, 0), ("T", pi, 1)] if pi < 2 else [("PT2x",)]
                    vv = VV[:, j, hh * 128:(hh + 1) * 128]
                    first, lastj = (j == 0), (j == 4 * (i + 1) - 1)

                    def fn_pv(pe, vv=vv, pi=pi, q0=q0, first=first, lastj=lastj):
                        pe.matmul(PS[BO1][:, q0:TT], lhsT=vv, rhs=PT2[pi][:, 0, q0:TT], start=first, stop=lastj)
                        pe.matmul(PS[BD1][:, q0:TT], lhsT=ones_b, rhs=PT2[pi][:, 0, q0:TT], start=first, stop=lastj)
                        pe.matmul(PS[BO2][:, q0:TT], lhsT=vv, rhs=PT2[pi][:, 1, q0:TT], start=first, stop=lastj)
                        return pe.matmul(PS[BD2][:, q0:TT], lhsT=ones_b, rhs=PT2[pi][:, 1, q0:TT], start=first, stop=lastj)
                    P.add("pe", fn_pv, reads=[("VV", j)] + ptok + [("CB",)],
                          writes=[("ps", BO1), ("ps", BO2), ("ps", BD1), ("ps", BD2)])
                    return lastj

                ltok = [("T", 2, 0), ("T", 2, 1), ("T", 3, 0), ("T", 3, 1)]
                ctok = [("T", 4, 0), ("T", 4, 1), ("T", 5, 0), ("T", 5, 1)]
                C12 = sb(f"C12_{hg}_{s}", [128, 2, TT], F32, o_T + 4 * 2048)

                def finalize(n):
                    hh, i, j, jd, q0 = geom(n)
                    head = hg * 4 + hh
                    oo, ooa, oob = tf(6)
                    sq = TH[7][0]
                    act(LT[:], PSA[:, BD1:BD2 + 1, :], AF.Ln, [("ps", BD1), ("ps", BD2)], ltok)
                    P.add("dve", lambda e: e.tensor_copy(out=C12[:], in_=PSA[:, BO1:BO2 + 1, :]),
                          reads=[("ps", BO1), ("ps", BO2)], writes=ctok)
                    yield
                    act(LT[:], LT[:], AF.Exp, ltok, ltok, scale=-1.0)
                    tt(C12[:], C12[:], LT[:], ALU.mult, ctok + ltok, ctok)
                    stt(oo[:], C12[:, 1, :], neglam, C12[:, 0, :], ALU.mult, ALU.add, ctok + [("LP2",)], [ooa, oob])
                    tt(sq[:], oo[:], oo[:], ALU.mult, [ooa, oob], [("T", 7, 0)])
                    yield
                    yield
                    yield
                    bank = 2 * ((cur[0] + 1) % 2)
                    P.add("pe", lambda pe, bank=bank: pe.matmul(PS[bank][:], lhsT=ones_b, rhs=sq[:], start=True, stop=True),
                          reads=[("T", 7, 0), ("CB",)], writes=[("ps", bank)])
                    rs_, rsa, rsb = RSB, ("RSB", 0), ("RSB", 1)
                    act(rs_[:], PS[bank][:], AF.Ln, [("ps", bank), ("LPe",)], [rsa, rsb], bias=eps_ap(1e-6), scale=1.0 / 128)
                    yield
                    act(rs_[:], rs_[:], AF.Exp, [rsa, rsb], [rsa, rsb], scale=-0.5)
                    stt(R2[:, head, i * TT:(i + 1) * TT], oo[:], gsubs, rs_[:], ALU.mult, ALU.mult,
                        [ooa, oob, rsa, rsb, ("LP3",)], [("R2", head, i)])

                N = len(its)
                cur = [0]
                queue = []
                emit_S(0)
                if N > 1:
                    emit_S(1)
                emit_exp(0)
                def emit_dummy(n, k):
                    hh, i, j, jd, q0 = geom(n)
                    if j == 4 * (i + 1) - 1 or k == 0:
                        return
                    def fn_d(pe, hh=hh, k=k):
                        ins = None
                        for _ in range(k):
                            ins = pe.matmul(PS[BO1][:], lhsT=zeros_b, rhs=KT[:, hh, 0:TT], start=False, stop=False)
                        return ins
                    P.add("pe", fn_d, reads=[("KT", hh, 0), ("CB",)], writes=[("ps", BO1)])

                for n in range(N):
                    cur[0] = n
                    if n + 2 < N:
                        emit_S(n + 2)
                    lastj = emit_PV(n)
                    emit_dummy(n, NDUM_FIN if queue else (NDUM_DIAG if geom(n)[3] >= 0 else 0))
                    if n + 1 < N:
                        emit_exp(n + 1)
                    for g in list(queue):
                        try:
                            next(g)
                        except StopIteration:
                            queue.remove(g)
                    if lastj:
                        g = finalize(n)
                        next(g)
                        queue.append(g)
                while queue:
                    cur[0] += 1
                    for g in list(queue):
                        try:
                            next(g)
                        except StopIteration:
                            queue.remove(g)
            stages.append((None, p4))

        def dump_on():
            dump("on", R2[:], [128, 8, S], BF16, [x for t in range(NT) for x in r2tok(t)])
        stages.append((None, dump_on))

        for j2 in range(4):
            spec = spec_cols(w_ap_v, [(j2 * 256, 256)]) + [
                (lambda j: WS[j][:, :, 256:512], w_in_v[:, :, C3 + D + j2 * 256: C3 + D + (j2 + 1) * 256])]

            def p5(slot, j2=j2):
                for cc in range(2):
                    c = j2 * 2 + cc
                    for t in range(NT):
                        by, bg = nbank(), nbank()
                        mm(by, [(WS[slot][:, k, cc * 128:(cc + 1) * 128], R2[:, k, t * TT:(t + 1) * TT]) for k in range(8)],
                           [("W", slot)] + r2tok(t))
                        mm(bg, [(WS[slot][:, k, 256 + cc * 128:256 + (cc + 1) * 128], R1[:, k, t * TT:(t + 1) * TT]) for k in range(8)],
                           [("W", slot)] + r1tok(t))
                        i2 = (c * NT + t) % 2
                        sg, sa, sb_ = tf(i2)
                        tm, tma, tmb = tf(2 + i2)
                        act(sg[:], PS[bg][:], AF.Sigmoid, [("ps", bg), ("CF",)], [sa, sb_], bias=b_gate(8 + c))
                        tt(tm[:], PS[by][:], sg[:], ALU.mult, [("ps", by), sa, sb_], [tma, tmb])
                        tt(R3[:, c, t * TT:(t + 1) * TT], tm[:], R3[:, c, t * TT:(t + 1) * TT], ALU.add,
                           [tma, tmb, ("R3", c, t)], [("R3", c, t)])
            stages.append((spec, p5))

        def dump_mg():
            dump("mg", R3[:], [128, 8, S], BF16, [x for t in range(NT) for x in r3tok(t)])
        stages.append((None, dump_mg))

        for j2 in range(2):
            spec = spec_cols(w_out_v, [(j2 * 512, 512)])

            def p6(slot, j2=j2):
                if j2 == 0:
                    P.barrier()
                for cc in range(4):
                    c = j2 * 4 + cc
                    for t in range(NT):
                        bz = nbank()
                        mm(bz, [(WS[slot][:, k, cc * 128:(cc + 1) * 128], R3[:, k, t * TT:(t + 1) * TT]) for k in range(8)],
                           [("W", slot)] + r3tok(t))
                        i2 = (c * NT + t) % 2
                        xs, xa, xb = tf(i2)
                        dma("sp", xs[:], xT[c * 128:(c + 1) * 128, tok0 + t * TT: tok0 + (t + 1) * TT], [], [xa, xb])
                        tt(X1[:, c, t * TT:(t + 1) * TT], PS[bz][:], xs[:], ALU.add, [("ps", bz), xa, xb], [("X1", c, t)])
            stages.append((spec, p6))

        def p6n():
            P.barrier()
            dump("x1", X1[:], [128, 8, S], F32, [("X1", c, t) for c in range(8) for t in range(NT)])
            for t in range(NT):
                b = nbank()
                for k in range(8):
                    sq, sqt = TH[2][k % 2], ("T", 2, k % 2)
                    act(sq[:], X1[:, k, t * TT:(t + 1) * TT], AF.Square, [("X1", k, t)], [sqt])
                    P.add("pe", lambda pe, k=k, sq=sq, b=b: pe.matmul(PS[b][:], lhsT=ones_b, rhs=sq[:], start=(k == 0), stop=(k == 7)),
                          reads=[sqt, ("CB",)], writes=[("ps", b)])
                rt, ra, rb = rstd_tile(b, 1.0 / D, 1e-6, 3, "sqrt")
                for k in range(8):
                    stt(R1[:, k, t * TT:(t + 1) * TT], X1[:, k, t * TT:(t + 1) * TT], g_ffn(k), rt[:], ALU.mult, ALU.mult,
                        [("X1", k, t), ra, rb, ("CF",)], [("R1", k, t)])
        stages.append((None, p6n))

        for g in range(NT // TG):
            tiles = [g * TG + x for x in range(TG)]
            for fs in range(11):
                spec = spec_cols(w_fg_v, [(fs * 256, 256)]) + [
                    (lambda j: WS[j][:, :, 256:512], w_fu_v[:, :, fs * 256:(fs + 1) * 256])]

                def p7a(slot, fs=fs, tiles=tiles):
                    for fc in range(2):
                        f = fs * 2 + fc
                        for ti, t in enumerate(tiles):
                            bg, bu = nbank(), nbank()
                            mm(bg, [(WS[slot][:, k, fc * 128:(fc + 1) * 128], R1[:, k, t * TT:(t + 1) * TT]) for k in range(8)],
                               [("W", slot)] + r1tok(t))
                            mm(bu, [(WS[slot][:, k, 256 + fc * 128:256 + (fc + 1) * 128], R1[:, k, t * TT:(t + 1) * TT]) for k in range(8)],
                               [("W", slot)] + r1tok(t))
                            sl, sla, slb = tf((f * TG + ti) % 2)
                            act(sl[:], PS[bg][:], AF.Silu, [("ps", bg)], [sla, slb])
                            tt(FF[:, f, ti * TT:(ti + 1) * TT], PS[bu][:], sl[:], ALU.mult, [("ps", bu), sla, slb], [("FF", f, ti)])
                stages.append((spec, p7a))
            for c in range(8):
                spec = spec_down(c)

                def p7b(slot, c=c, tiles=tiles):
                    for ti, t in enumerate(tiles):
                        bz = nbank()
                        mm(bz, [(WD[slot][:, f, :], FF[:, f, ti * TT:(ti + 1) * TT]) for f in range(FC)],
                           [("W", slot)] + [("FF", f, ti) for f in range(FC)])
                        tt(X1[:, c, t * TT:(t + 1) * TT], PS[bz][:], X1[:, c, t * TT:(t + 1) * TT], ALU.add,
                           [("ps", bz), ("X1", c, t)], [("X1", c, t)])
                stages.append((spec, p7b))

            def p7n(tiles=tiles):
                for t in tiles:
                    b = nbank()
                    for k in range(8):
                        sq, sqt = TH[2][k % 2], ("T", 2, k % 2)
                        act(sq[:], X1[:, k, t * TT:(t + 1) * TT], AF.Square, [("X1", k, t)], [sqt])
                        P.add("pe", lambda pe, k=k, sq=sq, b=b: pe.matmul(PS[b][:], lhsT=ones_b, rhs=sq[:], start=(k == 0), stop=(k == 7)),
                              reads=[sqt, ("CB",)], writes=[("ps", b)])
                    rt, ra, rb = rstd_tile(b, 1.0 / D, 1e-6, 3, "sqrt")
                    for k in range(8):
                        stt(X1[:, k, t * TT:(t + 1) * TT], X1[:, k, t * TT:(t + 1) * TT], g_fin(k), rt[:], ALU.mult, ALU.mult,
                            [("X1", k, t), ra, rb, ("CF",)], [("X1", k, t)])
                    dma("sp", outT_v[:, :, tok0 + t * TT: tok0 + (t + 1) * TT], X1[:, :, t * TT:(t + 1) * TT],
                        [("X1", k, t) for k in range(8)], [("OUT", s, t)])
            stages.append((None, p7n))
        return stages

    all_stages = []
    for s in range(NSEQ):
        all_stages.extend(seq_stages(s))
    widx = [i for i, (sp_, _) in enumerate(all_stages) if sp_ is not None]
    slots = {}
    issued = 0
    for i, (sp_, body) in enumerate(all_stages):
        r = sum(1 for x in widx if x < i)
        while issued < len(widx) and issued <= r + 1:
            wi = widx[issued]
            slots[wi] = fill(all_stages[wi][0])
            issued += 1
        if sp_ is None:
            body()
        else:
            body(slots[i])

    P.analyze()
    P.emit()
    return nc, dbg_out, P


def _host_consts(inp, S):
    cf = np.zeros((128, NCF), np.float32)

    def pk(v):
        return np.ascontiguousarray(np.asarray(v, np.float32).reshape(-1, 128).T)
    cf[:, 0:8] = pk(inp["g_mix"][0])
    cf[:, 8:24] = pk(inp["b_gate"][0])
    wdw = np.asarray(inp["w_dw"][0], np.float32)
    cf[:, 24:272] = wdw.T.reshape(8, 128, CW).transpose(1, 0, 2).reshape(128, 8 * CW)
    cf[:, 272:280] = pk(inp["b_dw"][0])
    cf[:, 280:288] = pk(inp["g_conv_ln"][0])
    cf[:, 288:296] = pk(inp["b_conv_ln"][0])
    cf[:, 296:304] = pk(inp["g_ffn"][0])
    cf[:, 304:312] = pk(inp["g_final"])
    cf[:, 312] = np.asarray(inp["g_subln"][0], np.float32)
    for i, nm in enumerate(("lambda_q1", "lambda_k1", "lambda_q2", "lambda_k2")):
        cf[0:64, 313 + i] = np.asarray(inp[nm][0], np.float32)
    cb = np.zeros((128, 1024), np.float32)
    cb[:, 0:128] = 1.0
    for o in (0, 64):
        for i in range(8):
            cb[o + i + 8, 128 + o + i] = -1.0
            cb[o + i, 128 + o + 8 + i] = 1.0
    kk = np.arange(128)
    cb[:, 256:384] = np.eye(128, dtype=np.float32)
    maskneg = np.where(kk[:, None] > kk[None, :], -30000.0, 0.0).astype(np.float32)
    for h in range(2):
        for p in range(64):
            cb[64 * h + p, 512 + p] = 1.0
            cb[64 * h + p, 512 + 128 + p + 64] = 1.0
        cb[64 * h:64 * h + 64, 768:896] = maskneg[0:64]
        cb[64 * h:64 * h + 64, 896:1024] = maskneg[64:128]
    cb = cb.astype(ml_dtypes.bfloat16)
    inv_freq = (np.float32(500000.0) ** (-np.arange(0, 16, 2, dtype=np.float32) / np.float32(16))).astype(np.float32)
    pos = np.arange(S, dtype=np.float32)
    ang = (pos[:, None] * inv_freq[None, :]).astype(np.float32)
    cs, sn = np.cos(ang).astype(np.float32).T, np.sin(ang).astype(np.float32).T
    rope = np.zeros((128, 2, S), np.float32)
    rope[:, 0, :] = 1.0
    for o in (0, 64):
        rope[o:o + 8, 0] = cs
        rope[o + 8:o + 16, 0] = cs
        rope[o:o + 8, 1] = sn
        rope[o + 8:o + 16, 1] = sn
    return cf, cb, rope


def make_in_maps(inp, S, NSEQ, ncores):
    x = np.asarray(inp["x"], np.float32)
    cf, cb, rope = _host_consts(inp, S)
    shared = {
        "w_in": np.ascontiguousarray(inp["w_in"][0], dtype=np.float32),
        "w_conv_pw": np.ascontiguousarray(inp["w_conv_pw"][0], dtype=np.float32),
        "w_attn_pw": np.ascontiguousarray(inp["w_attn_pw"][0], dtype=np.float32),
        "w_out": np.ascontiguousarray(inp["w_out"][0], dtype=np.float32),
        "w_ffn_gate": np.ascontiguousarray(inp["w_ffn_gate"][0], dtype=np.float32),
        "w_ffn_up": np.ascontiguousarray(inp["w_ffn_up"][0], dtype=np.float32),
        "w_ffn_down": np.ascontiguousarray(inp["w_ffn_down"][0], dtype=np.float32),
        "cf32": cf, "cb16": cb, "rope": rope,
    }
    maps = []
    for c in range(ncores):
        xs = x[c * NSEQ:(c + 1) * NSEQ, :S]
        xT = np.ascontiguousarray(xs.reshape(NSEQ * S, D).T)
        m = dict(shared)
        m["xT"] = xT
        maps.append(m)
    return maps


def kernel(**inputs):
    S, NSEQ, NCORES = 2048, 2, 8
    nc, _, _ = build_nc(S, NSEQ)
    in_maps = make_in_maps(inputs, S, NSEQ, NCORES)
    res = run_bass_kernel_spmd(nc, in_maps, core_ids=list(range(NCORES)))
    outs = []
    for c in range(NCORES):
        oT = np.asarray(res.results[c]["outT"], np.float32)
        outs.append(oT.T.reshape(NSEQ, S, D))
    return np.ascontiguousarray(np.concatenate(outs, axis=0), dtype=np.float32)
```

```python
import math
import numpy as np
import ml_dtypes
import concourse.bass as bass
import concourse.mybir as mybir
from concourse.bass_utils import run_bass_kernel_spmd

F32 = mybir.dt.float32
BF16 = mybir.dt.bfloat16
AF = mybir.ActivationFunctionType
ALU = mybir.AluOpType

D = 1024
KC = 8
NH = 8
DFF = 2816
FC = 22
CW = 31
IN_COLS = 7168
C0, C1, C2, C3 = 2048, 3072, 4096, 5120
LAMBDA_INIT = 0.8 - 0.6 * math.exp(-0.3 * 0)
NCF = 320
TT = 512
NDUM_FIN = 0
NPE_TAPS = 26
NDUM_DIAG = 0
DBG_TRI = False

ENGS = ("pe", "act", "dve", "pool", "sp")
SEM_LIMIT = 30000
NDMA_SEM = 8


class Op:
    __slots__ = ("eng", "fn", "reads", "writes", "dma", "deps", "ord", "gen", "slot", "target", "need_inc", "bar")

    def __init__(self, eng, fn, reads, writes, dma):
        self.eng = eng
        self.fn = fn
        self.reads = reads
        self.writes = writes
        self.dma = dma
        self.deps = ()
        self.ord = 0
        self.gen = 0
        self.slot = 0
        self.target = 0
        self.need_inc = False
        self.bar = False


class Prog:
    def __init__(self, nc):
        self.nc = nc
        self.ops = []

    def add(self, eng, fn, reads=(), writes=(), dma=False):
        self.ops.append(Op(eng, fn, tuple(reads), tuple(writes), dma))

    def barrier(self):
        o = Op("bar", None, (), (), False)
        o.bar = True
        self.ops.append(o)

    def analyze(self):
        ops = self.ops
        tok_w, tok_r = {}, {}
        last_c = {}
        last_d = {}
        bar_deps = {}
        dma_cnt = {"sp": 0, "pool": 0}
        for i, op in enumerate(ops):
            if op.bar:
                deps = set(last_c.values())
                for (q, s), j in last_d.items():
                    if q != "pool":
                        deps.add(j)
                for e in ENGS:
                    if e != "pool":
                        bar_deps.setdefault(e, set()).update(deps)
                continue
            deps = set()
            for t in op.reads:
                deps.update(tok_w.get(t, {}).values())
            for t in op.writes:
                deps.update(tok_w.get(t, {}).values())
                deps.update(tok_r.get(t, {}).values())
            if op.eng in bar_deps:
                deps.update(bar_deps.pop(op.eng))
            if op.dma:
                n = dma_cnt[op.eng]
                dma_cnt[op.eng] = n + 1
                op.slot = n % NDMA_SEM
                prev = last_d.get((op.eng, op.slot))
                if prev is not None:
                    deps.add(prev)
                    op.target = ops[prev].target + 16
                else:
                    op.target = 16
                last_d[(op.eng, op.slot)] = i
                key = ("d", i)
            else:
                last_c[op.eng] = i
                key = op.eng
            for t in op.reads:
                tok_r.setdefault(t, {})[key] = i
            for t in op.writes:
                if tok_r.get(t):
                    tok_w[t] = {key: i}
                    tok_r[t] = {}
                else:
                    tok_w.setdefault(t, {})[key] = i
            deps.discard(i)
            best = {}
            out = []
            for d in deps:
                dop = ops[d]
                if dop.dma:
                    out.append(d)
                else:
                    if dop.eng == "pe" and op.eng == "pe" and not op.dma:
                        continue
                    if d > best.get(dop.eng, -1):
                        best[dop.eng] = d
            out.extend(best.values())
            op.deps = tuple(sorted(out))
            for d in op.deps:
                ops[d].need_inc = True
        cnt = {e: 0 for e in ENGS}
        gen = {e: 0 for e in ENGS}
        for op in ops:
            if op.bar or op.dma or not op.need_inc:
                continue
            if cnt[op.eng] >= SEM_LIMIT:
                cnt[op.eng] = 0
                gen[op.eng] += 1
            cnt[op.eng] += 1
            op.ord = cnt[op.eng]
            op.gen = gen[op.eng]
        self.ngen = {e: gen[e] + 1 for e in ENGS}

    def emit(self):
        nc = self.nc
        engobj = {"pe": nc.tensor, "act": nc.scalar, "dve": nc.vector, "pool": nc.gpsimd, "sp": nc.sync}
        csem = {e: [nc.alloc_semaphore(f"c_{e}_{g}") for g in range(self.ngen[e])] for e in ENGS}
        dsem = {q: [nc.alloc_semaphore(f"d_{q}_{s}") for s in range(NDMA_SEM)] for q in ("sp", "pool")}
        waited = {e: {} for e in ENGS}
        ops = self.ops
        final_d = {}
        nwait = 0
        for op in ops:
            if op.bar:
                continue
            E = engobj[op.eng]
            w = waited[op.eng]
            for d in op.deps:
                dop = ops[d]
                if dop.dma:
                    key = ("d", dop.eng, dop.slot)
                    val = dop.target
                    sem = dsem[dop.eng][dop.slot]
                else:
                    key = ("c", dop.eng, dop.gen)
                    val = dop.ord
                    sem = csem[dop.eng][dop.gen]
                if w.get(key, 0) >= val:
                    continue
                E.wait_ge(sem, val)
                nwait += 1
                w[key] = val
            ins = op.fn(E)
            if op.dma:
                ins.then_inc(dsem[op.eng][op.slot], 16)
                final_d[(op.eng, op.slot)] = op.target
            elif op.need_inc:
                ins.then_inc(csem[op.eng][op.gen], 1)
        for (q, s), tgt in sorted(final_d.items()):
            nc.sync.wait_ge(dsem[q][s], tgt)
        self.nwait = nwait


def build_nc(S=2048, NSEQ=2, dbg=()):
    NT = S // TT
    TG = min(2, NT)
    NTOK = S * NSEQ
    nc = bass.Bass("TRN2", target_bir_lowering=False)
    P = Prog(nc)

    def din(name, shape, dt=F32):
        return nc.dram_tensor(name, shape, dt, kind="ExternalInput").ap()

    xT = din("xT", [D, NTOK])
    w_in = din("w_in", [D, IN_COLS])
    w_pw = din("w_conv_pw", [D, D])
    w_ap = din("w_attn_pw", [D, D])
    w_out = din("w_out", [D, D])
    w_fg = din("w_ffn_gate", [D, DFF])
    w_fu = din("w_ffn_up", [D, DFF])
    w_fd = din("w_ffn_down", [DFF, D])
    cf_d = din("cf32", [128, NCF])
    cb_d = din("cb16", [128, 1024], BF16)
    rope_d = din("rope", [128, 2, S])
    outT = nc.dram_tensor("outT", [D, NTOK], F32, kind="ExternalOutput").ap()
    dbg_out = {}

    w_in_v = w_in.rearrange("(k p) c -> p k c", p=128)
    w_pw_v = w_pw.rearrange("(k p) c -> p k c", p=128)
    w_ap_v = w_ap.rearrange("(k p) c -> p k c", p=128)
    w_out_v = w_out.rearrange("(k p) c -> p k c", p=128)
    w_fg_v = w_fg.rearrange("(k p) c -> p k c", p=128)
    w_fu_v = w_fu.rearrange("(k p) c -> p k c", p=128)
    w_fd_v = w_fd.rearrange("(f p) c -> p f c", p=128)
    xT_v = xT.rearrange("(k p) t -> p k t", p=128)
    outT_v = outT.rearrange("(k p) t -> p k t", p=128)

    def sb(name, shape, dt, off):
        return nc.alloc_sbuf_tensor_at(name, shape, dt, offset=off)

    BASE = 24576 - 256
    SB2 = S * 2 * 8
    o_R1 = BASE
    o_R2 = o_R1 + SB2
    o_A = o_R2 + SB2
    A_SZ = 49152
    o_R3 = o_A + A_SZ
    o_W = o_R3 + SB2
    NW = 3
    o_ROPE = o_W + NW * 8192
    o_T = o_ROPE + 2 * S * 4
    assert o_T + 16384 <= 229376 - 256, o_T
    R1 = sb("R1", [128, 8, S], BF16, o_R1)
    R2 = sb("R2", [128, 8, S], BF16, o_R2)
    R3 = sb("R3", [128, 8, S], BF16, o_R3)
    X1 = sb("X1", [128, 8, S], F32, o_R2)
    assert 8 * S * 4 <= SB2 + 32768
    FF = sb("FF", [128, FC, TG * TT], BF16, o_A + 32768)
    assert FC * TG * TT * 2 <= 16384 + SB2 or S < 2048
    QT = sb("QT", [128, 4, S], BF16, o_A)
    KT = sb("KT", [128, 4, S], BF16, o_A + 16384)
    VV = sb("VV", [128, S // 128, 512], BF16, o_A + 32768)
    XS = [sb(f"XS{i}", [128, 8, TT], F32, o_A + i * 16384) for i in range(2)]
    UL = 30 + S
    ub = ((UL * 2 + 31) // 32) * 32
    U16 = [sb(f"U16_{i}", [128, UL], BF16, o_A + i * ub) for i in range(2)]
    DG = [sb(f"DG{i}", [128, CW, 128], BF16, o_A + 2 * ub + i * CW * 256) for i in range(2)]
    assert 2 * ub + 2 * CW * 256 <= A_SZ
    WS = [sb(f"WS{j}", [128, 8, 512], BF16, o_W + j * 8192) for j in range(NW)]
    WD = [sb(f"WD{j}", [128, FC, 128], BF16, o_W + j * 8192) for j in range(NW)]
    RC = sb("RC", [128, S], F32, o_ROPE)
    RS = sb("RS", [128, S], F32, o_ROPE + S * 4)
    TF = [sb(f"TF{i}", [128, TT], F32, o_T + i * 2048) for i in range(8)]
    TH = [[sb(f"TH{i}_{h}", [128, TT], BF16, o_T + i * 2048 + h * 1024) for h in range(2)] for i in range(8)]

    PT2 = [sb(f"PT2_{i}", [128, 2, TT], BF16, o_T + i * 2048) for i in range(2)]
    LT = sb("LT", [128, 2, TT], F32, o_T + 2 * 2048)

    def tf(i):
        return TF[i], ("T", i, 0), ("T", i, 1)

    AUX = 16768
    CF = sb("CF", [128, NCF], F32, AUX)
    CB = sb("CB", [128, 1024], BF16, AUX + 1280)
    LP = sb("LP", [128, 8], F32, AUX + 1280 + 2048)
    RSB = sb("RSB", [128, TT], F32, AUX + 1280 + 2048 + 64)
    PT2.append(sb("PT2_2", [128, 2, TT], BF16, AUX + 1280 + 2048 + 64 + 2048))
    assert AUX + 1280 + 2048 + 64 + 4096 <= BASE
    ONESF = sb("ONESF", [128, 128], F32, o_T)
    ones_b = CB[:, 0:128]
    pm_b = CB[:, 128:256]
    ident_b = CB[:, 256:384]
    zeros_b = CB[:, 384:512]
    LM = lambda h, a: CB[64 * h:64 * h + 64, 512 + a * 128:512 + (a + 1) * 128]
    RM = lambda h, a: CB[64 * h:64 * h + 64, 768 + a * 128:768 + (a + 1) * 128]
    g_mix = lambda k: CF[:, k:k + 1]
    b_gate = lambda c: CF[:, 8 + c:9 + c]
    w_dw = lambda c, j: CF[:, 24 + c * CW + j:25 + c * CW + j]
    b_dw = lambda c: CF[:, 272 + c:273 + c]
    g_ln = lambda c: CF[:, 280 + c:281 + c]
    b_ln = lambda c: CF[:, 288 + c:289 + c]
    g_ffn = lambda k: CF[:, 296 + k:297 + k]
    g_fin = lambda k: CF[:, 304 + k:305 + k]
    neglam = LP[:, 4:5]
    gsubs = LP[:, 5:6]

    PSA = nc.alloc_psum_tensor("psa", [128, 8, TT], F32)

    class _B:
        def __init__(self, b):
            self.b = b

        def __getitem__(self, idx):
            if not isinstance(idx, tuple):
                idx = (idx,)
            return PSA[(idx[0], self.b) + tuple(idx[1:])]
    PS = [_B(i) for i in range(8)]
    bank_ctr = [0]

    def nbank():
        b = bank_ctr[0] % 8
        bank_ctr[0] += 1
        return b

    def mm(bank, pairs, reads, ncols=None, c0=0):
        out_ap = PS[bank][:, c0:(c0 + ncols)] if ncols is not None else PS[bank][:]

        def fn(pe, pairs=pairs, out_ap=out_ap):
            n = len(pairs)
            ins = None
            for i, (l, r) in enumerate(pairs):
                ins = pe.matmul(out_ap, lhsT=l, rhs=r, start=(i == 0), stop=(i == n - 1))
            return ins
        P.add("pe", fn, reads=reads, writes=[("ps", bank)])

    def act(out, in_, func, reads, writes, bias=None, scale=None):
        kw = {}
        if bias is not None:
            kw["bias"] = bias
        if scale is not None:
            kw["scale"] = scale
        P.add("act", lambda e, kw=kw: e.activation(out=out, in_=in_, func=func, **kw), reads=reads, writes=writes)

    def tt(out, in0, in1, op, reads, writes, eng="dve"):
        P.add(eng, lambda e: e.tensor_tensor(out=out, in0=in0, in1=in1, op=op), reads=reads, writes=writes)

    def stt(out, in0, scalar, in1, op0, op1, reads, writes):
        P.add("dve", lambda e: e.scalar_tensor_tensor(out=out, in0=in0, scalar=scalar, in1=in1, op0=op0, op1=op1),
              reads=reads, writes=writes)

    def ts(out, in0, s1, s2, op0, op1, reads, writes):
        if s2 is None:
            P.add("dve", lambda e: e.tensor_scalar(out=out, in0=in0, scalar1=s1, scalar2=None, op0=op0),
                  reads=reads, writes=writes)
        else:
            P.add("dve", lambda e: e.tensor_scalar(out=out, in0=in0, scalar1=s1, scalar2=s2, op0=op0, op1=op1),
                  reads=reads, writes=writes)

    def recip(out, in_, reads, writes):
        P.add("dve", lambda e: e.reciprocal(out=out, in_=in_), reads=reads, writes=writes)

    def dma(q, out, in_, reads, writes):
        P.add(q, lambda e: e.dma_start(out=out, in_=in_), reads=reads, writes=writes, dma=True)

    def dump(name, ap, shape, dt, reads):
        if name not in dbg:
            return
        t = nc.dram_tensor("dbg_" + name, shape, dt, kind="ExternalOutput").ap()
        dbg_out[name] = t
        dma("sp", t, ap, reads, [("dbg", name)])

    def rstd_tile(bank, scale, eps, slot, mode):
        t, ta, tb = tf(slot)
        act(t[:], PS[bank][:], AF.Ln, [("ps", bank), ("LPe",)], [ta, tb], bias=eps_ap(eps), scale=scale)
        act(t[:], t[:], AF.Exp, [ta, tb], [ta, tb], scale=-0.5)
        return t, ta, tb

    eps_cols = {1e-6: 6, 1e-5: 7}

    def eps_ap(eps):
        c = eps_cols[eps]
        return LP[:, c:c + 1]

    dma("sp", CF[:], cf_d, [], [("CF",)])
    dma("sp", CB[:], cb_d, [], [("CB",)])
    dma("sp", RC[:], rope_d[:, 0, :], [], [("ROPE",)])
    dma("sp", RS[:], rope_d[:, 1, :], [], [("ROPE",)])
    P.add("dve", lambda e: e.memset(ONESF[:], 1.0), writes=[("T", 0, 0)])
    P.add("dve", lambda e: e.memset(LP[:, 6:7], 1e-6), writes=[("LPe",)])
    P.add("dve", lambda e: e.memset(LP[:, 7:8], 1e-5), writes=[("LPe",)])
    tt(LP[:, 0:1], CF[:, 313:314], CF[:, 314:315], ALU.mult, [("CF",)], [("LP0",)])
    tt(LP[:, 1:2], CF[:, 315:316], CF[:, 316:317], ALU.mult, [("CF",)], [("LP0",)])
    b0 = nbank()
    P.add("pe", lambda pe: pe.matmul(PS[b0][:, 0:2], lhsT=ONESF[:], rhs=LP[:, 0:2], start=True, stop=True),
          reads=[("T", 0, 0), ("LP0",)], writes=[("ps", b0)])
    act(LP[:, 2:4], PS[b0][:, 0:2], AF.Exp, [("ps", b0)], [("LP1",)])
    tt(LP[:, 4:5], LP[:, 3:4], LP[:, 2:3], ALU.subtract, [("LP1",)], [("LP2",)])
    ts(LP[:, 4:5], LP[:, 4:5], -LAMBDA_INIT, None, ALU.add, None, [("LP2",)], [("LP2",)])
    ts(LP[:, 5:6], CF[:, 312:313], 1.0 - LAMBDA_INIT, None, ALU.mult, None, [("CF",)], [("LP3",)])
    CONST_R = [("CF",), ("CB",), ("ROPE",), ("LP2",), ("LP3",), ("LPe",)]

    wstate = {"n": 0}

    def fill(spec):
        j = wstate["n"] % NW
        wstate["n"] += 1
        for (dstf, src) in spec:
            dma("pool", dstf(j), src, [], [("W", j)])
        return j

    def spec_cols(view, ranges):
        sp = []
        o = 0
        for (c0, n) in ranges:
            sp.append((lambda j, o=o, n=n: WS[j][:, :, o:o + n], view[:, :, c0:c0 + n]))
            o += n
        return sp

    def spec_down(c):
        return [(lambda j: WD[j][:], w_fd_v[:, :, c * 128:(c + 1) * 128])]

    def r1tok(t):
        return [("R1", k, t) for k in range(8)]

    def r2tok(t):
        return [("R2", k, t) for k in range(8)]

    def r3tok(t):
        return [("R3", k, t) for k in range(8)]

    def seq_stages(s):
        stages = []
        tok0 = s * S

        def p1():
            P.barrier()
            for t in range(NT):
                xs = XS[t % 2]
                xtok = ("XS", t % 2)
                dma("sp", xs[:], xT_v[:, :, tok0 + t * TT: tok0 + (t + 1) * TT], [], [xtok])
                b = nbank()
                for k in range(8):
                    sq = TH[0][k % 2]
                    sqt = ("T", 0, k % 2)
                    act(sq[:], xs[:, k, :], AF.Square, [xtok], [sqt])
                    P.add("pe", lambda pe, k=k, sq=sq, b=b: pe.matmul(PS[b][:], lhsT=ones_b, rhs=sq[:], start=(k == 0), stop=(k == 7)),
                          reads=[sqt, ("CB",)], writes=[("ps", b)])
                rt, ra, rb = rstd_tile(b, 1.0 / D, 1e-6, 1, "sqrt")
                for k in range(8):
                    stt(R1[:, k, t * TT:(t + 1) * TT], xs[:, k, :], g_mix(k), rt[:], ALU.mult, ALU.mult,
                        [xtok, ra, rb, ("CF",)], [("R1", k, t)])
            dump("hT", R1[:], [128, 8, S], BF16, [x for t in range(NT) for x in r1tok(t)])
        stages.append((None, p1))

        for j2 in range(4):
            spec = spec_cols(w_in_v, [(j2 * 256, 256), (1024 + j2 * 256, 256)])

            def p2a(slot, j2=j2):
                if j2 == 0:
                    P.barrier()
                    for i in range(2):
                        P.add("dve", lambda e, i=i: e.memset(U16[i][:, 0:30], 0.0), writes=[("U16pad", i)])
                for cc in range(2):
                    c = j2 * 2 + cc
                    ui = c % 2
                    for j in range(NPE_TAPS):
                        P.add("dve", lambda e, j=j, c=c, ui=ui: e.tensor_scalar(out=DG[ui][:, j, :], in0=ident_b, scalar1=w_dw(c, j),
                                                                               scalar2=None, op0=ALU.mult),
                              reads=[("CB",), ("CF",)], writes=[("DG", ui)])
                    for t in range(NT):
                        ba, bg = nbank(), nbank()
                        mm(ba, [(WS[slot][:, k, cc * 128:(cc + 1) * 128], R1[:, k, t * TT:(t + 1) * TT]) for k in range(8)],
                           [("W", slot)] + r1tok(t))
                        mm(bg, [(WS[slot][:, k, 256 + cc * 128:256 + (cc + 1) * 128], R1[:, k, t * TT:(t + 1) * TT]) for k in range(8)],
                           [("W", slot)] + r1tok(t))
                        sg, sa, sb_ = tf(t % 2)
                        act(sg[:], PS[bg][:], AF.Sigmoid, [("ps", bg)], [sa, sb_])
                        tt(U16[ui][:, 30 + t * TT:30 + (t + 1) * TT], PS[ba][:], sg[:], ALU.mult,
                           [("ps", ba), sa, sb_], [("U16", ui, t)])
                    for t in range(NT):
                        bc = nbank()
                        rd = [("DG", ui), ("U16", ui, t), ("U16pad", ui)] + ([("U16", ui, t - 1)] if t > 0 else [])
                        mm(bc, [(DG[ui][:, j, :], U16[ui][:, t * TT + j:t * TT + j + TT]) for j in range(NPE_TAPS)], rd)
                        ca, caa, cab = tf(4 + (t % 2))
                        urd = [x for x in rd if x[0] != "DG"] + [("CF",)]
                        ts(ca[:], U16[ui][:, t * TT + NPE_TAPS:t * TT + NPE_TAPS + TT], w_dw(c, NPE_TAPS), None, ALU.mult, None,
                           urd, [caa, cab])
                        for j in range(NPE_TAPS + 1, CW):
                            stt(ca[:], U16[ui][:, t * TT + j:t * TT + j + TT], w_dw(c, j), ca[:], ALU.mult, ALU.add,
                                urd + [caa, cab], [caa, cab])
                        stt(R2[:, c, t * TT:(t + 1) * TT], PS[bc][:], b_dw(c), ca[:], ALU.add, ALU.add,
                            [("ps", bc), ("CF",), caa, cab], [("R2", c, t)])
            stages.append((spec, p2a))

        def p2ln():
            dump("cb", R2[:], [128, 8, S], BF16, [x for t in range(NT) for x in r2tok(t)])

            def ln_stats(t):
                bs, bq = nbank(), nbank()
                for c in range(8):
                    cbv = R2[:, c, t * TT:(t + 1) * TT]
                    sq = TH[2][c % 2]
                    sqt = ("T", 2, c % 2)
                    act(sq[:], cbv, AF.Square, [("R2", c, t)], [sqt])
                    P.add("pe", lambda pe, c=c, cbv=cbv, bs=bs: pe.matmul(PS[bs][:], lhsT=ones_b, rhs=cbv, start=(c == 0), stop=(c == 7)),
                          reads=[("R2", c, t), ("CB",)], writes=[("ps", bs)])
                    P.add("pe", lambda pe, c=c, sq=sq, bq=bq: pe.matmul(PS[bq][:], lhsT=ones_b, rhs=sq[:], start=(c == 0), stop=(c == 7)),
                          reads=[sqt, ("CB",)], writes=[("ps", bq)])
                m, ma, mb = tf(3 + 2 * (t % 2))
                q2, qa, qb = tf(4 + 2 * (t % 2))
                ts(m[:], PS[bs][:], 1.0 / D, None, ALU.mult, None, [("ps", bs)], [ma, mb])
                tt(q2[:], m[:], m[:], ALU.mult, [ma, mb], [qa, qb])
                stt(q2[:], PS[bq][:], 1.0 / D, q2[:], ALU.mult, ALU.subtract, [("ps", bq), qa, qb], [qa, qb])
                act(q2[:], q2[:], AF.Ln, [qa, qb, ("LPe",)], [qa, qb], bias=eps_ap(1e-5), scale=1.0)
                act(q2[:], q2[:], AF.Exp, [qa, qb], [qa, qb], scale=-0.5)
                tt(m[:], m[:], q2[:], ALU.mult, [ma, mb, qa, qb], [ma, mb])
                return (m, ma, mb, q2, qa, qb)

            def ln_norm(t, st):
                m, ma, mb, q2, qa, qb = st
                for c in range(8):
                    cbv = R2[:, c, t * TT:(t + 1) * TT]
                    t1, t1a, t1b = tf(c % 2)
                    tt(t1[:], cbv, q2[:], ALU.mult, [("R2", c, t), qa, qb], [t1a, t1b])
                    tt(t1[:], t1[:], m[:], ALU.subtract, [t1a, t1b, ma, mb], [t1a, t1b])
                    act(cbv, t1[:], AF.Silu, [t1a, t1b, ("CF",)], [("R2", c, t)], bias=b_ln(c), scale=g_ln(c))

            st = ln_stats(0)
            for t in range(NT):
                nxt = ln_stats(t + 1) if t + 1 < NT else None
                ln_norm(t, st)
                st = nxt
            dump("cs", R2[:], [128, 8, S], BF16, [x for t in range(NT) for x in r2tok(t)])
        stages.append((None, p2ln))

        for j2 in range(4):
            spec = spec_cols(w_pw_v, [(j2 * 256, 256)]) + [
                (lambda j: WS[j][:, :, 256:512], w_in_v[:, :, C3 + j2 * 256: C3 + (j2 + 1) * 256])]

            def p2b(slot, j2=j2):
                for cc in range(2):
                    c = j2 * 2 + cc
                    for t in range(NT):
                        by, bg = nbank(), nbank()
                        mm(by, [(WS[slot][:, k, cc * 128:(cc + 1) * 128], R2[:, k, t * TT:(t + 1) * TT]) for k in range(8)],
                           [("W", slot)] + r2tok(t))
                        mm(bg, [(WS[slot][:, k, 256 + cc * 128:256 + (cc + 1) * 128], R1[:, k, t * TT:(t + 1) * TT]) for k in range(8)],
                           [("W", slot)] + r1tok(t))
                        sg, sa, sb_ = tf((c * NT + t) % 2)
                        act(sg[:], PS[bg][:], AF.Sigmoid, [("ps", bg), ("CF",)], [sa, sb_], bias=b_gate(c))
                        tt(R3[:, c, t * TT:(t + 1) * TT], PS[by][:], sg[:], ALU.mult, [("ps", by), sa, sb_], [("R3", c, t)])
            stages.append((spec, p2b))

        def dump_m1():
            dump("m1", R3[:], [128, 8, S], BF16, [x for t in range(NT) for x in r3tok(t)])
        stages.append((None, dump_m1))

        for hg in range(2):
            for which in range(2):
                spec = spec_cols(w_in_v, [((C0 if which == 0 else C1) + hg * 512, 512)])

                def p3qk(slot, hg=hg, which=which):
                    if hg == 0 and which == 0:
                        P.barrier()
                    dst = QT if which == 0 else KT
                    dn = "QT" if which == 0 else "KT"
                    for hh in range(4):
                        for t in range(NT):
                            bz = nbank()
                            mm(bz, [(WS[slot][:, k, hh * 128:(hh + 1) * 128], R1[:, k, t * TT:(t + 1) * TT]) for k in range(8)],
                               [("W", slot)] + r1tok(t))
                            i2 = (hh * NT + t) % 2
                            qb_ = TH[0][i2]
                            qbt = ("T", 0, i2)
                            act(qb_[:], PS[bz][:], AF.Copy, [("ps", bz)], [qbt])
                            bp = nbank()
                            P.add("pe", lambda pe, bp=bp, qb_=qb_: pe.matmul(PS[bp][:], lhsT=pm_b, rhs=qb_[:], start=True, stop=True),
                                  reads=[qbt, ("CB",)], writes=[("ps", bp)])
                            tS, tSa, tSb = tf(1 + i2)
                            qf, qfa, qfb = tf(3 + i2)
                            tt(tS[:], PS[bp][:], RS[:, t * TT:(t + 1) * TT], ALU.mult, [("ps", bp), ("ROPE",)], [tSa, tSb])
                            tt(qf[:], PS[bz][:], RC[:, t * TT:(t + 1) * TT], ALU.mult, [("ps", bz), ("ROPE",)], [qfa, qfb])
                            tt(dst[:, hh, t * TT:(t + 1) * TT], qf[:], tS[:], ALU.add, [tSa, tSb, qfa, qfb], [(dn, hh, t)])
                stages.append((spec, p3qk))
            spec = spec_cols(w_in_v, [(C2 + hg * 512, 512)])

            def p3v(slot, hg=hg):
                for tq in range(S // 128):
                    bv = nbank()
                    mm(bv, [(R1[:, k, tq * 128:(tq + 1) * 128], WS[slot][:, k, :]) for k in range(8)],
                       [("W", slot)] + r1tok(tq // 4))
                    P.add("act", lambda e, tq=tq, bv=bv: e.activation(out=VV[:, tq, :], in_=PS[bv][:], func=AF.Copy),
                          reads=[("ps", bv)], writes=[("VV", tq)])
                if hg == 0:
                    dump("qT", QT[:], [128, 4, S], BF16, [("QT", hh, t) for hh in range(4) for t in range(NT)])
                    dump("kT", KT[:], [128, 4, S], BF16, [("KT", hh, t) for hh in range(4) for t in range(NT)])
                    dump("vv", VV[:], [128, S // 128, 512], BF16, [("VV", tq) for tq in range(S // 128)])
            stages.append((spec, p3v))

            def p4(hg=hg):
                its = [(hh, i, j) for hh in range(4) for i in range(NT) for j in range(4 * (i + 1))]
                BO1, BO2, BD1, BD2 = 4, 5, 6, 7

                def geom(n):
                    hh, i, j = its[n]
                    jd = j - 4 * i
                    q0 = jd * 128 if jd > 0 else 0
                    return hh, i, j, jd, q0

                def emit_S(n):
                    hh, i, j, jd, q0 = geom(n)
                    b = 2 * (n % 2)
                    qs = slice(i * TT + q0, (i + 1) * TT)

                    def fn_s(pe, j=j, hh=hh, qs=qs, b=b, q0=q0, diag=(jd >= 0)):
                        pe.matmul(PS[b][:, q0:TT], lhsT=KT[0:64, hh, j * 128:(j + 1) * 128], rhs=QT[0:64, hh, qs],
                                  start=True, stop=not diag)
                        ins = pe.matmul(PS[b + 1][:, q0:TT], lhsT=KT[64:128, hh, j * 128:(j + 1) * 128], rhs=QT[64:128, hh, qs],
                                        start=True, stop=not diag)
                        if diag:
                            for a in range(2):
                                pe.matmul(PS[b][:, q0:q0 + 128], lhsT=LM(0, a), rhs=RM(0, a), start=False, stop=(a == 1))
                                ins = pe.matmul(PS[b + 1][:, q0:q0 + 128], lhsT=LM(1, a), rhs=RM(1, a), start=False, stop=(a == 1))
                        return ins
                    P.add("pe", fn_s, reads=[("KT", hh, j // 4), ("QT", hh, i), ("CB",)], writes=[("ps", b), ("ps", b + 1)])

                def emit_exp(n):
                    hh, i, j, jd, q0 = geom(n)
                    pi = n % 3
                    b = 2 * (n % 2)
                    ptok = [("T", pi, 0), ("T", pi, 1)] if pi < 2 else [("PT2x",)]
                    act(PT2[pi][:, :, q0:TT], PSA[:, b:b + 2, q0:TT], AF.Exp, [("ps", b), ("ps", b + 1)], ptok, scale=0.125)

                def emit_PV(n):
                    hh, i, j, jd, q0 = geom(n)
                    pi = n % 3
                    ptok = [("T", pi, 0), ("T", pi, 1)] if pi < 2 else [("PT2x",)]
                    vv = VV[:, j, hh * 128:(hh + 1) * 128]
                    first, lastj = (j == 0), (j == 4 * (i + 1) - 1)

                    def fn_pv(pe, vv=vv, pi=pi, q0=q0, first=first, lastj=lastj):
                        pe.matmul(PS[BO1][:, q0:TT], lhsT=vv, rhs=PT2[pi][:, 0, q0:TT], start=first, stop=lastj)
                        pe.matmul(PS[BD1][:, q0:TT], lhsT=ones_b, rhs=PT2[pi][:, 0, q0:TT], start=first, stop=lastj)
                        pe.matmul(PS[BO2][:, q0:TT], lhsT=vv, rhs=PT2[pi][:, 1, q0:TT], start=first, stop=lastj)
                        return pe.matmul(PS[BD2][:, q0:TT], lhsT=ones_b, rhs=PT2[pi][:, 1, q0:TT], start=first, stop=lastj)
                    P.add("pe", fn_pv, reads=[("VV", j)] + ptok + [("CB",)],
                          writes=[("ps", BO1), ("ps", BO2), ("ps", BD1), ("ps", BD2)])
                    return lastj

                ltok = [("T", 2, 0), ("T", 2, 1), ("T", 3, 0), ("T", 3, 1)]
                ctok = [("T", 4, 0), ("T", 4, 1), ("T", 5, 0), ("T", 5, 1)]
                C12 = sb(f"C12_{hg}_{s}", [128, 2, TT], F32, o_T + 4 * 2048)

                def finalize(n):
                    hh, i, j, jd, q0 = geom(n)
                    head = hg * 4 + hh
                    oo, ooa, oob = tf(6)
                    sq = TH[7][0]
                    act(LT[:], PSA[:, BD1:BD2 + 1, :], AF.Ln, [("ps", BD1), ("ps", BD2)], ltok)
                    P.add("dve", lambda e: e.tensor_copy(out=C12[:], in_=PSA[:, BO1:BO2 + 1, :]),
                          reads=[("ps", BO1), ("ps", BO2)], writes=ctok)
                    yield
                    act(LT[:], LT[:], AF.Exp, ltok, ltok, scale=-1.0)
                    tt(C12[:], C12[:], LT[:], ALU.mult, ctok + ltok, ctok)
                    stt(oo[:], C12[:, 1, :], neglam, C12[:, 0, :], ALU.mult, ALU.add, ctok + [("LP2",)], [ooa, oob])
                    tt(sq[:], oo[:], oo[:], ALU.mult, [ooa, oob], [("T", 7, 0)])
                    yield
                    yield
                    yield
                    bank = 2 * ((cur[0] + 1) % 2)
                    P.add("pe", lambda pe, bank=bank: pe.matmul(PS[bank][:], lhsT=ones_b, rhs=sq[:], start=True, stop=True),
                          reads=[("T", 7, 0), ("CB",)], writes=[("ps", bank)])
                    rs_, rsa, rsb = RSB, ("RSB", 0), ("RSB", 1)
                    act(rs_[:], PS[bank][:], AF.Ln, [("ps", bank), ("LPe",)], [rsa, rsb], bias=eps_ap(1e-6), scale=1.0 / 128)
                    yield
                    act(rs_[:], rs_[:], AF.Exp, [rsa, rsb], [rsa, rsb], scale=-0.5)
                    stt(R2[:, head, i * TT:(i + 1) * TT], oo[:], gsubs, rs_[:], ALU.mult, ALU.mult,
                        [ooa, oob, rsa, rsb, ("LP3",)], [("R2", head, i)])

                N = len(its)
                cur = [0]
                queue = []
                emit_S(0)
                if N > 1:
                    emit_S(1)
                emit_exp(0)
                def emit_dummy(n, k):
                    hh, i, j, jd, q0 = geom(n)
                    if j == 4 * (i + 1) - 1 or k == 0:
                        return
                    def fn_d(pe, hh=hh, k=k):
                        ins = None
                        for _ in range(k):
                            ins = pe.matmul(PS[BO1][:], lhsT=zeros_b, rhs=KT[:, hh, 0:TT], start=False, stop=False)
                        return ins
                    P.add("pe", fn_d, reads=[("KT", hh, 0), ("CB",)], writes=[("ps", BO1)])

                for n in range(N):
                    cur[0] = n
                    if n + 2 < N:
                        emit_S(n + 2)
                    lastj = emit_PV(n)
                    emit_dummy(n, NDUM_FIN if queue else (NDUM_DIAG if geom(n)[3] >= 0 else 0))
                    if n + 1 < N:
                        emit_exp(n + 1)
                    for g in list(queue):
                        try:
                            next(g)
                        except StopIteration:
                            queue.remove(g)
                    if lastj:
                        g = finalize(n)
                        next(g)
                        queue.append(g)
                while queue:
                    cur[0] += 1
                    for g in list(queue):
                        try:
                            next(g)
                        except StopIteration:
                            queue.remove(g)
            stages.append((None, p4))

        def dump_on():
            dump("on", R2[:], [128, 8, S], BF16, [x for t in range(NT) for x in r2tok(t)])
        stages.append((None, dump_on))

        for j2 in range(4):
            spec = spec_cols(w_ap_v, [(j2 * 256, 256)]) + [
                (lambda j: WS[j][:, :, 256:512], w_in_v[:, :, C3 + D + j2 * 256: C3 + D + (j2 + 1) * 256])]

            def p5(slot, j2=j2):
                for cc in range(2):
                    c = j2 * 2 + cc
                    for t in range(NT):
                        by, bg = nbank(), nbank()
                        mm(by, [(WS[slot][:, k, cc * 128:(cc + 1) * 128], R2[:, k, t * TT:(t + 1) * TT]) for k in range(8)],
                           [("W", slot)] + r2tok(t))
                        mm(bg, [(WS[slot][:, k, 256 + cc * 128:256 + (cc + 1) * 128], R1[:, k, t * TT:(t + 1) * TT]) for k in range(8)],
                           [("W", slot)] + r1tok(t))
                        i2 = (c * NT + t) % 2
                        sg, sa, sb_ = tf(i2)
                        tm, tma, tmb = tf(2 + i2)
                        act(sg[:], PS[bg][:], AF.Sigmoid, [("ps", bg), ("CF",)], [sa, sb_], bias=b_gate(8 + c))
                        tt(tm[:], PS[by][:], sg[:], ALU.mult, [("ps", by), sa, sb_], [tma, tmb])
                        tt(R3[:, c, t * TT:(t + 1) * TT], tm[:], R3[:, c, t * TT:(t + 1) * TT], ALU.add,
                           [tma, tmb, ("R3", c, t)], [("R3", c, t)])
            stages.append((spec, p5))

        def dump_mg():
            dump("mg", R3[:], [128, 8, S], BF16, [x for t in range(NT) for x in r3tok(t)])
        stages.append((None, dump_mg))

        for j2 in range(2):
            spec = spec_cols(w_out_v, [(j2 * 512, 512)])

            def p6(slot, j2=j2):
                if j2 == 0:
                    P.barrier()
                for cc in range(4):
                    c = j2 * 4 + cc
                    for t in range(NT):
                        bz = nbank()
                        mm(bz, [(WS[slot][:, k, cc * 128:(cc + 1) * 128], R3[:, k, t * TT:(t + 1) * TT]) for k in range(8)],
                           [("W", slot)] + r3tok(t))
                        i2 = (c * NT + t) % 2
                        xs, xa, xb = tf(i2)
                        dma("sp", xs[:], xT[c * 128:(c + 1) * 128, tok0 + t * TT: tok0 + (t + 1) * TT], [], [xa, xb])
                        tt(X1[:, c, t * TT:(t + 1) * TT], PS[bz][:], xs[:], ALU.add, [("ps", bz), xa, xb], [("X1", c, t)])
            stages.append((spec, p6))

        def p6n():
            P.barrier()
            dump("x1", X1[:], [128, 8, S], F32, [("X1", c, t) for c in range(8) for t in range(NT)])
            for t in range(NT):
                b = nbank()
                for k in range(8):
                    sq, sqt = TH[2][k % 2], ("T", 2, k % 2)
                    act(sq[:], X1[:, k, t * TT:(t + 1) * TT], AF.Square, [("X1", k, t)], [sqt])
                    P.add("pe", lambda pe, k=k, sq=sq, b=b: pe.matmul(PS[b][:], lhsT=ones_b, rhs=sq[:], start=(k == 0), stop=(k == 7)),
                          reads=[sqt, ("CB",)], writes=[("ps", b)])
                rt, ra, rb = rstd_tile(b, 1.0 / D, 1e-6, 3, "sqrt")
                for k in range(8):
                    stt(R1[:, k, t * TT:(t + 1) * TT], X1[:, k, t * TT:(t + 1) * TT], g_ffn(k), rt[:], ALU.mult, ALU.mult,
                        [("X1", k, t), ra, rb, ("CF",)], [("R1", k, t)])
        stages.append((None, p6n))

        for g in range(NT // TG):
            tiles = [g * TG + x for x in range(TG)]
            for fs in range(11):
                spec = spec_cols(w_fg_v, [(fs * 256, 256)]) + [
                    (lambda j: WS[j][:, :, 256:512], w_fu_v[:, :, fs * 256:(fs + 1) * 256])]

                def p7a(slot, fs=fs, tiles=tiles):
                    for fc in range(2):
                        f = fs * 2 + fc
                        for ti, t in enumerate(tiles):
                            bg, bu = nbank(), nbank()
                            mm(bg, [(WS[slot][:, k, fc * 128:(fc + 1) * 128], R1[:, k, t * TT:(t + 1) * TT]) for k in range(8)],
                               [("W", slot)] + r1tok(t))
                            mm(bu, [(WS[slot][:, k, 256 + fc * 128:256 + (fc + 1) * 128], R1[:, k, t * TT:(t + 1) * TT]) for k in range(8)],
                               [("W", slot)] + r1tok(t))
                            sl, sla, slb = tf((f * TG + ti) % 2)
                            act(sl[:], PS[bg][:], AF.Silu, [("ps", bg)], [sla, slb])
                            tt(FF[:, f, ti * TT:(ti + 1) * TT], PS[bu][:], sl[:], ALU.mult, [("ps", bu), sla, slb], [("FF", f, ti)])
                stages.append((spec, p7a))
            for c in range(8):
                spec = spec_down(c)

                def p7b(slot, c=c, tiles=tiles):
                    for ti, t in enumerate(tiles):
                        bz = nbank()
                        mm(bz, [(WD[slot][:, f, :], FF[:, f, ti * TT:(ti + 1) * TT]) for f in range(FC)],
                           [("W", slot)] + [("FF", f, ti) for f in range(FC)])
                        tt(X1[:, c, t * TT:(t + 1) * TT], PS[bz][:], X1[:, c, t * TT:(t + 1) * TT], ALU.add,
                           [("ps", bz), ("X1", c, t)], [("X1", c, t)])
                stages.append((spec, p7b))

            def p7n(tiles=tiles):
                for t in tiles:
                    b = nbank()
                    for k in range(8):
                        sq, sqt = TH[2][k % 2], ("T", 2, k % 2)
                        act(sq[:], X1[:, k, t * TT:(t + 1) * TT], AF.Square, [("X1", k, t)], [sqt])
                        P.add("pe", lambda pe, k=k, sq=sq, b=b: pe.matmul(PS[b][:], lhsT=ones_b, rhs=sq[:], start=(k == 0), stop=(k == 7)),
                              reads=[sqt, ("CB",)], writes=[("ps", b)])
                    rt, ra, rb = rstd_tile(b, 1.0 / D, 1e-6, 3, "sqrt")
                    for k in range(8):
                        stt(X1[:, k, t * TT:(t + 1) * TT], X1[:, k, t * TT:(t + 1) * TT], g_fin(k), rt[:], ALU.mult, ALU.mult,
                            [("X1", k, t), ra, rb, ("CF",)], [("X1", k, t)])
                    dma("sp", outT_v[:, :, tok0 + t * TT: tok0 + (t + 1) * TT], X1[:, :, t * TT:(t + 1) * TT],
                        [("X1", k, t) for k in range(8)], [("OUT", s, t)])
            stages.append((None, p7n))
        return stages

    all_stages = []
    for s in range(NSEQ):
        all_stages.extend(seq_stages(s))
    widx = [i for i, (sp_, _) in enumerate(all_stages) if sp_ is not None]
    slots = {}
    issued = 0
    for i, (sp_, body) in enumerate(all_stages):
        r = sum(1 for x in widx if x < i)
        while issued < len(widx) and issued <= r + 1:
            wi = widx[issued]
            slots[wi] = fill(all_stages[wi][0])
            issued += 1
        if sp_ is None:
            body()
        else:
            body(slots[i])

    P.analyze()
    P.emit()
    return nc, dbg_out, P


def _host_consts(inp, S):
    cf = np.zeros((128, NCF), np.float32)

    def pk(v):
        return np.ascontiguousarray(np.asarray(v, np.float32).reshape(-1, 128).T)
    cf[:, 0:8] = pk(inp["g_mix"][0])
    cf[:, 8:24] = pk(inp["b_gate"][0])
    wdw = np.asarray(inp["w_dw"][0], np.float32)
    cf[:, 24:272] = wdw.T.reshape(8, 128, CW).transpose(1, 0, 2).reshape(128, 8 * CW)
    cf[:, 272:280] = pk(inp["b_dw"][0])
    cf[:, 280:288] = pk(inp["g_conv_ln"][0])
    cf[:, 288:296] = pk(inp["b_conv_ln"][0])
    cf[:, 296:304] = pk(inp["g_ffn"][0])
    cf[:, 304:312] = pk(inp["g_final"])
    cf[:, 312] = np.asarray(inp["g_subln"][0], np.float32)
    for i, nm in enumerate(("lambda_q1", "lambda_k1", "lambda_q2", "lambda_k2")):
        cf[0:64, 313 + i] = np.asarray(inp[nm][0], np.float32)
    cb = np.zeros((128, 1024), np.float32)
    cb[:, 0:128] = 1.0
    for o in (0, 64):
        for i in range(8):
            cb[o + i + 8, 128 + o + i] = -1.0
            cb[o + i, 128 + o + 8 + i] = 1.0
    kk = np.arange(128)
    cb[:, 256:384] = np.eye(128, dtype=np.float32)
    maskneg = np.where(kk[:, None] > kk[None, :], -30000.0, 0.0).astype(np.float32)
    for h in range(2):
        for p in range(64):
            cb[64 * h + p, 512 + p] = 1.0
            cb[64 * h + p, 512 + 128 + p + 64] = 1.0
        cb[64 * h:64 * h + 64, 768:896] = maskneg[0:64]
        cb[64 * h:64 * h + 64, 896:1024] = maskneg[64:128]
    cb = cb.astype(ml_dtypes.bfloat16)
    inv_freq = (np.float32(500000.0) ** (-np.arange(0, 16, 2, dtype=np.float32) / np.float32(16))).astype(np.float32)
    pos = np.arange(S, dtype=np.float32)
    ang = (pos[:, None] * inv_freq[None, :]).astype(np.float32)
    cs, sn = np.cos(ang).astype(np.float32).T, np.sin(ang).astype(np.float32).T
    rope = np.zeros((128, 2, S), np.float32)
    rope[:, 0, :] = 1.0
    for o in (0, 64):
        rope[o:o + 8, 0] = cs
        rope[o + 8:o + 16, 0] = cs
        rope[o:o + 8, 1] = sn
        rope[o + 8:o + 16, 1] = sn
    return cf, cb, rope


def make_in_maps(inp, S, NSEQ, ncores):
    x = np.asarray(inp["x"], np.float32)
    cf, cb, rope = _host_consts(inp, S)
    shared = {
        "w_in": np.ascontiguousarray(inp["w_in"][0], dtype=np.float32),
        "w_conv_pw": np.ascontiguousarray(inp["w_conv_pw"][0], dtype=np.float32),
        "w_attn_pw": np.ascontiguousarray(inp["w_attn_pw"][0], dtype=np.float32),
        "w_out": np.ascontiguousarray(inp["w_out"][0], dtype=np.float32),
        "w_ffn_gate": np.ascontiguousarray(inp["w_ffn_gate"][0], dtype=np.float32),
        "w_ffn_up": np.ascontiguousarray(inp["w_ffn_up"][0], dtype=np.float32),
        "w_ffn_down": np.ascontiguousarray(inp["w_ffn_down"][0], dtype=np.float32),
        "cf32": cf, "cb16": cb, "rope": rope,
    }
    maps = []
    for c in range(ncores):
        xs = x[c * NSEQ:(c + 1) * NSEQ, :S]
        xT = np.ascontiguousarray(xs.reshape(NSEQ * S, D).T)
        m = dict(shared)
        m["xT"] = xT
        maps.append(m)
    return maps


def kernel(**inputs):
    S, NSEQ, NCORES = 2048, 2, 8
    nc, _, _ = build_nc(S, NSEQ)
    in_maps = make_in_maps(inputs, S, NSEQ, NCORES)
    res = run_bass_kernel_spmd(nc, in_maps, core_ids=list(range(NCORES)))
    outs = []
    for c in range(NCORES):
        oT = np.asarray(res.results[c]["outT"], np.float32)
        outs.append(oT.T.reshape(NSEQ, S, D))
    return np.ascontiguousarray(np.concatenate(outs, axis=0), dtype=np.float32)
```

```python
import math
import numpy as np
import ml_dtypes
import concourse.bass as bass
import concourse.mybir as mybir
from concourse.bass_utils import run_bass_kernel_spmd

F32 = mybir.dt.float32
BF16 = mybir.dt.bfloat16
AF = mybir.ActivationFunctionType
ALU = mybir.AluOpType

D = 1024
KC = 8
NH = 8
DFF = 2816
FC = 22
CW = 31
IN_COLS = 7168
C0, C1, C2, C3 = 2048, 3072, 4096, 5120
LAMBDA_INIT = 0.8 - 0.6 * math.exp(-0.3 * 0)
NCF = 320
TT = 512
NDUM_FIN = 0
NPE_TAPS = 26
NDUM_DIAG = 0
DBG_TRI = False

ENGS = ("pe", "act", "dve", "pool", "sp")
SEM_LIMIT = 30000
NDMA_SEM = 8


class Op:
    __slots__ = ("eng", "fn", "reads", "writes", "dma", "deps", "ord", "gen", "slot", "target", "need_inc", "bar")

    def __init__(self, eng, fn, reads, writes, dma):
        self.eng = eng
        self.fn = fn
        self.reads = reads
        self.writes = writes
        self.dma = dma
        self.deps = ()
        self.ord = 0
        self.gen = 0
        self.slot = 0
        self.target = 0
        self.need_inc = False
        self.bar = False


class Prog:
    def __init__(self, nc):
        self.nc = nc
        self.ops = []

    def add(self, eng, fn, reads=(), writes=(), dma=False):
        self.ops.append(Op(eng, fn, tuple(reads), tuple(writes), dma))

    def barrier(self):
        o = Op("bar", None, (), (), False)
        o.bar = True
        self.ops.append(o)

    def analyze(self):
        ops = self.ops
        tok_w, tok_r = {}, {}
        last_c = {}
        last_d = {}
        bar_deps = {}
        dma_cnt = {"sp": 0, "pool": 0}
        for i, op in enumerate(ops):
            if op.bar:
                deps = set(last_c.values())
                for (q, s), j in last_d.items():
                    if q != "pool":
                        deps.add(j)
                for e in ENGS:
                    if e != "pool":
                        bar_deps.setdefault(e, set()).update(deps)
                continue
            deps = set()
            for t in op.reads:
                deps.update(tok_w.get(t, {}).values())
            for t in op.writes:
                deps.update(tok_w.get(t, {}).values())
                deps.update(tok_r.get(t, {}).values())
            if op.eng in bar_deps:
                deps.update(bar_deps.pop(op.eng))
            if op.dma:
                n = dma_cnt[op.eng]
                dma_cnt[op.eng] = n + 1
                op.slot = n % NDMA_SEM
                prev = last_d.get((op.eng, op.slot))
                if prev is not None:
                    deps.add(prev)
                    op.target = ops[prev].target + 16
                else:
                    op.target = 16
                last_d[(op.eng, op.slot)] = i
                key = ("d", i)
            else:
                last_c[op.eng] = i
                key = op.eng
            for t in op.reads:
                tok_r.setdefault(t, {})[key] = i
            for t in op.writes:
                if tok_r.get(t):
                    tok_w[t] = {key: i}
                    tok_r[t] = {}
                else:
                    tok_w.setdefault(t, {})[key] = i
            deps.discard(i)
            best = {}
            out = []
            for d in deps:
                dop = ops[d]
                if dop.dma:
                    out.append(d)
                else:
                    if dop.eng == "pe" and op.eng == "pe" and not op.dma:
                        continue
                    if d > best.get(dop.eng, -1):
                        best[dop.eng] = d
            out.extend(best.values())
            op.deps = tuple(sorted(out))
            for d in op.deps:
                ops[d].need_inc = True
        cnt = {e: 0 for e in ENGS}
        gen = {e: 0 for e in ENGS}
        for op in ops:
            if op.bar or op.dma or not op.need_inc:
                continue
            if cnt[op.eng] >= SEM_LIMIT:
                cnt[op.eng] = 0
                gen[op.eng] += 1
            cnt[op.eng] += 1
            op.ord = cnt[op.eng]
            op.gen = gen[op.eng]
        self.ngen = {e: gen[e] + 1 for e in ENGS}

    def emit(self):
        nc = self.nc
        engobj = {"pe": nc.tensor, "act": nc.scalar, "dve": nc.vector, "pool": nc.gpsimd, "sp": nc.sync}
        csem = {e: [nc.alloc_semaphore(f"c_{e}_{g}") for g in range(self.ngen[e])] for e in ENGS}
        dsem = {q: [nc.alloc_semaphore(f"d_{q}_{s}") for s in range(NDMA_SEM)] for q in ("sp", "pool")}
        waited = {e: {} for e in ENGS}
        ops = self.ops
        final_d = {}
        nwait = 0
        for op in ops:
            if op.bar:
                continue
            E = engobj[op.eng]
            w = waited[op.eng]
            for d in op.deps:
                dop = ops[d]
                if dop.dma:
                    key = ("d", dop.eng, dop.slot)
                    val = dop.target
                    sem = dsem[dop.eng][dop.slot]
                else:
                    key = ("c", dop.eng, dop.gen)
                    val = dop.ord
                    sem = csem[dop.eng][dop.gen]
                if w.get(key, 0) >= val:
                    continue
                E.wait_ge(sem, val)
                nwait += 1
                w[key] = val
            ins = op.fn(E)
            if op.dma:
                ins.then_inc(dsem[op.eng][op.slot], 16)
                final_d[(op.eng, op.slot)] = op.target
            elif op.need_inc:
                ins.then_inc(csem[op.eng][op.gen], 1)
        for (q, s), tgt in sorted(final_d.items()):
            nc.sync.wait_ge(dsem[q][s], tgt)
        self.nwait = nwait


def build_nc(S=2048, NSEQ=2, dbg=()):
    NT = S // TT
    TG = min(2, NT)
    NTOK = S * NSEQ
    nc = bass.Bass("TRN2", target_bir_lowering=False)
    P = Prog(nc)

    def din(name, shape, dt=F32):
        return nc.dram_tensor(name, shape, dt, kind="ExternalInput").ap()

    xT = din("xT", [D, NTOK])
    w_in = din("w_in", [D, IN_COLS])
    w_pw = din("w_conv_pw", [D, D])
    w_ap = din("w_attn_pw", [D, D])
    w_out = din("w_out", [D, D])
    w_fg = din("w_ffn_gate", [D, DFF])
    w_fu = din("w_ffn_up", [D, DFF])
    w_fd = din("w_ffn_down", [DFF, D])
    cf_d = din("cf32", [128, NCF])
    cb_d = din("cb16", [128, 1024], BF16)
    rope_d = din("rope", [128, 2, S])
    outT = nc.dram_tensor("outT", [D, NTOK], F32, kind="ExternalOutput").ap()
    dbg_out = {}

    w_in_v = w_in.rearrange("(k p) c -> p k c", p=128)
    w_pw_v = w_pw.rearrange("(k p) c -> p k c", p=128)
    w_ap_v = w_ap.rearrange("(k p) c -> p k c", p=128)
    w_out_v = w_out.rearrange("(k p) c -> p k c", p=128)
    w_fg_v = w_fg.rearrange("(k p) c -> p k c", p=128)
    w_fu_v = w_fu.rearrange("(k p) c -> p k c", p=128)
    w_fd_v = w_fd.rearrange("(f p) c -> p f c", p=128)
    xT_v = xT.rearrange("(k p) t -> p k t", p=128)
    outT_v = outT.rearrange("(k p) t -> p k t", p=128)

    def sb(name, shape, dt, off):
        return nc.alloc_sbuf_tensor_at(name, shape, dt, offset=off)

    BASE = 24576 - 256
    SB2 = S * 2 * 8
    o_R1 = BASE
    o_R2 = o_R1 + SB2
    o_A = o_R2 + SB2
    A_SZ = 49152
    o_R3 = o_A + A_SZ
    o_W = o_R3 + SB2
    NW = 3
    o_ROPE = o_W + NW * 8192
    o_T = o_ROPE + 2 * S * 4
    assert o_T + 16384 <= 229376 - 256, o_T
    R1 = sb("R1", [128, 8, S], BF16, o_R1)
    R2 = sb("R2", [128, 8, S], BF16, o_R2)
    R3 = sb("R3", [128, 8, S], BF16, o_R3)
    X1 = sb("X1", [128, 8, S], F32, o_R2)
    assert 8 * S * 4 <= SB2 + 32768
    FF = sb("FF", [128, FC, TG * TT], BF16, o_A + 32768)
    assert FC * TG * TT * 2 <= 16384 + SB2 or S < 2048
    QT = sb("QT", [128, 4, S], BF16, o_A)
    KT = sb("KT", [128, 4, S], BF16, o_A + 16384)
    VV = sb("VV", [128, S // 128, 512], BF16, o_A + 32768)
    XS = [sb(f"XS{i}", [128, 8, TT], F32, o_A + i * 16384) for i in range(2)]
    UL = 30 + S
    ub = ((UL * 2 + 31) // 32) * 32
    U16 = [sb(f"U16_{i}", [128, UL], BF16, o_A + i * ub) for i in range(2)]
    DG = [sb(f"DG{i}", [128, CW, 128], BF16, o_A + 2 * ub + i * CW * 256) for i in range(2)]
    assert 2 * ub + 2 * CW * 256 <= A_SZ
    WS = [sb(f"WS{j}", [128, 8, 512], BF16, o_W + j * 8192) for j in range(NW)]
    WD = [sb(f"WD{j}", [128, FC, 128], BF16, o_W + j * 8192) for j in range(NW)]
    RC = sb("RC", [128, S], F32, o_ROPE)
    RS = sb("RS", [128, S], F32, o_ROPE + S * 4)
    TF = [sb(f"TF{i}", [128, TT], F32, o_T + i * 2048) for i in range(8)]
    TH = [[sb(f"TH{i}_{h}", [128, TT], BF16, o_T + i * 2048 + h * 1024) for h in range(2)] for i in range(8)]

    PT2 = [sb(f"PT2_{i}", [128, 2, TT], BF16, o_T + i * 2048) for i in range(2)]
    LT = sb("LT", [128, 2, TT], F32, o_T + 2 * 2048)

    def tf(i):
        return TF[i], ("T", i, 0), ("T", i, 1)

    AUX = 16768
    CF = sb("CF", [128, NCF], F32, AUX)
    CB = sb("CB", [128, 1024], BF16, AUX + 1280)
    LP = sb("LP", [128, 8], F32, AUX + 1280 + 2048)
    RSB = sb("RSB", [128, TT], F32, AUX + 1280 + 2048 + 64)
    PT2.append(sb("PT2_2", [128, 2, TT], BF16, AUX + 1280 + 2048 + 64 + 2048))
    assert AUX + 1280 + 2048 + 64 + 4096 <= BASE
    ONESF = sb("ONESF", [128, 128], F32, o_T)
    ones_b = CB[:, 0:128]
    pm_b = CB[:, 128:256]
    ident_b = CB[:, 256:384]
    zeros_b = CB[:, 384:512]
    LM = lambda h, a: CB[64 * h:64 * h + 64, 512 + a * 128:512 + (a + 1) * 128]
    RM = lambda h, a: CB[64 * h:64 * h + 64, 768 + a * 128:768 + (a + 1) * 128]
    g_mix = lambda k: CF[:, k:k + 1]
    b_gate = lambda c: CF[:, 8 + c:9 + c]
    w_dw = lambda c, j: CF[:, 24 + c * CW + j:25 + c * CW + j]
    b_dw = lambda c: CF[:, 272 + c:273 + c]
    g_ln = lambda c: CF[:, 280 + c:281 + c]
    b_ln = lambda c: CF[:, 288 + c:289 + c]
    g_ffn = lambda k: CF[:, 296 + k:297 + k]
    g_fin = lambda k: CF[:, 304 + k:305 + k]
    neglam = LP[:, 4:5]
    gsubs = LP[:, 5:6]

    PSA = nc.alloc_psum_tensor("psa", [128, 8, TT], F32)

    class _B:
        def __init__(self, b):
            self.b = b

        def __getitem__(self, idx):
            if not isinstance(idx, tuple):
                idx = (idx,)
            return PSA[(idx[0], self.b) + tuple(idx[1:])]
    PS = [_B(i) for i in range(8)]
    bank_ctr = [0]

    def nbank():
        b = bank_ctr[0] % 8
        bank_ctr[0] += 1
        return b

    def mm(bank, pairs, reads, ncols=None, c0=0):
        out_ap = PS[bank][:, c0:(c0 + ncols)] if ncols is not None else PS[bank][:]

        def fn(pe, pairs=pairs, out_ap=out_ap):
            n = len(pairs)
            ins = None
            for i, (l, r) in enumerate(pairs):
                ins = pe.matmul(out_ap, lhsT=l, rhs=r, start=(i == 0), stop=(i == n - 1))
            return ins
        P.add("pe", fn, reads=reads, writes=[("ps", bank)])

    def act(out, in_, func, reads, writes, bias=None, scale=None):
        kw = {}
        if bias is not None:
            kw["bias"] = bias
        if scale is not None:
            kw["scale"] = scale
        P.add("act", lambda e, kw=kw: e.activation(out=out, in_=in_, func=func, **kw), reads=reads, writes=writes)

    def tt(out, in0, in1, op, reads, writes, eng="dve"):
        P.add(eng, lambda e: e.tensor_tensor(out=out, in0=in0, in1=in1, op=op), reads=reads, writes=writes)

    def stt(out, in0, scalar, in1, op0, op1, reads, writes):
        P.add("dve", lambda e: e.scalar_tensor_tensor(out=out, in0=in0, scalar=scalar, in1=in1, op0=op0, op1=op1),
              reads=reads, writes=writes)

    def ts(out, in0, s1, s2, op0, op1, reads, writes):
        if s2 is None:
            P.add("dve", lambda e: e.tensor_scalar(out=out, in0=in0, scalar1=s1, scalar2=None, op0=op0),
                  reads=reads, writes=writes)
        else:
            P.add("dve", lambda e: e.tensor_scalar(out=out, in0=in0, scalar1=s1, scalar2=s2, op0=op0, op1=op1),
                  reads=reads, writes=writes)

    def recip(out, in_, reads, writes):
        P.add("dve", lambda e: e.reciprocal(out=out, in_=in_), reads=reads, writes=writes)

    def dma(q, out, in_, reads, writes):
        P.add(q, lambda e: e.dma_start(out=out, in_=in_), reads=reads, writes=writes, dma=True)

    def dump(name, ap, shape, dt, reads):
        if name not in dbg:
            return
        t = nc.dram_tensor("dbg_" + name, shape, dt, kind="ExternalOutput").ap()
        dbg_out[name] = t
        dma("sp", t, ap, reads, [("dbg", name)])

    def rstd_tile(bank, scale, eps, slot, mode):
        t, ta, tb = tf(slot)
        act(t[:], PS[bank][:], AF.Ln, [("ps", bank), ("LPe",)], [ta, tb], bias=eps_ap(eps), scale=scale)
        act(t[:], t[:], AF.Exp, [ta, tb], [ta, tb], scale=-0.5)
        return t, ta, tb

    eps_cols = {1e-6: 6, 1e-5: 7}

    def eps_ap(eps):
        c = eps_cols[eps]
        return LP[:, c:c + 1]

    dma("sp", CF[:], cf_d, [], [("CF",)])
    dma("sp", CB[:], cb_d, [], [("CB",)])
    dma("sp", RC[:], rope_d[:, 0, :], [], [("ROPE",)])
    dma("sp", RS[:], rope_d[:, 1, :], [], [("ROPE",)])
    P.add("dve", lambda e: e.memset(ONESF[:], 1.0), writes=[("T", 0, 0)])
    P.add("dve", lambda e: e.memset(LP[:, 6:7], 1e-6), writes=[("LPe",)])
    P.add("dve", lambda e: e.memset(LP[:, 7:8], 1e-5), writes=[("LPe",)])
    tt(LP[:, 0:1], CF[:, 313:314], CF[:, 314:315], ALU.mult, [("CF",)], [("LP0",)])
    tt(LP[:, 1:2], CF[:, 315:316], CF[:, 316:317], ALU.mult, [("CF",)], [("LP0",)])
    b0 = nbank()
    P.add("pe", lambda pe: pe.matmul(PS[b0][:, 0:2], lhsT=ONESF[:], rhs=LP[:, 0:2], start=True, stop=True),
          reads=[("T", 0, 0), ("LP0",)], writes=[("ps", b0)])
    act(LP[:, 2:4], PS[b0][:, 0:2], AF.Exp, [("ps", b0)], [("LP1",)])
    tt(LP[:, 4:5], LP[:, 3:4], LP[:, 2:3], ALU.subtract, [("LP1",)], [("LP2",)])
    ts(LP[:, 4:5], LP[:, 4:5], -LAMBDA_INIT, None, ALU.add, None, [("LP2",)], [("LP2",)])
    ts(LP[:, 5:6], CF[:, 312:313], 1.0 - LAMBDA_INIT, None, ALU.mult, None, [("CF",)], [("LP3",)])
    CONST_R = [("CF",), ("CB",), ("ROPE",), ("LP2",), ("LP3",), ("LPe",)]

    wstate = {"n": 0}

    def fill(spec):
        j = wstate["n"] % NW
        wstate["n"] += 1
        for (dstf, src) in spec:
            dma("pool", dstf(j), src, [], [("W", j)])
        return j

    def spec_cols(view, ranges):
        sp = []
        o = 0
        for (c0, n) in ranges:
            sp.append((lambda j, o=o, n=n: WS[j][:, :, o:o + n], view[:, :, c0:c0 + n]))
            o += n
        return sp

    def spec_down(c):
        return [(lambda j: WD[j][:], w_fd_v[:, :, c * 128:(c + 1) * 128])]

    def r1tok(t):
        return [("R1", k, t) for k in range(8)]

    def r2tok(t):
        return [("R2", k, t) for k in range(8)]

    def r3tok(t):
        return [("R3", k, t) for k in range(8)]

    def seq_stages(s):
        stages = []
        tok0 = s * S

        def p1():
            P.barrier()
            for t in range(NT):
                xs = XS[t % 2]
                xtok = ("XS", t % 2)
                dma("sp", xs[:], xT_v[:, :, tok0 + t * TT: tok0 + (t + 1) * TT], [], [xtok])
                b = nbank()
                for k in range(8):
                    sq = TH[0][k % 2]
                    sqt = ("T", 0, k % 2)
                    act(sq[:], xs[:, k, :], AF.Square, [xtok], [sqt])
                    P.add("pe", lambda pe, k=k, sq=sq, b=b: pe.matmul(PS[b][:], lhsT=ones_b, rhs=sq[:], start=(k == 0), stop=(k == 7)),
                          reads=[sqt, ("CB",)], writes=[("ps", b)])
                rt, ra, rb = rstd_tile(b, 1.0 / D, 1e-6, 1, "sqrt")
                for k in range(8):
                    stt(R1[:, k, t * TT:(t + 1) * TT], xs[:, k, :], g_mix(k), rt[:], ALU.mult, ALU.mult,
                        [xtok, ra, rb, ("CF",)], [("R1", k, t)])
            dump("hT", R1[:], [128, 8, S], BF16, [x for t in range(NT) for x in r1tok(t)])
        stages.append((None, p1))

        for j2 in range(4):
            spec = spec_cols(w_in_v, [(j2 * 256, 256), (1024 + j2 * 256, 256)])

            def p2a(slot, j2=j2):
                if j2 == 0:
                    P.barrier()
                    for i in range(2):
                        P.add("dve", lambda e, i=i: e.memset(U16[i][:, 0:30], 0.0), writes=[("U16pad", i)])
                for cc in range(2):
                    c = j2 * 2 + cc
                    ui = c % 2
                    for j in range(NPE_TAPS):
                        P.add("dve", lambda e, j=j, c=c, ui=ui: e.tensor_scalar(out=DG[ui][:, j, :], in0=ident_b, scalar1=w_dw(c, j),
                                                                               scalar2=None, op0=ALU.mult),
                              reads=[("CB",), ("CF",)], writes=[("DG", ui)])
                    for t in range(NT):
                        ba, bg = nbank(), nbank()
                        mm(ba, [(WS[slot][:, k, cc * 128:(cc + 1) * 128], R1[:, k, t * TT:(t + 1) * TT]) for k in range(8)],
                           [("W", slot)] + r1tok(t))
                        mm(bg, [(WS[slot][:, k, 256 + cc * 128:256 + (cc + 1) * 128], R1[:, k, t * TT:(t + 1) * TT]) for k in range(8)],
                           [("W", slot)] + r1tok(t))
                        sg, sa, sb_ = tf(t % 2)
                        act(sg[:], PS[bg][:], AF.Sigmoid, [("ps", bg)], [sa, sb_])
                        tt(U16[ui][:, 30 + t * TT:30 + (t + 1) * TT], PS[ba][:], sg[:], ALU.mult,
                           [("ps", ba), sa, sb_], [("U16", ui, t)])
                    for t in range(NT):
                        bc = nbank()
                        rd = [("DG", ui), ("U16", ui, t), ("U16pad", ui)] + ([("U16", ui, t - 1)] if t > 0 else [])
                        mm(bc, [(DG[ui][:, j, :], U16[ui][:, t * TT + j:t * TT + j + TT]) for j in range(NPE_TAPS)], rd)
                        ca, caa, cab = tf(4 + (t % 2))
                        urd = [x for x in rd if x[0] != "DG"] + [("CF",)]
                        ts(ca[:], U16[ui][:, t * TT + NPE_TAPS:t * TT + NPE_TAPS + TT], w_dw(c, NPE_TAPS), None, ALU.mult, None,
                           urd, [caa, cab])
                        for j in range(NPE_TAPS + 1, CW):
                            stt(ca[:], U16[ui][:, t * TT + j:t * TT + j + TT], w_dw(c, j), ca[:], ALU.mult, ALU.add,
                                urd + [caa, cab], [caa, cab])
                        stt(R2[:, c, t * TT:(t + 1) * TT], PS[bc][:], b_dw(c), ca[:], ALU.add, ALU.add,
                            [("ps", bc), ("CF",), caa, cab], [("R2", c, t)])
            stages.append((spec, p2a))

        def p2ln():
            dump("cb", R2[:], [128, 8, S], BF16, [x for t in range(NT) for x in r2tok(t)])

            def ln_stats(t):
                bs, bq = nbank(), nbank()
                for c in range(8):
                    cbv = R2[:, c, t * TT:(t + 1) * TT]
                    sq = TH[2][c % 2]
                    sqt = ("T", 2, c % 2)
                    act(sq[:], cbv, AF.Square, [("R2", c, t)], [sqt])
                    P.add("pe", lambda pe, c=c, cbv=cbv, bs=bs: pe.matmul(PS[bs][:], lhsT=ones_b, rhs=cbv, start=(c == 0), stop=(c == 7)),
                          reads=[("R2", c, t), ("CB",)], writes=[("ps", bs)])
                    P.add("pe", lambda pe, c=c, sq=sq, bq=bq: pe.matmul(PS[bq][:], lhsT=ones_b, rhs=sq[:], start=(c == 0), stop=(c == 7)),
                          reads=[sqt, ("CB",)], writes=[("ps", bq)])
                m, ma, mb = tf(3 + 2 * (t % 2))
                q2, qa, qb = tf(4 + 2 * (t % 2))
                ts(m[:], PS[bs][:], 1.0 / D, None, ALU.mult, None, [("ps", bs)], [ma, mb])
                tt(q2[:], m[:], m[:], ALU.mult, [ma, mb], [qa, qb])
                stt(q2[:], PS[bq][:], 1.0 / D, q2[:], ALU.mult, ALU.subtract, [("ps", bq), qa, qb], [qa, qb])
                act(q2[:], q2[:], AF.Ln, [qa, qb, ("LPe",)], [qa, qb], bias=eps_ap(1e-5), scale=1.0)
                act(q2[:], q2[:], AF.Exp, [qa, qb], [qa, qb], scale=-0.5)
                negm = TH[7][t % 2]
                ts(negm[:], PS[bs][:], -1.0 / D, None, ALU.mult, None, [("ps", bs)], [("T", 7, t % 2)])
                return (negm, ("T", 7, t % 2), q2, qa, qb)

            def ln_norm(t, st):
                negm, nmt, q2, qa, qb = st
                for c in range(8):
                    cbv = R2[:, c, t * TT:(t + 1) * TT]
                    t1, t1a, t1b = tf(c % 2)
                    bc_ = nbank()
                    P.add("pe", lambda pe, cbv=cbv, negm=negm, bc_=bc_: (
                        pe.matmul(PS[bc_][:], lhsT=ident_b, rhs=cbv, start=True, stop=False),
                        pe.matmul(PS[bc_][:], lhsT=ident_b, rhs=negm[:], start=False, stop=True))[1],
                        reads=[("R2", c, t), nmt, ("CB",)], writes=[("ps", bc_)])
                    tt(t1[:], PS[bc_][:], q2[:], ALU.mult, [("ps", bc_), qa, qb], [t1a, t1b])
                    act(cbv, t1[:], AF.Silu, [t1a, t1b, ("CF",)], [("R2", c, t)], bias=b_ln(c), scale=g_ln(c))

            st = ln_stats(0)
            for t in range(NT):
                nxt = ln_stats(t + 1) if t + 1 < NT else None
                ln_norm(t, st)
                st = nxt
            dump("cs", R2[:], [128, 8, S], BF16, [x for t in range(NT) for x in r2tok(t)])
        stages.append((None, p2ln))

        for j2 in range(4):
            spec = spec_cols(w_pw_v, [(j2 * 256, 256)]) + [
                (lambda j: WS[j][:, :, 256:512], w_in_v[:, :, C3 + j2 * 256: C3 + (j2 + 1) * 256])]

            def p2b(slot, j2=j2):
                for cc in range(2):
                    c = j2 * 2 + cc
                    for t in range(NT):
                        by, bg = nbank(), nbank()
                        mm(by, [(WS[slot][:, k, cc * 128:(cc + 1) * 128], R2[:, k, t * TT:(t + 1) * TT]) for k in range(8)],
                           [("W", slot)] + r2tok(t))
                        mm(bg, [(WS[slot][:, k, 256 + cc * 128:256 + (cc + 1) * 128], R1[:, k, t * TT:(t + 1) * TT]) for k in range(8)],
                           [("W", slot)] + r1tok(t))
                        sg, sa, sb_ = tf((c * NT + t) % 2)
                        act(sg[:], PS[bg][:], AF.Sigmoid, [("ps", bg), ("CF",)], [sa, sb_], bias=b_gate(c))
                        tt(R3[:, c, t * TT:(t + 1) * TT], PS[by][:], sg[:], ALU.mult, [("ps", by), sa, sb_], [("R3", c, t)])
            stages.append((spec, p2b))

        def dump_m1():
            dump("m1", R3[:], [128, 8, S], BF16, [x for t in range(NT) for x in r3tok(t)])
        stages.append((None, dump_m1))

        for hg in range(2):
            for which in range(2):
                spec = spec_cols(w_in_v, [((C0 if which == 0 else C1) + hg * 512, 512)])

                def p3qk(slot, hg=hg, which=which):
                    if hg == 0 and which == 0:
                        P.barrier()
                    dst = QT if which == 0 else KT
                    dn = "QT" if which == 0 else "KT"
                    for hh in range(4):
                        for t in range(NT):
                            bz = nbank()
                            mm(bz, [(WS[slot][:, k, hh * 128:(hh + 1) * 128], R1[:, k, t * TT:(t + 1) * TT]) for k in range(8)],
                               [("W", slot)] + r1tok(t))
                            i2 = (hh * NT + t) % 2
                            qb_ = TH[0][i2]
                            qbt = ("T", 0, i2)
                            act(qb_[:], PS[bz][:], AF.Copy, [("ps", bz)], [qbt])
                            bp = nbank()
                            P.add("pe", lambda pe, bp=bp, qb_=qb_: pe.matmul(PS[bp][:], lhsT=pm_b, rhs=qb_[:], start=True, stop=True),
                                  reads=[qbt, ("CB",)], writes=[("ps", bp)])
                            tS, tSa, tSb = tf(1 + i2)
                            qf, qfa, qfb = tf(3 + i2)
                            tt(tS[:], PS[bp][:], RS[:, t * TT:(t + 1) * TT], ALU.mult, [("ps", bp), ("ROPE",)], [tSa, tSb])
                            tt(qf[:], PS[bz][:], RC[:, t * TT:(t + 1) * TT], ALU.mult, [("ps", bz), ("ROPE",)], [qfa, qfb])
                            tt(dst[:, hh, t * TT:(t + 1) * TT], qf[:], tS[:], ALU.add, [tSa, tSb, qfa, qfb], [(dn, hh, t)])
                stages.append((spec, p3qk))
            spec = spec_cols(w_in_v, [(C2 + hg * 512, 512)])

            def p3v(slot, hg=hg):
                for tq in range(S // 128):
                    bv = nbank()
                    mm(bv, [(R1[:, k, tq * 128:(tq + 1) * 128], WS[slot][:, k, :]) for k in range(8)],
                       [("W", slot)] + r1tok(tq // 4))
                    P.add("act", lambda e, tq=tq, bv=bv: e.activation(out=VV[:, tq, :], in_=PS[bv][:], func=AF.Copy),
                          reads=[("ps", bv)], writes=[("VV", tq)])
                if hg == 0:
                    dump("qT", QT[:], [128, 4, S], BF16, [("QT", hh, t) for hh in range(4) for t in range(NT)])
                    dump("kT", KT[:], [128, 4, S], BF16, [("KT", hh, t) for hh in range(4) for t in range(NT)])
                    dump("vv", VV[:], [128, S // 128, 512], BF16, [("VV", tq) for tq in range(S // 128)])
            stages.append((spec, p3v))

            def p4(hg=hg):
                its = [(hh, i, j) for hh in range(4) for i in range(NT) for j in range(4 * (i + 1))]
                BO1, BO2, BD1, BD2 = 4, 5, 6, 7

                def geom(n):
                    hh, i, j = its[n]
                    jd = j - 4 * i
                    q0 = jd * 128 if jd > 0 else 0
                    return hh, i, j, jd, q0

                def emit_S(n):
                    hh, i, j, jd, q0 = geom(n)
                    b = 2 * (n % 2)
                    qs = slice(i * TT + q0, (i + 1) * TT)

                    def fn_s(pe, j=j, hh=hh, qs=qs, b=b, q0=q0, diag=(jd >= 0)):
                        pe.matmul(PS[b][:, q0:TT], lhsT=KT[0:64, hh, j * 128:(j + 1) * 128], rhs=QT[0:64, hh, qs],
                                  start=True, stop=not diag)
                        ins = pe.matmul(PS[b + 1][:, q0:TT], lhsT=KT[64:128, hh, j * 128:(j + 1) * 128], rhs=QT[64:128, hh, qs],
                                        start=True, stop=not diag)
                        if diag:
                            for a in range(2):
                                pe.matmul(PS[b][:, q0:q0 + 128], lhsT=LM(0, a), rhs=RM(0, a), start=False, stop=(a == 1))
                                ins = pe.matmul(PS[b + 1][:, q0:q0 + 128], lhsT=LM(1, a), rhs=RM(1, a), start=False, stop=(a == 1))
                        return ins
                    P.add("pe", fn_s, reads=[("KT", hh, j // 4), ("QT", hh, i), ("CB",)], writes=[("ps", b), ("ps", b + 1)])

                def emit_exp(n):
                    hh, i, j, jd, q0 = geom(n)
                    pi = n % 3
                    b = 2 * (n % 2)
                    ptok = [("T", pi, 0), ("T", pi, 1)] if pi < 2 else [("PT2x",)]
                    act(PT2[pi][:, :, q0:TT], PSA[:, b:b + 2, q0:TT], AF.Exp, [("ps", b), ("ps", b + 1)], ptok, scale=0.125)

                def emit_PV(n):
                    hh, i, j, jd, q0 = geom(n)
                    pi = n % 3
                    ptok = [("T", pi, 0), ("T", pi, 1)] if pi < 2 else [("PT2x",)]
                    vv = VV[:, j, hh * 128:(hh + 1) * 128]
                    first, lastj = (j == 0), (j == 4 * (i + 1) - 1)

                    def fn_pv(pe, vv=vv, pi=pi, q0=q0, first=first, lastj=lastj):
                        pe.matmul(PS[BO1][:, q0:TT], lhsT=vv, rhs=PT2[pi][:, 0, q0:TT], start=first, stop=lastj)
                        pe.matmul(PS[BD1][:, q0:TT], lhsT=ones_b, rhs=PT2[pi][:, 0, q0:TT], start=first, stop=lastj)
                        pe.matmul(PS[BO2][:, q0:TT], lhsT=vv, rhs=PT2[pi][:, 1, q0:TT], start=first, stop=lastj)
                        return pe.matmul(PS[BD2][:, q0:TT], lhsT=ones_b, rhs=PT2[pi][:, 1, q0:TT], start=first, stop=lastj)
                    P.add("pe", fn_pv, reads=[("VV", j)] + ptok + [("CB",)],
                          writes=[("ps", BO1), ("ps", BO2), ("ps", BD1), ("ps", BD2)])
                    return lastj

                ltok = [("T", 2, 0), ("T", 2, 1), ("T", 3, 0), ("T", 3, 1)]
                ctok = [("T", 4, 0), ("T", 4, 1), ("T", 5, 0), ("T", 5, 1)]
                C12 = sb(f"C12_{hg}_{s}", [128, 2, TT], F32, o_T + 4 * 2048)

                def finalize(n):
                    hh, i, j, jd, q0 = geom(n)
                    head = hg * 4 + hh
                    oo, ooa, oob = tf(6)
                    sq = TH[7][0]
                    act(LT[:], PSA[:, BD1:BD2 + 1, :], AF.Ln, [("ps", BD1), ("ps", BD2)], ltok)
                    P.add("dve", lambda e: e.tensor_copy(out=C12[:], in_=PSA[:, BO1:BO2 + 1, :]),
                          reads=[("ps", BO1), ("ps", BO2)], writes=ctok)
                    yield
                    act(LT[:], LT[:], AF.Exp, ltok, ltok, scale=-1.0)
                    tt(C12[:], C12[:], LT[:], ALU.mult, ctok + ltok, ctok)
                    stt(oo[:], C12[:, 1, :], neglam, C12[:, 0, :], ALU.mult, ALU.add, ctok + [("LP2",)], [ooa, oob])
                    tt(sq[:], oo[:], oo[:], ALU.mult, [ooa, oob], [("T", 7, 0)])
                    yield
                    yield
                    yield
                    bank = 2 * ((cur[0] + 1) % 2)
                    P.add("pe", lambda pe, bank=bank: pe.matmul(PS[bank][:], lhsT=ones_b, rhs=sq[:], start=True, stop=True),
                          reads=[("T", 7, 0), ("CB",)], writes=[("ps", bank)])
                    rs_, rsa, rsb = RSB, ("RSB", 0), ("RSB", 1)
                    act(rs_[:], PS[bank][:], AF.Ln, [("ps", bank), ("LPe",)], [rsa, rsb], bias=eps_ap(1e-6), scale=1.0 / 128)
                    yield
                    act(rs_[:], rs_[:], AF.Exp, [rsa, rsb], [rsa, rsb], scale=-0.5)
                    stt(R2[:, head, i * TT:(i + 1) * TT], oo[:], gsubs, rs_[:], ALU.mult, ALU.mult,
                        [ooa, oob, rsa, rsb, ("LP3",)], [("R2", head, i)])

                N = len(its)
                cur = [0]
                queue = []
                emit_S(0)
                if N > 1:
                    emit_S(1)
                emit_exp(0)
                def emit_dummy(n, k):
                    hh, i, j, jd, q0 = geom(n)
                    if j == 4 * (i + 1) - 1 or k == 0:
                        return
                    def fn_d(pe, hh=hh, k=k):
                        ins = None
                        for _ in range(k):
                            ins = pe.matmul(PS[BO1][:], lhsT=zeros_b, rhs=KT[:, hh, 0:TT], start=False, stop=False)
                        return ins
                    P.add("pe", fn_d, reads=[("KT", hh, 0), ("CB",)], writes=[("ps", BO1)])

                for n in range(N):
                    cur[0] = n
                    if n + 2 < N:
                        emit_S(n + 2)
                    lastj = emit_PV(n)
                    emit_dummy(n, NDUM_FIN if queue else (NDUM_DIAG if geom(n)[3] >= 0 else 0))
                    if n + 1 < N:
                        emit_exp(n + 1)
                    for g in list(queue):
                        try:
                            next(g)
                        except StopIteration:
                            queue.remove(g)
                    if lastj:
                        g = finalize(n)
                        next(g)
                        queue.append(g)
                while queue:
                    cur[0] += 1
                    for g in list(queue):
                        try:
                            next(g)
                        except StopIteration:
                            queue.remove(g)
            stages.append((None, p4))

        def dump_on():
            dump("on", R2[:], [128, 8, S], BF16, [x for t in range(NT) for x in r2tok(t)])
        stages.append((None, dump_on))

        for j2 in range(4):
            spec = spec_cols(w_ap_v, [(j2 * 256, 256)]) + [
                (lambda j: WS[j][:, :, 256:512], w_in_v[:, :, C3 + D + j2 * 256: C3 + D + (j2 + 1) * 256])]

            def p5(slot, j2=j2):
                for cc in range(2):
                    c = j2 * 2 + cc
                    for t in range(NT):
                        by, bg = nbank(), nbank()
                        mm(by, [(WS[slot][:, k, cc * 128:(cc + 1) * 128], R2[:, k, t * TT:(t + 1) * TT]) for k in range(8)],
                           [("W", slot)] + r2tok(t))
                        mm(bg, [(WS[slot][:, k, 256 + cc * 128:256 + (cc + 1) * 128], R1[:, k, t * TT:(t + 1) * TT]) for k in range(8)],
                           [("W", slot)] + r1tok(t))
                        i2 = (c * NT + t) % 2
                        sg, sa, sb_ = tf(i2)
                        tm, tma, tmb = tf(2 + i2)
                        act(sg[:], PS[bg][:], AF.Sigmoid, [("ps", bg), ("CF",)], [sa, sb_], bias=b_gate(8 + c))
                        tt(tm[:], PS[by][:], sg[:], ALU.mult, [("ps", by), sa, sb_], [tma, tmb])
                        tt(R3[:, c, t * TT:(t + 1) * TT], tm[:], R3[:, c, t * TT:(t + 1) * TT], ALU.add,
                           [tma, tmb, ("R3", c, t)], [("R3", c, t)])
            stages.append((spec, p5))

        def dump_mg():
            dump("mg", R3[:], [128, 8, S], BF16, [x for t in range(NT) for x in r3tok(t)])
        stages.append((None, dump_mg))

        for j2 in range(2):
            spec = spec_cols(w_out_v, [(j2 * 512, 512)])

            def p6(slot, j2=j2):
                if j2 == 0:
                    P.barrier()
                for cc in range(4):
                    c = j2 * 4 + cc
                    for t in range(NT):
                        bz = nbank()
                        mm(bz, [(WS[slot][:, k, cc * 128:(cc + 1) * 128], R3[:, k, t * TT:(t + 1) * TT]) for k in range(8)],
                           [("W", slot)] + r3tok(t))
                        i2 = (c * NT + t) % 2
                        xs, xa, xb = tf(i2)
                        dma("sp", xs[:], xT[c * 128:(c + 1) * 128, tok0 + t * TT: tok0 + (t + 1) * TT], [], [xa, xb])
                        tt(X1[:, c, t * TT:(t + 1) * TT], PS[bz][:], xs[:], ALU.add, [("ps", bz), xa, xb], [("X1", c, t)])
            stages.append((spec, p6))

        def p6n():
            P.barrier()
            dump("x1", X1[:], [128, 8, S], F32, [("X1", c, t) for c in range(8) for t in range(NT)])
            for t in range(NT):
                b = nbank()
                for k in range(8):
                    sq, sqt = TH[2][k % 2], ("T", 2, k % 2)
                    act(sq[:], X1[:, k, t * TT:(t + 1) * TT], AF.Square, [("X1", k, t)], [sqt])
                    P.add("pe", lambda pe, k=k, sq=sq, b=b: pe.matmul(PS[b][:], lhsT=ones_b, rhs=sq[:], start=(k == 0), stop=(k == 7)),
                          reads=[sqt, ("CB",)], writes=[("ps", b)])
                rt, ra, rb = rstd_tile(b, 1.0 / D, 1e-6, 3, "sqrt")
                for k in range(8):
                    stt(R1[:, k, t * TT:(t + 1) * TT], X1[:, k, t * TT:(t + 1) * TT], g_ffn(k), rt[:], ALU.mult, ALU.mult,
                        [("X1", k, t), ra, rb, ("CF",)], [("R1", k, t)])
        stages.append((None, p6n))

        for g in range(NT // TG):
            tiles = [g * TG + x for x in range(TG)]
            for fs in range(11):
                spec = spec_cols(w_fg_v, [(fs * 256, 256)]) + [
                    (lambda j: WS[j][:, :, 256:512], w_fu_v[:, :, fs * 256:(fs + 1) * 256])]

                def p7a(slot, fs=fs, tiles=tiles):
                    for fc in range(2):
                        f = fs * 2 + fc
                        for ti, t in enumerate(tiles):
                            bg, bu = nbank(), nbank()
                            mm(bg, [(WS[slot][:, k, fc * 128:(fc + 1) * 128], R1[:, k, t * TT:(t + 1) * TT]) for k in range(8)],
                               [("W", slot)] + r1tok(t))
                            mm(bu, [(WS[slot][:, k, 256 + fc * 128:256 + (fc + 1) * 128], R1[:, k, t * TT:(t + 1) * TT]) for k in range(8)],
                               [("W", slot)] + r1tok(t))
                            sl, sla, slb = tf((f * TG + ti) % 2)
                            act(sl[:], PS[bg][:], AF.Silu, [("ps", bg)], [sla, slb])
                            tt(FF[:, f, ti * TT:(ti + 1) * TT], PS[bu][:], sl[:], ALU.mult, [("ps", bu), sla, slb], [("FF", f, ti)])
                stages.append((spec, p7a))
            for c in range(8):
                spec = spec_down(c)

                def p7b(slot, c=c, tiles=tiles):
                    for ti, t in enumerate(tiles):
                        bz = nbank()
                        mm(bz, [(WD[slot][:, f, :], FF[:, f, ti * TT:(ti + 1) * TT]) for f in range(FC)],
                           [("W", slot)] + [("FF", f, ti) for f in range(FC)])
                        tt(X1[:, c, t * TT:(t + 1) * TT], PS[bz][:], X1[:, c, t * TT:(t + 1) * TT], ALU.add,
                           [("ps", bz), ("X1", c, t)], [("X1", c, t)])
                stages.append((spec, p7b))

            def p7n(tiles=tiles):
                for t in tiles:
                    b = nbank()
                    for k in range(8):
                        sq, sqt = TH[2][k % 2], ("T", 2, k % 2)
                        act(sq[:], X1[:, k, t * TT:(t + 1) * TT], AF.Square, [("X1", k, t)], [sqt])
                        P.add("pe", lambda pe, k=k, sq=sq, b=b: pe.matmul(PS[b][:], lhsT=ones_b, rhs=sq[:], start=(k == 0), stop=(k == 7)),
                              reads=[sqt, ("CB",)], writes=[("ps", b)])
                    rt, ra, rb = rstd_tile(b, 1.0 / D, 1e-6, 3, "sqrt")
                    for k in range(8):
                        stt(X1[:, k, t * TT:(t + 1) * TT], X1[:, k, t * TT:(t + 1) * TT], g_fin(k), rt[:], ALU.mult, ALU.mult,
                            [("X1", k, t), ra, rb, ("CF",)], [("X1", k, t)])
                    dma("sp", outT_v[:, :, tok0 + t * TT: tok0 + (t + 1) * TT], X1[:, :, t * TT:(t + 1) * TT],
                        [("X1", k, t) for k in range(8)], [("OUT", s, t)])
            stages.append((None, p7n))
        return stages

    all_stages = []
    for s in range(NSEQ):
        all_stages.extend(seq_stages(s))
    widx = [i for i, (sp_, _) in enumerate(all_stages) if sp_ is not None]
    slots = {}
    issued = 0
    for i, (sp_, body) in enumerate(all_stages):
        r = sum(1 for x in widx if x < i)
        while issued < len(widx) and issued <= r + 1:
            wi = widx[issued]
            slots[wi] = fill(all_stages[wi][0])
            issued += 1
        if sp_ is None:
            body()
        else:
            body(slots[i])

    P.analyze()
    P.emit()
    return nc, dbg_out, P


def _host_consts(inp, S):
    cf = np.zeros((128, NCF), np.float32)

    def pk(v):
        return np.ascontiguousarray(np.asarray(v, np.float32).reshape(-1, 128).T)
    cf[:, 0:8] = pk(inp["g_mix"][0])
    cf[:, 8:24] = pk(inp["b_gate"][0])
    wdw = np.asarray(inp["w_dw"][0], np.float32)
    cf[:, 24:272] = wdw.T.reshape(8, 128, CW).transpose(1, 0, 2).reshape(128, 8 * CW)
    cf[:, 272:280] = pk(inp["b_dw"][0])
    cf[:, 280:288] = pk(inp["g_conv_ln"][0])
    cf[:, 288:296] = pk(inp["b_conv_ln"][0])
    cf[:, 296:304] = pk(inp["g_ffn"][0])
    cf[:, 304:312] = pk(inp["g_final"])
    cf[:, 312] = np.asarray(inp["g_subln"][0], np.float32)
    for i, nm in enumerate(("lambda_q1", "lambda_k1", "lambda_q2", "lambda_k2")):
        cf[0:64, 313 + i] = np.asarray(inp[nm][0], np.float32)
    cb = np.zeros((128, 1024), np.float32)
    cb[:, 0:128] = 1.0
    for o in (0, 64):
        for i in range(8):
            cb[o + i + 8, 128 + o + i] = -1.0
            cb[o + i, 128 + o + 8 + i] = 1.0
    kk = np.arange(128)
    cb[:, 256:384] = np.eye(128, dtype=np.float32)
    maskneg = np.where(kk[:, None] > kk[None, :], -30000.0, 0.0).astype(np.float32)
    for h in range(2):
        for p in range(64):
            cb[64 * h + p, 512 + p] = 1.0
            cb[64 * h + p, 512 + 128 + p + 64] = 1.0
        cb[64 * h:64 * h + 64, 768:896] = maskneg[0:64]
        cb[64 * h:64 * h + 64, 896:1024] = maskneg[64:128]
    cb = cb.astype(ml_dtypes.bfloat16)
    inv_freq = (np.float32(500000.0) ** (-np.arange(0, 16, 2, dtype=np.float32) / np.float32(16))).astype(np.float32)
    pos = np.arange(S, dtype=np.float32)
    ang = (pos[:, None] * inv_freq[None, :]).astype(np.float32)
    cs, sn = np.cos(ang).astype(np.float32).T, np.sin(ang).astype(np.float32).T
    rope = np.zeros((128, 2, S), np.float32)
    rope[:, 0, :] = 1.0
    for o in (0, 64):
        rope[o:o + 8, 0] = cs
        rope[o + 8:o + 16, 0] = cs
        rope[o:o + 8, 1] = sn
        rope[o + 8:o + 16, 1] = sn
    return cf, cb, rope


def make_in_maps(inp, S, NSEQ, ncores):
    x = np.asarray(inp["x"], np.float32)
    cf, cb, rope = _host_consts(inp, S)
    shared = {
        "w_in": np.ascontiguousarray(inp["w_in"][0], dtype=np.float32),
        "w_conv_pw": np.ascontiguousarray(inp["w_conv_pw"][0], dtype=np.float32),
        "w_attn_pw": np.ascontiguousarray(inp["w_attn_pw"][0], dtype=np.float32),
        "w_out": np.ascontiguousarray(inp["w_out"][0], dtype=np.float32),
        "w_ffn_gate": np.ascontiguousarray(inp["w_ffn_gate"][0], dtype=np.float32),
        "w_ffn_up": np.ascontiguousarray(inp["w_ffn_up"][0], dtype=np.float32),
        "w_ffn_down": np.ascontiguousarray(inp["w_ffn_down"][0], dtype=np.float32),
        "cf32": cf, "cb16": cb, "rope": rope,
    }
    maps = []
    for c in range(ncores):
        xs = x[c * NSEQ:(c + 1) * NSEQ, :S]
        xT = np.ascontiguousarray(xs.reshape(NSEQ * S, D).T)
        m = dict(shared)
        m["xT"] = xT
        maps.append(m)
    return maps


def kernel(**inputs):
    S, NSEQ, NCORES = 2048, 2, 8
    nc, _, _ = build_nc(S, NSEQ)
    in_maps = make_in_maps(inputs, S, NSEQ, NCORES)
    res = run_bass_kernel_spmd(nc, in_maps, core_ids=list(range(NCORES)))
    outs = []
    for c in range(NCORES):
        oT = np.asarray(res.results[c]["outT"], np.float32)
        outs.append(oT.T.reshape(NSEQ, S, D))
    return np.ascontiguousarray(np.concatenate(outs, axis=0), dtype=np.float32)
```

```python
import math
import numpy as np
import ml_dtypes
import concourse.bass as bass
import concourse.mybir as mybir
from concourse.bass_utils import run_bass_kernel_spmd

F32 = mybir.dt.float32
BF16 = mybir.dt.bfloat16
AF = mybir.ActivationFunctionType
ALU = mybir.AluOpType

D = 1024
KC = 8
NH = 8
DFF = 2816
FC = 22
CW = 31
IN_COLS = 7168
C0, C1, C2, C3 = 2048, 3072, 4096, 5120
LAMBDA_INIT = 0.8 - 0.6 * math.exp(-0.3 * 0)
NCF = 320
TT = 512
NDUM_FIN = 0
NPE_TAPS = 25
NDUM_DIAG = 0
DBG_TRI = False

ENGS = ("pe", "act", "dve", "pool", "sp")
SEM_LIMIT = 30000
NDMA_SEM = 8


class Op:
    __slots__ = ("eng", "fn", "reads", "writes", "dma", "deps", "ord", "gen", "slot", "target", "need_inc", "bar")

    def __init__(self, eng, fn, reads, writes, dma):
        self.eng = eng
        self.fn = fn
        self.reads = reads
        self.writes = writes
        self.dma = dma
        self.deps = ()
        self.ord = 0
        self.gen = 0
        self.slot = 0
        self.target = 0
        self.need_inc = False
        self.bar = False


class Prog:
    def __init__(self, nc):
        self.nc = nc
        self.ops = []

    def add(self, eng, fn, reads=(), writes=(), dma=False):
        self.ops.append(Op(eng, fn, tuple(reads), tuple(writes), dma))

    def barrier(self):
        o = Op("bar", None, (), (), False)
        o.bar = True
        self.ops.append(o)

    def analyze(self):
        ops = self.ops
        tok_w, tok_r = {}, {}
        last_c = {}
        last_d = {}
        bar_deps = {}
        dma_cnt = {"sp": 0, "pool": 0}
        for i, op in enumerate(ops):
            if op.bar:
                deps = set(last_c.values())
                for (q, s), j in last_d.items():
                    if q != "pool":
                        deps.add(j)
                for e in ENGS:
                    if e != "pool":
                        bar_deps.setdefault(e, set()).update(deps)
                continue
            deps = set()
            for t in op.reads:
                deps.update(tok_w.get(t, {}).values())
            for t in op.writes:
                deps.update(tok_w.get(t, {}).values())
                deps.update(tok_r.get(t, {}).values())
            if op.eng in bar_deps:
                deps.update(bar_deps.pop(op.eng))
            if op.dma:
                n = dma_cnt[op.eng]
                dma_cnt[op.eng] = n + 1
                op.slot = n % NDMA_SEM
                prev = last_d.get((op.eng, op.slot))
                if prev is not None:
                    deps.add(prev)
                    op.target = ops[prev].target + 16
                else:
                    op.target = 16
                last_d[(op.eng, op.slot)] = i
                key = ("d", i)
            else:
                last_c[op.eng] = i
                key = op.eng
            for t in op.reads:
                tok_r.setdefault(t, {})[key] = i
            for t in op.writes:
                if tok_r.get(t):
                    tok_w[t] = {key: i}
                    tok_r[t] = {}
                else:
                    tok_w.setdefault(t, {})[key] = i
            deps.discard(i)
            best = {}
            out = []
            for d in deps:
                dop = ops[d]
                if dop.dma:
                    out.append(d)
                else:
                    if dop.eng == "pe" and op.eng == "pe" and not op.dma:
                        continue
                    if d > best.get(dop.eng, -1):
                        best[dop.eng] = d
            out.extend(best.values())
            op.deps = tuple(sorted(out))
            for d in op.deps:
                ops[d].need_inc = True
        cnt = {e: 0 for e in ENGS}
        gen = {e: 0 for e in ENGS}
        for op in ops:
            if op.bar or op.dma or not op.need_inc:
                continue
            if cnt[op.eng] >= SEM_LIMIT:
                cnt[op.eng] = 0
                gen[op.eng] += 1
            cnt[op.eng] += 1
            op.ord = cnt[op.eng]
            op.gen = gen[op.eng]
        self.ngen = {e: gen[e] + 1 for e in ENGS}

    def emit(self):
        nc = self.nc
        engobj = {"pe": nc.tensor, "act": nc.scalar, "dve": nc.vector, "pool": nc.gpsimd, "sp": nc.sync}
        csem = {e: [nc.alloc_semaphore(f"c_{e}_{g}") for g in range(self.ngen[e])] for e in ENGS}
        dsem = {q: [nc.alloc_semaphore(f"d_{q}_{s}") for s in range(NDMA_SEM)] for q in ("sp", "pool")}
        waited = {e: {} for e in ENGS}
        ops = self.ops
        final_d = {}
        nwait = 0
        for op in ops:
            if op.bar:
                continue
            E = engobj[op.eng]
            w = waited[op.eng]
            for d in op.deps:
                dop = ops[d]
                if dop.dma:
                    key = ("d", dop.eng, dop.slot)
                    val = dop.target
                    sem = dsem[dop.eng][dop.slot]
                else:
                    key = ("c", dop.eng, dop.gen)
                    val = dop.ord
                    sem = csem[dop.eng][dop.gen]
                if w.get(key, 0) >= val:
                    continue
                E.wait_ge(sem, val)
                nwait += 1
                w[key] = val
            ins = op.fn(E)
            if op.dma:
                ins.then_inc(dsem[op.eng][op.slot], 16)
                final_d[(op.eng, op.slot)] = op.target
            elif op.need_inc:
                ins.then_inc(csem[op.eng][op.gen], 1)
        for (q, s), tgt in sorted(final_d.items()):
            nc.sync.wait_ge(dsem[q][s], tgt)
        self.nwait = nwait


def build_nc(S=2048, NSEQ=2, dbg=()):
    NT = S // TT
    TG = min(2, NT)
    NTOK = S * NSEQ
    nc = bass.Bass("TRN2", target_bir_lowering=False)
    P = Prog(nc)

    def din(name, shape, dt=F32):
        return nc.dram_tensor(name, shape, dt, kind="ExternalInput").ap()

    xT = din("xT", [D, NTOK])
    w_in = din("w_in", [D, IN_COLS])
    w_pw = din("w_conv_pw", [D, D])
    w_ap = din("w_attn_pw", [D, D])
    w_out = din("w_out", [D, D])
    w_fg = din("w_ffn_gate", [D, DFF])
    w_fu = din("w_ffn_up", [D, DFF])
    w_fd = din("w_ffn_down", [DFF, D])
    cf_d = din("cf32", [128, NCF])
    cb_d = din("cb16", [128, 1024], BF16)
    rope_d = din("rope", [128, 2, S])
    outT = nc.dram_tensor("outT", [D, NTOK], F32, kind="ExternalOutput").ap()
    dbg_out = {}

    w_in_v = w_in.rearrange("(k p) c -> p k c", p=128)
    w_pw_v = w_pw.rearrange("(k p) c -> p k c", p=128)
    w_ap_v = w_ap.rearrange("(k p) c -> p k c", p=128)
    w_out_v = w_out.rearrange("(k p) c -> p k c", p=128)
    w_fg_v = w_fg.rearrange("(k p) c -> p k c", p=128)
    w_fu_v = w_fu.rearrange("(k p) c -> p k c", p=128)
    w_fd_v = w_fd.rearrange("(f p) c -> p f c", p=128)
    xT_v = xT.rearrange("(k p) t -> p k t", p=128)
    outT_v = outT.rearrange("(k p) t -> p k t", p=128)

    def sb(name, shape, dt, off):
        return nc.alloc_sbuf_tensor_at(name, shape, dt, offset=off)

    BASE = 24576 - 256
    SB2 = S * 2 * 8
    o_R1 = BASE
    o_R2 = o_R1 + SB2
    o_A = o_R2 + SB2
    A_SZ = 49152
    o_R3 = o_A + A_SZ
    o_W = o_R3 + SB2
    NW = 3
    o_ROPE = o_W + NW * 8192
    o_T = o_ROPE + 2 * S * 4
    assert o_T + 16384 <= 229376 - 256, o_T
    R1 = sb("R1", [128, 8, S], BF16, o_R1)
    R2 = sb("R2", [128, 8, S], BF16, o_R2)
    R3 = sb("R3", [128, 8, S], BF16, o_R3)
    X1 = sb("X1", [128, 8, S], F32, o_R2)
    assert 8 * S * 4 <= SB2 + 32768
    FF = sb("FF", [128, FC, TG * TT], BF16, o_A + 32768)
    assert FC * TG * TT * 2 <= 16384 + SB2 or S < 2048
    QT = sb("QT", [128, 4, S], BF16, o_A)
    KT = sb("KT", [128, 4, S], BF16, o_A + 16384)
    VV = sb("VV", [128, S // 128, 512], BF16, o_A + 32768)
    XS = [sb(f"XS{i}", [128, 8, TT], F32, o_A + i * 16384) for i in range(2)]
    UL = 30 + S
    ub = ((UL * 2 + 31) // 32) * 32
    U16 = [sb(f"U16_{i}", [128, UL], BF16, o_A + i * ub) for i in range(2)]
    DG = [sb(f"DG{i}", [128, CW, 128], BF16, o_A + 2 * ub + i * CW * 256) for i in range(2)]
    assert 2 * ub + 2 * CW * 256 <= A_SZ
    WS = [sb(f"WS{j}", [128, 8, 512], BF16, o_W + j * 8192) for j in range(NW)]
    WD = [sb(f"WD{j}", [128, FC, 128], BF16, o_W + j * 8192) for j in range(NW)]
    RC = sb("RC", [128, S], F32, o_ROPE)
    RS = sb("RS", [128, S], F32, o_ROPE + S * 4)
    TF = [sb(f"TF{i}", [128, TT], F32, o_T + i * 2048) for i in range(8)]
    TH = [[sb(f"TH{i}_{h}", [128, TT], BF16, o_T + i * 2048 + h * 1024) for h in range(2)] for i in range(8)]

    PT2 = [sb(f"PT2_{i}", [128, 2, TT], BF16, o_T + i * 2048) for i in range(2)]
    LT = sb("LT", [128, 2, TT], F32, o_T + 2 * 2048)

    def tf(i):
        return TF[i], ("T", i, 0), ("T", i, 1)

    AUX = 16768
    CF = sb("CF", [128, NCF], F32, AUX)
    CB = sb("CB", [128, 1024], BF16, AUX + 1280)
    LP = sb("LP", [128, 8], F32, AUX + 1280 + 2048)
    RSB = sb("RSB", [128, TT], F32, AUX + 1280 + 2048 + 64)
    PT2.append(sb("PT2_2", [128, 2, TT], BF16, AUX + 1280 + 2048 + 64 + 2048))
    assert AUX + 1280 + 2048 + 64 + 4096 <= BASE
    ONESF = sb("ONESF", [128, 128], F32, o_T)
    ones_b = CB[:, 0:128]
    pm_b = CB[:, 128:256]
    ident_b = CB[:, 256:384]
    zeros_b = CB[:, 384:512]
    LM = lambda h, a: CB[64 * h:64 * h + 64, 512 + a * 128:512 + (a + 1) * 128]
    RM = lambda h, a: CB[64 * h:64 * h + 64, 768 + a * 128:768 + (a + 1) * 128]
    g_mix = lambda k: CF[:, k:k + 1]
    b_gate = lambda c: CF[:, 8 + c:9 + c]
    w_dw = lambda c, j: CF[:, 24 + c * CW + j:25 + c * CW + j]
    b_dw = lambda c: CF[:, 272 + c:273 + c]
    g_ln = lambda c: CF[:, 280 + c:281 + c]
    b_ln = lambda c: CF[:, 288 + c:289 + c]
    g_ffn = lambda k: CF[:, 296 + k:297 + k]
    g_fin = lambda k: CF[:, 304 + k:305 + k]
    neglam = LP[:, 4:5]
    gsubs = LP[:, 5:6]

    PSA = nc.alloc_psum_tensor("psa", [128, 8, TT], F32)

    class _B:
        def __init__(self, b):
            self.b = b

        def __getitem__(self, idx):
            if not isinstance(idx, tuple):
                idx = (idx,)
            return PSA[(idx[0], self.b) + tuple(idx[1:])]
    PS = [_B(i) for i in range(8)]
    bank_ctr = [0]

    def nbank():
        b = bank_ctr[0] % 8
        bank_ctr[0] += 1
        return b

    def mm(bank, pairs, reads, ncols=None, c0=0):
        out_ap = PS[bank][:, c0:(c0 + ncols)] if ncols is not None else PS[bank][:]

        def fn(pe, pairs=pairs, out_ap=out_ap):
            n = len(pairs)
            ins = None
            for i, (l, r) in enumerate(pairs):
                ins = pe.matmul(out_ap, lhsT=l, rhs=r, start=(i == 0), stop=(i == n - 1))
            return ins
        P.add("pe", fn, reads=reads, writes=[("ps", bank)])

    def act(out, in_, func, reads, writes, bias=None, scale=None):
        kw = {}
        if bias is not None:
            kw["bias"] = bias
        if scale is not None:
            kw["scale"] = scale
        P.add("act", lambda e, kw=kw: e.activation(out=out, in_=in_, func=func, **kw), reads=reads, writes=writes)

    def tt(out, in0, in1, op, reads, writes, eng="dve"):
        P.add(eng, lambda e: e.tensor_tensor(out=out, in0=in0, in1=in1, op=op), reads=reads, writes=writes)

    def stt(out, in0, scalar, in1, op0, op1, reads, writes):
        P.add("dve", lambda e: e.scalar_tensor_tensor(out=out, in0=in0, scalar=scalar, in1=in1, op0=op0, op1=op1),
              reads=reads, writes=writes)

    def ts(out, in0, s1, s2, op0, op1, reads, writes):
        if s2 is None:
            P.add("dve", lambda e: e.tensor_scalar(out=out, in0=in0, scalar1=s1, scalar2=None, op0=op0),
                  reads=reads, writes=writes)
        else:
            P.add("dve", lambda e: e.tensor_scalar(out=out, in0=in0, scalar1=s1, scalar2=s2, op0=op0, op1=op1),
                  reads=reads, writes=writes)

    def recip(out, in_, reads, writes):
        P.add("dve", lambda e: e.reciprocal(out=out, in_=in_), reads=reads, writes=writes)

    def dma(q, out, in_, reads, writes):
        P.add(q, lambda e: e.dma_start(out=out, in_=in_), reads=reads, writes=writes, dma=True)

    def dump(name, ap, shape, dt, reads):
        if name not in dbg:
            return
        t = nc.dram_tensor("dbg_" + name, shape, dt, kind="ExternalOutput").ap()
        dbg_out[name] = t
        dma("sp", t, ap, reads, [("dbg", name)])

    def rstd_tile(bank, scale, eps, slot, mode):
        t, ta, tb = tf(slot)
        act(t[:], PS[bank][:], AF.Ln, [("ps", bank), ("LPe",)], [ta, tb], bias=eps_ap(eps), scale=scale)
        act(t[:], t[:], AF.Exp, [ta, tb], [ta, tb], scale=-0.5)
        return t, ta, tb

    eps_cols = {1e-6: 6, 1e-5: 7}

    def eps_ap(eps):
        c = eps_cols[eps]
        return LP[:, c:c + 1]

    dma("sp", CF[:], cf_d, [], [("CF",)])
    dma("sp", CB[:], cb_d, [], [("CB",)])
    dma("sp", RC[:], rope_d[:, 0, :], [], [("ROPE",)])
    dma("sp", RS[:], rope_d[:, 1, :], [], [("ROPE",)])
    P.add("dve", lambda e: e.memset(ONESF[:], 1.0), writes=[("T", 0, 0)])
    P.add("dve", lambda e: e.memset(LP[:, 6:7], 1e-6), writes=[("LPe",)])
    P.add("dve", lambda e: e.memset(LP[:, 7:8], 1e-5), writes=[("LPe",)])
    tt(LP[:, 0:1], CF[:, 313:314], CF[:, 314:315], ALU.mult, [("CF",)], [("LP0",)])
    tt(LP[:, 1:2], CF[:, 315:316], CF[:, 316:317], ALU.mult, [("CF",)], [("LP0",)])
    b0 = nbank()
    P.add("pe", lambda pe: pe.matmul(PS[b0][:, 0:2], lhsT=ONESF[:], rhs=LP[:, 0:2], start=True, stop=True),
          reads=[("T", 0, 0), ("LP0",)], writes=[("ps", b0)])
    act(LP[:, 2:4], PS[b0][:, 0:2], AF.Exp, [("ps", b0)], [("LP1",)])
    tt(LP[:, 4:5], LP[:, 3:4], LP[:, 2:3], ALU.subtract, [("LP1",)], [("LP2",)])
    ts(LP[:, 4:5], LP[:, 4:5], -LAMBDA_INIT, None, ALU.add, None, [("LP2",)], [("LP2",)])
    ts(LP[:, 5:6], CF[:, 312:313], 1.0 - LAMBDA_INIT, None, ALU.mult, None, [("CF",)], [("LP3",)])
    CONST_R = [("CF",), ("CB",), ("ROPE",), ("LP2",), ("LP3",), ("LPe",)]

    wstate = {"n": 0}

    def fill(spec):
        j = wstate["n"] % NW
        wstate["n"] += 1
        for (dstf, src) in spec:
            dma("pool", dstf(j), src, [], [("W", j)])
        return j

    def spec_cols(view, ranges):
        sp = []
        o = 0
        for (c0, n) in ranges:
            sp.append((lambda j, o=o, n=n: WS[j][:, :, o:o + n], view[:, :, c0:c0 + n]))
            o += n
        return sp

    def spec_down(c):
        return [(lambda j: WD[j][:], w_fd_v[:, :, c * 128:(c + 1) * 128])]

    def r1tok(t):
        return [("R1", k, t) for k in range(8)]

    def r2tok(t):
        return [("R2", k, t) for k in range(8)]

    def r3tok(t):
        return [("R3", k, t) for k in range(8)]

    def seq_stages(s):
        stages = []
        tok0 = s * S

        def p1():
            P.barrier()
            for t in range(NT):
                xs = XS[t % 2]
                xtok = ("XS", t % 2)
                dma("sp", xs[:], xT_v[:, :, tok0 + t * TT: tok0 + (t + 1) * TT], [], [xtok])
                b = nbank()
                for k in range(8):
                    sq = TH[0][k % 2]
                    sqt = ("T", 0, k % 2)
                    act(sq[:], xs[:, k, :], AF.Square, [xtok], [sqt])
                    P.add("pe", lambda pe, k=k, sq=sq, b=b: pe.matmul(PS[b][:], lhsT=ones_b, rhs=sq[:], start=(k == 0), stop=(k == 7)),
                          reads=[sqt, ("CB",)], writes=[("ps", b)])
                rt, ra, rb = rstd_tile(b, 1.0 / D, 1e-6, 1, "sqrt")
                for k in range(8):
                    stt(R1[:, k, t * TT:(t + 1) * TT], xs[:, k, :], g_mix(k), rt[:], ALU.mult, ALU.mult,
                        [xtok, ra, rb, ("CF",)], [("R1", k, t)])
            dump("hT", R1[:], [128, 8, S], BF16, [x for t in range(NT) for x in r1tok(t)])
        stages.append((None, p1))

        for j2 in range(4):
            spec = spec_cols(w_in_v, [(j2 * 256, 256), (1024 + j2 * 256, 256)])

            def p2a(slot, j2=j2):
                if j2 == 0:
                    P.barrier()
                    for i in range(2):
                        P.add("dve", lambda e, i=i: e.memset(U16[i][:, 0:30], 0.0), writes=[("U16pad", i)])
                for cc in range(2):
                    c = j2 * 2 + cc
                    ui = c % 2
                    for j in range(NPE_TAPS):
                        P.add("dve", lambda e, j=j, c=c, ui=ui: e.tensor_scalar(out=DG[ui][:, j, :], in0=ident_b, scalar1=w_dw(c, j),
                                                                               scalar2=None, op0=ALU.mult),
                              reads=[("CB",), ("CF",)], writes=[("DG", ui)])
                    for t in range(NT):
                        ba, bg = nbank(), nbank()
                        mm(ba, [(WS[slot][:, k, cc * 128:(cc + 1) * 128], R1[:, k, t * TT:(t + 1) * TT]) for k in range(8)],
                           [("W", slot)] + r1tok(t))
                        mm(bg, [(WS[slot][:, k, 256 + cc * 128:256 + (cc + 1) * 128], R1[:, k, t * TT:(t + 1) * TT]) for k in range(8)],
                           [("W", slot)] + r1tok(t))
                        sg, sa, sb_ = tf(t % 2)
                        act(sg[:], PS[bg][:], AF.Sigmoid, [("ps", bg)], [sa, sb_])
                        tt(U16[ui][:, 30 + t * TT:30 + (t + 1) * TT], PS[ba][:], sg[:], ALU.mult,
                           [("ps", ba), sa, sb_], [("U16", ui, t)])
                    for t in range(NT):
                        bc = nbank()
                        rd = [("DG", ui), ("U16", ui, t), ("U16pad", ui)] + ([("U16", ui, t - 1)] if t > 0 else [])
                        mm(bc, [(DG[ui][:, j, :], U16[ui][:, t * TT + j:t * TT + j + TT]) for j in range(NPE_TAPS)], rd)
                        ca, caa, cab = tf(4 + (t % 2))
                        urd = [x for x in rd if x[0] != "DG"] + [("CF",)]
                        ts(ca[:], U16[ui][:, t * TT + NPE_TAPS:t * TT + NPE_TAPS + TT], w_dw(c, NPE_TAPS), None, ALU.mult, None,
                           urd, [caa, cab])
                        for j in range(NPE_TAPS + 1, CW):
                            stt(ca[:], U16[ui][:, t * TT + j:t * TT + j + TT], w_dw(c, j), ca[:], ALU.mult, ALU.add,
                                urd + [caa, cab], [caa, cab])
                        stt(R2[:, c, t * TT:(t + 1) * TT], PS[bc][:], b_dw(c), ca[:], ALU.add, ALU.add,
                            [("ps", bc), ("CF",), caa, cab], [("R2", c, t)])
            stages.append((spec, p2a))

        def p2ln():
            dump("cb", R2[:], [128, 8, S], BF16, [x for t in range(NT) for x in r2tok(t)])

            def ln_stats(t):
                bs, bq = nbank(), nbank()
                for c in range(8):
                    cbv = R2[:, c, t * TT:(t + 1) * TT]
                    sq = TH[2][c % 2]
                    sqt = ("T", 2, c % 2)
                    act(sq[:], cbv, AF.Square, [("R2", c, t)], [sqt])
                    P.add("pe", lambda pe, c=c, cbv=cbv, bs=bs: pe.matmul(PS[bs][:], lhsT=ones_b, rhs=cbv, start=(c == 0), stop=(c == 7)),
                          reads=[("R2", c, t), ("CB",)], writes=[("ps", bs)])
                    P.add("pe", lambda pe, c=c, sq=sq, bq=bq: pe.matmul(PS[bq][:], lhsT=ones_b, rhs=sq[:], start=(c == 0), stop=(c == 7)),
                          reads=[sqt, ("CB",)], writes=[("ps", bq)])
                m, ma, mb = tf(3 + 2 * (t % 2))
                q2, qa, qb = tf(4 + 2 * (t % 2))
                ts(m[:], PS[bs][:], 1.0 / D, None, ALU.mult, None, [("ps", bs)], [ma, mb])
                tt(q2[:], m[:], m[:], ALU.mult, [ma, mb], [qa, qb])
                stt(q2[:], PS[bq][:], 1.0 / D, q2[:], ALU.mult, ALU.subtract, [("ps", bq), qa, qb], [qa, qb])
                act(q2[:], q2[:], AF.Ln, [qa, qb, ("LPe",)], [qa, qb], bias=eps_ap(1e-5), scale=1.0)
                act(q2[:], q2[:], AF.Exp, [qa, qb], [qa, qb], scale=-0.5)
                negm = TH[7][t % 2]
                ts(negm[:], PS[bs][:], -1.0 / D, None, ALU.mult, None, [("ps", bs)], [("T", 7, t % 2)])
                return (negm, ("T", 7, t % 2), q2, qa, qb)

            def ln_norm(t, st):
                negm, nmt, q2, qa, qb = st
                for c in range(8):
                    cbv = R2[:, c, t * TT:(t + 1) * TT]
                    t1, t1a, t1b = tf(c % 2)
                    bc_ = nbank()
                    P.add("pe", lambda pe, cbv=cbv, negm=negm, bc_=bc_: (
                        pe.matmul(PS[bc_][:], lhsT=ident_b, rhs=cbv, start=True, stop=False),
                        pe.matmul(PS[bc_][:], lhsT=ident_b, rhs=negm[:], start=False, stop=True))[1],
                        reads=[("R2", c, t), nmt, ("CB",)], writes=[("ps", bc_)])
                    tt(t1[:], PS[bc_][:], q2[:], ALU.mult, [("ps", bc_), qa, qb], [t1a, t1b])
                    act(cbv, t1[:], AF.Silu, [t1a, t1b, ("CF",)], [("R2", c, t)], bias=b_ln(c), scale=g_ln(c))

            st = ln_stats(0)
            for t in range(NT):
                nxt = ln_stats(t + 1) if t + 1 < NT else None
                ln_norm(t, st)
                st = nxt
            dump("cs", R2[:], [128, 8, S], BF16, [x for t in range(NT) for x in r2tok(t)])
        stages.append((None, p2ln))

        for j2 in range(4):
            spec = spec_cols(w_pw_v, [(j2 * 256, 256)]) + [
                (lambda j: WS[j][:, :, 256:512], w_in_v[:, :, C3 + j2 * 256: C3 + (j2 + 1) * 256])]

            def p2b(slot, j2=j2):
                for cc in range(2):
                    c = j2 * 2 + cc
                    for t in range(NT):
                        by, bg = nbank(), nbank()
                        mm(by, [(WS[slot][:, k, cc * 128:(cc + 1) * 128], R2[:, k, t * TT:(t + 1) * TT]) for k in range(8)],
                           [("W", slot)] + r2tok(t))
                        mm(bg, [(WS[slot][:, k, 256 + cc * 128:256 + (cc + 1) * 128], R1[:, k, t * TT:(t + 1) * TT]) for k in range(8)],
                           [("W", slot)] + r1tok(t))
                        sg, sa, sb_ = tf((c * NT + t) % 2)
                        act(sg[:], PS[bg][:], AF.Sigmoid, [("ps", bg), ("CF",)], [sa, sb_], bias=b_gate(c))
                        tt(R3[:, c, t * TT:(t + 1) * TT], PS[by][:], sg[:], ALU.mult, [("ps", by), sa, sb_], [("R3", c, t)])
            stages.append((spec, p2b))

        def dump_m1():
            dump("m1", R3[:], [128, 8, S], BF16, [x for t in range(NT) for x in r3tok(t)])
        stages.append((None, dump_m1))

        for hg in range(2):
            for which in range(2):
                spec = spec_cols(w_in_v, [((C0 if which == 0 else C1) + hg * 512, 512)])

                def p3qk(slot, hg=hg, which=which):
                    if hg == 0 and which == 0:
                        P.barrier()
                    dst = QT if which == 0 else KT
                    dn = "QT" if which == 0 else "KT"
                    for hh in range(4):
                        for t in range(NT):
                            bz = nbank()
                            mm(bz, [(WS[slot][:, k, hh * 128:(hh + 1) * 128], R1[:, k, t * TT:(t + 1) * TT]) for k in range(8)],
                               [("W", slot)] + r1tok(t))
                            i2 = (hh * NT + t) % 2
                            qb_ = TH[0][i2]
                            qbt = ("T", 0, i2)
                            act(qb_[:], PS[bz][:], AF.Copy, [("ps", bz)], [qbt])
                            bp = nbank()
                            P.add("pe", lambda pe, bp=bp, qb_=qb_: pe.matmul(PS[bp][:], lhsT=pm_b, rhs=qb_[:], start=True, stop=True),
                                  reads=[qbt, ("CB",)], writes=[("ps", bp)])
                            tS, tSa, tSb = tf(1 + i2)
                            qf, qfa, qfb = tf(3 + i2)
                            tt(tS[:], PS[bp][:], RS[:, t * TT:(t + 1) * TT], ALU.mult, [("ps", bp), ("ROPE",)], [tSa, tSb])
                            tt(qf[:], PS[bz][:], RC[:, t * TT:(t + 1) * TT], ALU.mult, [("ps", bz), ("ROPE",)], [qfa, qfb])
                            tt(dst[:, hh, t * TT:(t + 1) * TT], qf[:], tS[:], ALU.add, [tSa, tSb, qfa, qfb], [(dn, hh, t)])
                stages.append((spec, p3qk))
            spec = spec_cols(w_in_v, [(C2 + hg * 512, 512)])

            def p3v(slot, hg=hg):
                for tq in range(S // 128):
                    bv = nbank()
                    mm(bv, [(R1[:, k, tq * 128:(tq + 1) * 128], WS[slot][:, k, :]) for k in range(8)],
                       [("W", slot)] + r1tok(tq // 4))
                    P.add("act", lambda e, tq=tq, bv=bv: e.activation(out=VV[:, tq, :], in_=PS[bv][:], func=AF.Copy),
                          reads=[("ps", bv)], writes=[("VV", tq)])
                if hg == 0:
                    dump("qT", QT[:], [128, 4, S], BF16, [("QT", hh, t) for hh in range(4) for t in range(NT)])
                    dump("kT", KT[:], [128, 4, S], BF16, [("KT", hh, t) for hh in range(4) for t in range(NT)])
                    dump("vv", VV[:], [128, S // 128, 512], BF16, [("VV", tq) for tq in range(S // 128)])
            stages.append((spec, p3v))

            def p4(hg=hg):
                its = [(hh, i, j) for hh in range(4) for i in range(NT) for j in range(4 * (i + 1))]
                BO1, BO2, BD1, BD2 = 4, 5, 6, 7

                def geom(n):
                    hh, i, j = its[n]
                    jd = j - 4 * i
                    q0 = jd * 128 if jd > 0 else 0
                    return hh, i, j, jd, q0

                def emit_S(n):
                    hh, i, j, jd, q0 = geom(n)
                    b = 2 * (n % 2)
                    qs = slice(i * TT + q0, (i + 1) * TT)

                    def fn_s(pe, j=j, hh=hh, qs=qs, b=b, q0=q0, diag=(jd >= 0)):
                        pe.matmul(PS[b][:, q0:TT], lhsT=KT[0:64, hh, j * 128:(j + 1) * 128], rhs=QT[0:64, hh, qs],
                                  start=True, stop=not diag)
                        ins = pe.matmul(PS[b + 1][:, q0:TT], lhsT=KT[64:128, hh, j * 128:(j + 1) * 128], rhs=QT[64:128, hh, qs],
                                        start=True, stop=not diag)
                        if diag:
                            for a in range(2):
                                pe.matmul(PS[b][:, q0:q0 + 128], lhsT=LM(0, a), rhs=RM(0, a), start=False, stop=(a == 1))
                                ins = pe.matmul(PS[b + 1][:, q0:q0 + 128], lhsT=LM(1, a), rhs=RM(1, a), start=False, stop=(a == 1))
                        return ins
                    P.add("pe", fn_s, reads=[("KT", hh, j // 4), ("QT", hh, i), ("CB",)], writes=[("ps", b), ("ps", b + 1)])

                def emit_exp(n):
                    hh, i, j, jd, q0 = geom(n)
                    pi = n % 3
                    b = 2 * (n % 2)
                    ptok = [("T", pi, 0), ("T", pi, 1)] if pi < 2 else [("PT2x",)]
                    act(PT2[pi][:, :, q0:TT], PSA[:, b:b + 2, q0:TT], AF.Exp, [("ps", b), ("ps", b + 1)], ptok, scale=0.125)

                def emit_PV(n):
                    hh, i, j, jd, q0 = geom(n)
                    pi = n % 3
                    ptok = [("T", pi, 0), ("T", pi, 1)] if pi < 2 else [("PT2x",)]
                    vv = VV[:, j, hh * 128:(hh + 1) * 128]
                    first, lastj = (j == 0), (j == 4 * (i + 1) - 1)

                    def fn_pv(pe, vv=vv, pi=pi, q0=q0, first=first, lastj=lastj):
                        pe.matmul(PS[BO1][:, q0:TT], lhsT=vv, rhs=PT2[pi][:, 0, q0:TT], start=first, stop=lastj)
                        pe.matmul(PS[BD1][:, q0:TT], lhsT=ones_b, rhs=PT2[pi][:, 0, q0:TT], start=first, stop=lastj)
                        pe.matmul(PS[BO2][:, q0:TT], lhsT=vv, rhs=PT2[pi][:, 1, q0:TT], start=first, stop=lastj)
                        return pe.matmul(PS[BD2][:, q0:TT], lhsT=ones_b, rhs=PT2[pi][:, 1, q0:TT], start=first, stop=lastj)
                    P.add("pe", fn_pv, reads=[("VV", j)] + ptok + [("CB",)],
                          writes=[("ps", BO1), ("ps", BO2), ("ps", BD1), ("ps", BD2)])
                    return lastj

                ltok = [("T", 2, 0), ("T", 2, 1), ("T", 3, 0), ("T", 3, 1)]
                ctok = [("T", 4, 0), ("T", 4, 1), ("T", 5, 0), ("T", 5, 1)]
                C12 = sb(f"C12_{hg}_{s}", [128, 2, TT], F32, o_T + 4 * 2048)

                def finalize(n):
                    hh, i, j, jd, q0 = geom(n)
                    head = hg * 4 + hh
                    oo, ooa, oob = tf(6)
                    sq = TH[7][0]
                    act(LT[:], PSA[:, BD1:BD2 + 1, :], AF.Ln, [("ps", BD1), ("ps", BD2)], ltok)
                    P.add("dve", lambda e: e.tensor_copy(out=C12[:], in_=PSA[:, BO1:BO2 + 1, :]),
                          reads=[("ps", BO1), ("ps", BO2)], writes=ctok)
                    yield
                    act(LT[:], LT[:], AF.Exp, ltok, ltok, scale=-1.0)
                    tt(C12[:], C12[:], LT[:], ALU.mult, ctok + ltok, ctok)
                    stt(oo[:], C12[:, 1, :], neglam, C12[:, 0, :], ALU.mult, ALU.add, ctok + [("LP2",)], [ooa, oob])
                    tt(sq[:], oo[:], oo[:], ALU.mult, [ooa, oob], [("T", 7, 0)])
                    yield
                    yield
                    yield
                    bank = 2 * ((cur[0] + 1) % 2)
                    P.add("pe", lambda pe, bank=bank: pe.matmul(PS[bank][:], lhsT=ones_b, rhs=sq[:], start=True, stop=True),
                          reads=[("T", 7, 0), ("CB",)], writes=[("ps", bank)])
                    rs_, rsa, rsb = RSB, ("RSB", 0), ("RSB", 1)
                    act(rs_[:], PS[bank][:], AF.Ln, [("ps", bank), ("LPe",)], [rsa, rsb], bias=eps_ap(1e-6), scale=1.0 / 128)
                    yield
                    act(rs_[:], rs_[:], AF.Exp, [rsa, rsb], [rsa, rsb], scale=-0.5)
                    stt(R2[:, head, i * TT:(i + 1) * TT], oo[:], gsubs, rs_[:], ALU.mult, ALU.mult,
                        [ooa, oob, rsa, rsb, ("LP3",)], [("R2", head, i)])

                N = len(its)
                cur = [0]
                queue = []
                emit_S(0)
                if N > 1:
                    emit_S(1)
                emit_exp(0)
                def emit_dummy(n, k):
                    hh, i, j, jd, q0 = geom(n)
                    if j == 4 * (i + 1) - 1 or k == 0:
                        return
                    def fn_d(pe, hh=hh, k=k):
                        ins = None
                        for _ in range(k):
                            ins = pe.matmul(PS[BO1][:], lhsT=zeros_b, rhs=KT[:, hh, 0:TT], start=False, stop=False)
                        return ins
                    P.add("pe", fn_d, reads=[("KT", hh, 0), ("CB",)], writes=[("ps", BO1)])

                for n in range(N):
                    cur[0] = n
                    if n + 2 < N:
                        emit_S(n + 2)
                    lastj = emit_PV(n)
                    emit_dummy(n, NDUM_FIN if queue else (NDUM_DIAG if geom(n)[3] >= 0 else 0))
                    if n + 1 < N:
                        emit_exp(n + 1)
                    for g in list(queue):
                        try:
                            next(g)
                        except StopIteration:
                            queue.remove(g)
                    if lastj:
                        g = finalize(n)
                        next(g)
                        queue.append(g)
                while queue:
                    cur[0] += 1
                    for g in list(queue):
                        try:
                            next(g)
                        except StopIteration:
                            queue.remove(g)
            stages.append((None, p4))

        def dump_on():
            dump("on", R2[:], [128, 8, S], BF16, [x for t in range(NT) for x in r2tok(t)])
        stages.append((None, dump_on))

        for j2 in range(4):
            spec = spec_cols(w_ap_v, [(j2 * 256, 256)]) + [
                (lambda j: WS[j][:, :, 256:512], w_in_v[:, :, C3 + D + j2 * 256: C3 + D + (j2 + 1) * 256])]

            def p5(slot, j2=j2):
                for cc in range(2):
                    c = j2 * 2 + cc
                    for t in range(NT):
                        by, bg = nbank(), nbank()
                        mm(by, [(WS[slot][:, k, cc * 128:(cc + 1) * 128], R2[:, k, t * TT:(t + 1) * TT]) for k in range(8)],
                           [("W", slot)] + r2tok(t))
                        mm(bg, [(WS[slot][:, k, 256 + cc * 128:256 + (cc + 1) * 128], R1[:, k, t * TT:(t + 1) * TT]) for k in range(8)],
                           [("W", slot)] + r1tok(t))
                        i2 = (c * NT + t) % 2
                        sg, sa, sb_ = tf(i2)
                        tm, tma, tmb = tf(2 + i2)
                        act(sg[:], PS[bg][:], AF.Sigmoid, [("ps", bg), ("CF",)], [sa, sb_], bias=b_gate(8 + c))
                        tt(tm[:], PS[by][:], sg[:], ALU.mult, [("ps", by), sa, sb_], [tma, tmb])
                        tt(R3[:, c, t * TT:(t + 1) * TT], tm[:], R3[:, c, t * TT:(t + 1) * TT], ALU.add,
                           [tma, tmb, ("R3", c, t)], [("R3", c, t)])
            stages.append((spec, p5))

        def dump_mg():
            dump("mg", R3[:], [128, 8, S], BF16, [x for t in range(NT) for x in r3tok(t)])
        stages.append((None, dump_mg))

        for j2 in range(2):
            spec = spec_cols(w_out_v, [(j2 * 512, 512)])

            def p6(slot, j2=j2):
                if j2 == 0:
                    P.barrier()
                for cc in range(4):
                    c = j2 * 4 + cc
                    for t in range(NT):
                        bz = nbank()
                        mm(bz, [(WS[slot][:, k, cc * 128:(cc + 1) * 128], R3[:, k, t * TT:(t + 1) * TT]) for k in range(8)],
                           [("W", slot)] + r3tok(t))
                        i2 = (c * NT + t) % 2
                        xs, xa, xb = tf(i2)
                        dma("sp", xs[:], xT[c * 128:(c + 1) * 128, tok0 + t * TT: tok0 + (t + 1) * TT], [], [xa, xb])
                        tt(X1[:, c, t * TT:(t + 1) * TT], PS[bz][:], xs[:], ALU.add, [("ps", bz), xa, xb], [("X1", c, t)])
            stages.append((spec, p6))

        def p6n():
            P.barrier()
            dump("x1", X1[:], [128, 8, S], F32, [("X1", c, t) for c in range(8) for t in range(NT)])
            for t in range(NT):
                b = nbank()
                for k in range(8):
                    sq, sqt = TH[2][k % 2], ("T", 2, k % 2)
                    act(sq[:], X1[:, k, t * TT:(t + 1) * TT], AF.Square, [("X1", k, t)], [sqt])
                    P.add("pe", lambda pe, k=k, sq=sq, b=b: pe.matmul(PS[b][:], lhsT=ones_b, rhs=sq[:], start=(k == 0), stop=(k == 7)),
                          reads=[sqt, ("CB",)], writes=[("ps", b)])
                rt, ra, rb = rstd_tile(b, 1.0 / D, 1e-6, 3, "sqrt")
                for k in range(8):
                    stt(R1[:, k, t * TT:(t + 1) * TT], X1[:, k, t * TT:(t + 1) * TT], g_ffn(k), rt[:], ALU.mult, ALU.mult,
                        [("X1", k, t), ra, rb, ("CF",)], [("R1", k, t)])
        stages.append((None, p6n))

        for g in range(NT // TG):
            tiles = [g * TG + x for x in range(TG)]
            for fs in range(11):
                spec = spec_cols(w_fg_v, [(fs * 256, 256)]) + [
                    (lambda j: WS[j][:, :, 256:512], w_fu_v[:, :, fs * 256:(fs + 1) * 256])]

                def p7a(slot, fs=fs, tiles=tiles):
                    for fc in range(2):
                        f = fs * 2 + fc
                        for ti, t in enumerate(tiles):
                            bg, bu = nbank(), nbank()
                            mm(bg, [(WS[slot][:, k, fc * 128:(fc + 1) * 128], R1[:, k, t * TT:(t + 1) * TT]) for k in range(8)],
                               [("W", slot)] + r1tok(t))
                            mm(bu, [(WS[slot][:, k, 256 + fc * 128:256 + (fc + 1) * 128], R1[:, k, t * TT:(t + 1) * TT]) for k in range(8)],
                               [("W", slot)] + r1tok(t))
                            sl, sla, slb = tf((f * TG + ti) % 2)
                            act(sl[:], PS[bg][:], AF.Silu, [("ps", bg)], [sla, slb])
                            tt(FF[:, f, ti * TT:(ti + 1) * TT], PS[bu][:], sl[:], ALU.mult, [("ps", bu), sla, slb], [("FF", f, ti)])
                stages.append((spec, p7a))
            for c in range(8):
                spec = spec_down(c)

                def p7b(slot, c=c, tiles=tiles):
                    for ti, t in enumerate(tiles):
                        bz = nbank()
                        mm(bz, [(WD[slot][:, f, :], FF[:, f, ti * TT:(ti + 1) * TT]) for f in range(FC)],
                           [("W", slot)] + [("FF", f, ti) for f in range(FC)])
                        tt(X1[:, c, t * TT:(t + 1) * TT], PS[bz][:], X1[:, c, t * TT:(t + 1) * TT], ALU.add,
                           [("ps", bz), ("X1", c, t)], [("X1", c, t)])
                stages.append((spec, p7b))

            def p7n(tiles=tiles):
                for t in tiles:
                    b = nbank()
                    for k in range(8):
                        sq, sqt = TH[2][k % 2], ("T", 2, k % 2)
                        act(sq[:], X1[:, k, t * TT:(t + 1) * TT], AF.Square, [("X1", k, t)], [sqt])
                        P.add("pe", lambda pe, k=k, sq=sq, b=b: pe.matmul(PS[b][:], lhsT=ones_b, rhs=sq[:], start=(k == 0), stop=(k == 7)),
                              reads=[sqt, ("CB",)], writes=[("ps", b)])
                    rt, ra, rb = rstd_tile(b, 1.0 / D, 1e-6, 3, "sqrt")
                    for k in range(8):
                        stt(X1[:, k, t * TT:(t + 1) * TT], X1[:, k, t * TT:(t + 1) * TT], g_fin(k), rt[:], ALU.mult, ALU.mult,
                            [("X1", k, t), ra, rb, ("CF",)], [("X1", k, t)])
                    dma("sp", outT_v[:, :, tok0 + t * TT: tok0 + (t + 1) * TT], X1[:, :, t * TT:(t + 1) * TT],
                        [("X1", k, t) for k in range(8)], [("OUT", s, t)])
            stages.append((None, p7n))
        return stages

    all_stages = []
    for s in range(NSEQ):
        all_stages.extend(seq_stages(s))
    widx = [i for i, (sp_, _) in enumerate(all_stages) if sp_ is not None]
    slots = {}
    issued = 0
    for i, (sp_, body) in enumerate(all_stages):
        r = sum(1 for x in widx if x < i)
        while issued < len(widx) and issued <= r + 1:
            wi = widx[issued]
            slots[wi] = fill(all_stages[wi][0])
            issued += 1
        if sp_ is None:
            body()
        else:
            body(slots[i])

    P.analyze()
    P.emit()
    return nc, dbg_out, P


def _host_consts(inp, S):
    cf = np.zeros((128, NCF), np.float32)

    def pk(v):
        return np.ascontiguousarray(np.asarray(v, np.float32).reshape(-1, 128).T)
    cf[:, 0:8] = pk(inp["g_mix"][0])
    cf[:, 8:24] = pk(inp["b_gate"][0])
    wdw = np.asarray(inp["w_dw"][0], np.float32)
    cf[:, 24:272] = wdw.T.reshape(8, 128, CW).transpose(1, 0, 2).reshape(128, 8 * CW)
    cf[:, 272:280] = pk(inp["b_dw"][0])
    cf[:, 280:288] = pk(inp["g_conv_ln"][0])
    cf[:, 288:296] = pk(inp["b_conv_ln"][0])
    cf[:, 296:304] = pk(inp["g_ffn"][0])
    cf[:, 304:312] = pk(inp["g_final"])
    cf[:, 312] = np.asarray(inp["g_subln"][0], np.float32)
    for i, nm in enumerate(("lambda_q1", "lambda_k1", "lambda_q2", "lambda_k2")):
        cf[0:64, 313 + i] = np.asarray(inp[nm][0], np.float32)
    cb = np.zeros((128, 1024), np.float32)
    cb[:, 0:128] = 1.0
    for o in (0, 64):
        for i in range(8):
            cb[o + i + 8, 128 + o + i] = -1.0
            cb[o + i, 128 + o + 8 + i] = 1.0
    kk = np.arange(128)
    cb[:, 256:384] = np.eye(128, dtype=np.float32)
    maskneg = np.where(kk[:, None] > kk[None, :], -30000.0, 0.0).astype(np.float32)
    for h in range(2):
        for p in range(64):
            cb[64 * h + p, 512 + p] = 1.0
            cb[64 * h + p, 512 + 128 + p + 64] = 1.0
        cb[64 * h:64 * h + 64, 768:896] = maskneg[0:64]
        cb[64 * h:64 * h + 64, 896:1024] = maskneg[64:128]
    cb = cb.astype(ml_dtypes.bfloat16)
    inv_freq = (np.float32(500000.0) ** (-np.arange(0, 16, 2, dtype=np.float32) / np.float32(16))).astype(np.float32)
    pos = np.arange(S, dtype=np.float32)
    ang = (pos[:, None] * inv_freq[None, :]).astype(np.float32)
    cs, sn = np.cos(ang).astype(np.float32).T, np.sin(ang).astype(np.float32).T
    rope = np.zeros((128, 2, S), np.float32)
    rope[:, 0, :] = 1.0
    for o in (0, 64):
        rope[o:o + 8, 0] = cs
        rope[o + 8:o + 16, 0] = cs
        rope[o:o + 8, 1] = sn
        rope[o + 8:o + 16, 1] = sn
    return cf, cb, rope


def make_in_maps(inp, S, NSEQ, ncores):
    x = np.asarray(inp["x"], np.float32)
    cf, cb, rope = _host_consts(inp, S)
    shared = {
        "w_in": np.ascontiguousarray(inp["w_in"][0], dtype=np.float32),
        "w_conv_pw": np.ascontiguousarray(inp["w_conv_pw"][0], dtype=np.float32),
        "w_attn_pw": np.ascontiguousarray(inp["w_attn_pw"][0], dtype=np.float32),
        "w_out": np.ascontiguousarray(inp["w_out"][0], dtype=np.float32),
        "w_ffn_gate": np.ascontiguousarray(inp["w_ffn_gate"][0], dtype=np.float32),
        "w_ffn_up": np.ascontiguousarray(inp["w_ffn_up"][0], dtype=np.float32),
        "w_ffn_down": np.ascontiguousarray(inp["w_ffn_down"][0], dtype=np.float32),
        "cf32": cf, "cb16": cb, "rope": rope,
    }
    maps = []
    for c in range(ncores):
        xs = x[c * NSEQ:(c + 1) * NSEQ, :S]
        xT = np.ascontiguousarray(xs.reshape(NSEQ * S, D).T)
        m = dict(shared)
        m["xT"] = xT
        maps.append(m)
    return maps


def kernel(**inputs):
    S, NSEQ, NCORES = 2048, 2, 8
    nc, _, _ = build_nc(S, NSEQ)
    in_maps = make_in_maps(inputs, S, NSEQ, NCORES)
    res = run_bass_kernel_spmd(nc, in_maps, core_ids=list(range(NCORES)))
    outs = []
    for c in range(NCORES):
        oT = np.asarray(res.results[c]["outT"], np.float32)
        outs.append(oT.T.reshape(NSEQ, S, D))
    return np.ascontiguousarray(np.concatenate(outs, axis=0), dtype=np.float32)
```

```python
import math
import numpy as np
import ml_dtypes
import concourse.bass as bass
import concourse.mybir as mybir
from concourse.bass_utils import run_bass_kernel_spmd

F32 = mybir.dt.float32
BF16 = mybir.dt.bfloat16
AF = mybir.ActivationFunctionType
ALU = mybir.AluOpType

D = 1024
KC = 8
NH = 8
DFF = 2816
FC = 22
CW = 31
IN_COLS = 7168
C0, C1, C2, C3 = 2048, 3072, 4096, 5120
LAMBDA_INIT = 0.8 - 0.6 * math.exp(-0.3 * 0)
NCF = 320
TT = 512
NDUM_FIN = 0
NPE_TAPS = 25
NDUM_DIAG = 0
DBG_TRI = False

ENGS = ("pe", "act", "dve", "pool", "sp")
SEM_LIMIT = 30000
NDMA_SEM = 8


class Op:
    __slots__ = ("eng", "fn", "reads", "writes", "dma", "deps", "ord", "gen", "slot", "target", "need_inc", "bar")

    def __init__(self, eng, fn, reads, writes, dma):
        self.eng = eng
        self.fn = fn
        self.reads = reads
        self.writes = writes
        self.dma = dma
        self.deps = ()
        self.ord = 0
        self.gen = 0
        self.slot = 0
        self.target = 0
        self.need_inc = False
        self.bar = False


class Prog:
    def __init__(self, nc):
        self.nc = nc
        self.ops = []

    def add(self, eng, fn, reads=(), writes=(), dma=False):
        self.ops.append(Op(eng, fn, tuple(reads), tuple(writes), dma))

    def barrier(self):
        o = Op("bar", None, (), (), False)
        o.bar = True
        self.ops.append(o)

    def analyze(self):
        ops = self.ops
        tok_w, tok_r = {}, {}
        last_c = {}
        last_d = {}
        bar_deps = {}
        dma_cnt = {"sp": 0, "pool": 0}
        for i, op in enumerate(ops):
            if op.bar:
                deps = set(last_c.values())
                for (q, s), j in last_d.items():
                    if q != "pool":
                        deps.add(j)
                for e in ENGS:
                    if e != "pool":
                        bar_deps.setdefault(e, set()).update(deps)
                continue
            deps = set()
            for t in op.reads:
                deps.update(tok_w.get(t, {}).values())
            for t in op.writes:
                deps.update(tok_w.get(t, {}).values())
                deps.update(tok_r.get(t, {}).values())
            if op.eng in bar_deps:
                deps.update(bar_deps.pop(op.eng))
            if op.dma:
                n = dma_cnt[op.eng]
                dma_cnt[op.eng] = n + 1
                op.slot = n % NDMA_SEM
                prev = last_d.get((op.eng, op.slot))
                if prev is not None:
                    deps.add(prev)
                    op.target = ops[prev].target + 16
                else:
                    op.target = 16
                last_d[(op.eng, op.slot)] = i
                key = ("d", i)
            else:
                last_c[op.eng] = i
                key = op.eng
            for t in op.reads:
                tok_r.setdefault(t, {})[key] = i
            for t in op.writes:
                if tok_r.get(t):
                    tok_w[t] = {key: i}
                    tok_r[t] = {}
                else:
                    tok_w.setdefault(t, {})[key] = i
            deps.discard(i)
            best = {}
            out = []
            for d in deps:
                dop = ops[d]
                if dop.dma:
                    out.append(d)
                else:
                    if dop.eng == "pe" and op.eng == "pe" and not op.dma:
                        continue
                    if d > best.get(dop.eng, -1):
                        best[dop.eng] = d
            out.extend(best.values())
            op.deps = tuple(sorted(out))
            for d in op.deps:
                ops[d].need_inc = True
        cnt = {e: 0 for e in ENGS}
        gen = {e: 0 for e in ENGS}
        for op in ops:
            if op.bar or op.dma or not op.need_inc:
                continue
            if cnt[op.eng] >= SEM_LIMIT:
                cnt[op.eng] = 0
                gen[op.eng] += 1
            cnt[op.eng] += 1
            op.ord = cnt[op.eng]
            op.gen = gen[op.eng]
        self.ngen = {e: gen[e] + 1 for e in ENGS}

    def emit(self):
        nc = self.nc
        engobj = {"pe": nc.tensor, "act": nc.scalar, "dve": nc.vector, "pool": nc.gpsimd, "sp": nc.sync}
        csem = {e: [nc.alloc_semaphore(f"c_{e}_{g}") for g in range(self.ngen[e])] for e in ENGS}
        dsem = {q: [nc.alloc_semaphore(f"d_{q}_{s}") for s in range(NDMA_SEM)] for q in ("sp", "pool")}
        waited = {e: {} for e in ENGS}
        ops = self.ops
        final_d = {}
        nwait = 0
        for op in ops:
            if op.bar:
                continue
            E = engobj[op.eng]
            w = waited[op.eng]
            for d in op.deps:
                dop = ops[d]
                if dop.dma:
                    key = ("d", dop.eng, dop.slot)
                    val = dop.target
                    sem = dsem[dop.eng][dop.slot]
                else:
                    key = ("c", dop.eng, dop.gen)
                    val = dop.ord
                    sem = csem[dop.eng][dop.gen]
                if w.get(key, 0) >= val:
                    continue
                E.wait_ge(sem, val)
                nwait += 1
                w[key] = val
            ins = op.fn(E)
            if op.dma:
                ins.then_inc(dsem[op.eng][op.slot], 16)
                final_d[(op.eng, op.slot)] = op.target
            elif op.need_inc:
                ins.then_inc(csem[op.eng][op.gen], 1)
        for (q, s), tgt in sorted(final_d.items()):
            nc.sync.wait_ge(dsem[q][s], tgt)
        self.nwait = nwait


def build_nc(S=2048, NSEQ=2, dbg=()):
    NT = S // TT
    TG = min(2, NT)
    NTOK = S * NSEQ
    nc = bass.Bass("TRN2", target_bir_lowering=False)
    P = Prog(nc)

    def din(name, shape, dt=F32):
        return nc.dram_tensor(name, shape, dt, kind="ExternalInput").ap()

    xT = din("xT", [D, NTOK])
    w_in = din("w_in", [D, IN_COLS])
    w_pw = din("w_conv_pw", [D, D])
    w_ap = din("w_attn_pw", [D, D])
    w_out = din("w_out", [D, D])
    w_fg = din("w_ffn_gate", [D, DFF])
    w_fu = din("w_ffn_up", [D, DFF])
    w_fd = din("w_ffn_down", [DFF, D])
    cf_d = din("cf32", [128, NCF])
    cb_d = din("cb16", [128, 1024], BF16)
    rope_d = din("rope", [128, 2, S])
    outT = nc.dram_tensor("outT", [D, NTOK], F32, kind="ExternalOutput").ap()
    dbg_out = {}

    w_in_v = w_in.rearrange("(k p) c -> p k c", p=128)
    w_pw_v = w_pw.rearrange("(k p) c -> p k c", p=128)
    w_ap_v = w_ap.rearrange("(k p) c -> p k c", p=128)
    w_out_v = w_out.rearrange("(k p) c -> p k c", p=128)
    w_fg_v = w_fg.rearrange("(k p) c -> p k c", p=128)
    w_fu_v = w_fu.rearrange("(k p) c -> p k c", p=128)
    w_fd_v = w_fd.rearrange("(f p) c -> p f c", p=128)
    xT_v = xT.rearrange("(k p) t -> p k t", p=128)
    outT_v = outT.rearrange("(k p) t -> p k t", p=128)

    def sb(name, shape, dt, off):
        return nc.alloc_sbuf_tensor_at(name, shape, dt, offset=off)

    BASE = 24576 - 256
    SB2 = S * 2 * 8
    o_R1 = BASE
    o_R2 = o_R1 + SB2
    o_A = o_R2 + SB2
    A_SZ = 49152
    o_R3 = o_A + A_SZ
    o_W = o_R3 + SB2
    NW = 3
    o_ROPE = o_W + NW * 8192
    o_T = o_ROPE + 2 * S * 4
    assert o_T + 16384 <= 229376 - 256, o_T
    R1 = sb("R1", [128, 8, S], BF16, o_R1)
    R2 = sb("R2", [128, 8, S], BF16, o_R2)
    R3 = sb("R3", [128, 8, S], BF16, o_R3)
    X1 = sb("X1", [128, 8, S], F32, o_R2)
    assert 8 * S * 4 <= SB2 + 32768
    FF = sb("FF", [128, FC, TG * TT], BF16, o_A + 32768)
    assert FC * TG * TT * 2 <= 16384 + SB2 or S < 2048
    QT = sb("QT", [128, 4, S], BF16, o_A)
    KT = sb("KT", [128, 4, S], BF16, o_A + 16384)
    VV = sb("VV", [128, S // 128, 512], BF16, o_A + 32768)
    XS = [sb(f"XS{i}", [128, 8, TT], F32, o_A + i * 16384) for i in range(2)]
    UL = 30 + S
    ub = ((UL * 2 + 31) // 32) * 32
    U16 = [sb(f"U16_{i}", [128, UL], BF16, o_A + i * ub) for i in range(2)]
    DG = [sb(f"DG{i}", [128, CW, 128], BF16, o_A + 2 * ub + i * CW * 256) for i in range(2)]
    assert 2 * ub + 2 * CW * 256 <= A_SZ
    WS = [sb(f"WS{j}", [128, 8, 512], BF16, o_W + j * 8192) for j in range(NW)]
    WD = [sb(f"WD{j}", [128, FC, 128], BF16, o_W + j * 8192) for j in range(NW)]
    RC = sb("RC", [128, S], F32, o_ROPE)
    RS = sb("RS", [128, S], F32, o_ROPE + S * 4)
    TF = [sb(f"TF{i}", [128, TT], F32, o_T + i * 2048) for i in range(8)]
    TH = [[sb(f"TH{i}_{h}", [128, TT], BF16, o_T + i * 2048 + h * 1024) for h in range(2)] for i in range(8)]

    PT2 = [sb(f"PT2_{i}", [128, 2, TT], BF16, o_T + i * 2048) for i in range(2)]
    LT = sb("LT", [128, 2, TT], F32, o_T + 2 * 2048)

    def tf(i):
        return TF[i], ("T", i, 0), ("T", i, 1)

    AUX = 16768
    CF = sb("CF", [128, NCF], F32, AUX)
    CB = sb("CB", [128, 1024], BF16, AUX + 1280)
    LP = sb("LP", [128, 8], F32, AUX + 1280 + 2048)
    RSB = sb("RSB", [128, TT], F32, AUX + 1280 + 2048 + 64)
    PT2.append(sb("PT2_2", [128, 2, TT], BF16, AUX + 1280 + 2048 + 64 + 2048))
    assert AUX + 1280 + 2048 + 64 + 4096 <= BASE
    ONESF = sb("ONESF", [128, 128], F32, o_T)
    ones_b = CB[:, 0:128]
    pm_b = CB[:, 128:256]
    ident_b = CB[:, 256:384]
    zeros_b = CB[:, 384:512]
    LM = lambda h, a: CB[64 * h:64 * h + 64, 512 + a * 128:512 + (a + 1) * 128]
    RM = lambda h, a: CB[64 * h:64 * h + 64, 768 + a * 128:768 + (a + 1) * 128]
    g_mix = lambda k: CF[:, k:k + 1]
    b_gate = lambda c: CF[:, 8 + c:9 + c]
    w_dw = lambda c, j: CF[:, 24 + c * CW + j:25 + c * CW + j]
    b_dw = lambda c: CF[:, 272 + c:273 + c]
    g_ln = lambda c: CF[:, 280 + c:281 + c]
    b_ln = lambda c: CF[:, 288 + c:289 + c]
    g_ffn = lambda k: CF[:, 296 + k:297 + k]
    g_fin = lambda k: CF[:, 304 + k:305 + k]
    neglam = LP[:, 4:5]
    gsubs = LP[:, 5:6]

    PSA = nc.alloc_psum_tensor("psa", [128, 8, TT], F32)

    class _B:
        def __init__(self, b):
            self.b = b

        def __getitem__(self, idx):
            if not isinstance(idx, tuple):
                idx = (idx,)
            return PSA[(idx[0], self.b) + tuple(idx[1:])]
    PS = [_B(i) for i in range(8)]
    bank_ctr = [0]

    def nbank():
        b = bank_ctr[0] % 8
        bank_ctr[0] += 1
        return b

    def mm(bank, pairs, reads, ncols=None, c0=0):
        out_ap = PS[bank][:, c0:(c0 + ncols)] if ncols is not None else PS[bank][:]

        def fn(pe, pairs=pairs, out_ap=out_ap):
            n = len(pairs)
            ins = None
            for i, (l, r) in enumerate(pairs):
                ins = pe.matmul(out_ap, lhsT=l, rhs=r, start=(i == 0), stop=(i == n - 1))
            return ins
        P.add("pe", fn, reads=reads, writes=[("ps", bank)])

    def act(out, in_, func, reads, writes, bias=None, scale=None):
        kw = {}
        if bias is not None:
            kw["bias"] = bias
        if scale is not None:
            kw["scale"] = scale
        P.add("act", lambda e, kw=kw: e.activation(out=out, in_=in_, func=func, **kw), reads=reads, writes=writes)

    def tt(out, in0, in1, op, reads, writes, eng="dve"):
        P.add(eng, lambda e: e.tensor_tensor(out=out, in0=in0, in1=in1, op=op), reads=reads, writes=writes)

    def stt(out, in0, scalar, in1, op0, op1, reads, writes):
        P.add("dve", lambda e: e.scalar_tensor_tensor(out=out, in0=in0, scalar=scalar, in1=in1, op0=op0, op1=op1),
              reads=reads, writes=writes)

    def ts(out, in0, s1, s2, op0, op1, reads, writes):
        if s2 is None:
            P.add("dve", lambda e: e.tensor_scalar(out=out, in0=in0, scalar1=s1, scalar2=None, op0=op0),
                  reads=reads, writes=writes)
        else:
            P.add("dve", lambda e: e.tensor_scalar(out=out, in0=in0, scalar1=s1, scalar2=s2, op0=op0, op1=op1),
                  reads=reads, writes=writes)

    def recip(out, in_, reads, writes):
        P.add("dve", lambda e: e.reciprocal(out=out, in_=in_), reads=reads, writes=writes)

    def dma(q, out, in_, reads, writes):
        P.add(q, lambda e: e.dma_start(out=out, in_=in_), reads=reads, writes=writes, dma=True)

    def dump(name, ap, shape, dt, reads):
        if name not in dbg:
            return
        t = nc.dram_tensor("dbg_" + name, shape, dt, kind="ExternalOutput").ap()
        dbg_out[name] = t
        dma("sp", t, ap, reads, [("dbg", name)])

    def rstd_tile(bank, scale, eps, slot, mode):
        t, ta, tb = tf(slot)
        act(t[:], PS[bank][:], AF.Ln, [("ps", bank), ("LPe",)], [ta, tb], bias=eps_ap(eps), scale=scale)
        act(t[:], t[:], AF.Exp, [ta, tb], [ta, tb], scale=-0.5)
        return t, ta, tb

    eps_cols = {1e-6: 6, 1e-5: 7}

    def eps_ap(eps):
        c = eps_cols[eps]
        return LP[:, c:c + 1]

    dma("sp", CF[:], cf_d, [], [("CF",)])
    dma("sp", CB[:], cb_d, [], [("CB",)])
    P.add("dve", lambda e: e.memset(ONESF[:], 1.0), writes=[("T", 0, 0)])
    P.add("dve", lambda e: e.memset(LP[:, 6:7], 1e-6), writes=[("LPe",)])
    P.add("dve", lambda e: e.memset(LP[:, 7:8], 1e-5), writes=[("LPe",)])
    tt(LP[:, 0:1], CF[:, 313:314], CF[:, 314:315], ALU.mult, [("CF",)], [("LP0",)])
    tt(LP[:, 1:2], CF[:, 315:316], CF[:, 316:317], ALU.mult, [("CF",)], [("LP0",)])
    b0 = nbank()
    P.add("pe", lambda pe: pe.matmul(PS[b0][:, 0:2], lhsT=ONESF[:], rhs=LP[:, 0:2], start=True, stop=True),
          reads=[("T", 0, 0), ("LP0",)], writes=[("ps", b0)])
    act(LP[:, 2:4], PS[b0][:, 0:2], AF.Exp, [("ps", b0)], [("LP1",)])
    tt(LP[:, 4:5], LP[:, 3:4], LP[:, 2:3], ALU.subtract, [("LP1",)], [("LP2",)])
    ts(LP[:, 4:5], LP[:, 4:5], -LAMBDA_INIT, None, ALU.add, None, [("LP2",)], [("LP2",)])
    ts(LP[:, 5:6], CF[:, 312:313], 1.0 - LAMBDA_INIT, None, ALU.mult, None, [("CF",)], [("LP3",)])
    CONST_R = [("CF",), ("CB",), ("ROPE",), ("LP2",), ("LP3",), ("LPe",)]

    wstate = {"n": 0}

    def fill(spec):
        j = wstate["n"] % NW
        wstate["n"] += 1
        for (dstf, src) in spec:
            dma("pool", dstf(j), src, [], [("W", j)])
        return j

    def spec_cols(view, ranges):
        sp = []
        o = 0
        for (c0, n) in ranges:
            sp.append((lambda j, o=o, n=n: WS[j][:, :, o:o + n], view[:, :, c0:c0 + n]))
            o += n
        return sp

    def spec_down(c):
        return [(lambda j: WD[j][:], w_fd_v[:, :, c * 128:(c + 1) * 128])]

    def r1tok(t):
        return [("R1", k, t) for k in range(8)]

    def r2tok(t):
        return [("R2", k, t) for k in range(8)]

    def r3tok(t):
        return [("R3", k, t) for k in range(8)]

    def seq_stages(s):
        stages = []
        tok0 = s * S

        def p1():
            P.barrier()
            for t in range(NT):
                xs = XS[t % 2]
                xtok = ("XS", t % 2)
                dma("sp", xs[:], xT_v[:, :, tok0 + t * TT: tok0 + (t + 1) * TT], [], [xtok])
                b = nbank()
                for k in range(8):
                    sq = TH[0][k % 2]
                    sqt = ("T", 0, k % 2)
                    act(sq[:], xs[:, k, :], AF.Square, [xtok], [sqt])
                    P.add("pe", lambda pe, k=k, sq=sq, b=b: pe.matmul(PS[b][:], lhsT=ones_b, rhs=sq[:], start=(k == 0), stop=(k == 7)),
                          reads=[sqt, ("CB",)], writes=[("ps", b)])
                rt, ra, rb = rstd_tile(b, 1.0 / D, 1e-6, 1, "sqrt")
                for k in range(8):
                    stt(R1[:, k, t * TT:(t + 1) * TT], xs[:, k, :], g_mix(k), rt[:], ALU.mult, ALU.mult,
                        [xtok, ra, rb, ("CF",)], [("R1", k, t)])
            if s == 0:
                dma("sp", RC[:], rope_d[:, 0, :], [], [("ROPE",)])
                dma("sp", RS[:], rope_d[:, 1, :], [], [("ROPE",)])
            dump("hT", R1[:], [128, 8, S], BF16, [x for t in range(NT) for x in r1tok(t)])
        stages.append((None, p1))

        for j2 in range(4):
            spec = spec_cols(w_in_v, [(j2 * 256, 256), (1024 + j2 * 256, 256)])

            def p2a(slot, j2=j2):
                if j2 == 0:
                    P.barrier()
                    for i in range(2):
                        P.add("dve", lambda e, i=i: e.memset(U16[i][:, 0:30], 0.0), writes=[("U16pad", i)])
                for cc in range(2):
                    c = j2 * 2 + cc
                    ui = c % 2
                    for j in range(NPE_TAPS):
                        P.add("dve", lambda e, j=j, c=c, ui=ui: e.tensor_scalar(out=DG[ui][:, j, :], in0=ident_b, scalar1=w_dw(c, j),
                                                                               scalar2=None, op0=ALU.mult),
                              reads=[("CB",), ("CF",)], writes=[("DG", ui)])
                    for t in range(NT):
                        ba, bg = nbank(), nbank()
                        mm(ba, [(WS[slot][:, k, cc * 128:(cc + 1) * 128], R1[:, k, t * TT:(t + 1) * TT]) for k in range(8)],
                           [("W", slot)] + r1tok(t))
                        mm(bg, [(WS[slot][:, k, 256 + cc * 128:256 + (cc + 1) * 128], R1[:, k, t * TT:(t + 1) * TT]) for k in range(8)],
                           [("W", slot)] + r1tok(t))
                        sg, sa, sb_ = tf(t % 2)
                        act(sg[:], PS[bg][:], AF.Sigmoid, [("ps", bg)], [sa, sb_])
                        tt(U16[ui][:, 30 + t * TT:30 + (t + 1) * TT], PS[ba][:], sg[:], ALU.mult,
                           [("ps", ba), sa, sb_], [("U16", ui, t)])
                    for t in range(NT):
                        bc = nbank()
                        rd = [("DG", ui), ("U16", ui, t), ("U16pad", ui)] + ([("U16", ui, t - 1)] if t > 0 else [])
                        mm(bc, [(DG[ui][:, j, :], U16[ui][:, t * TT + j:t * TT + j + TT]) for j in range(NPE_TAPS)], rd)
                        ca, caa, cab = tf(4 + (t % 2))
                        urd = [x for x in rd if x[0] != "DG"] + [("CF",)]
                        ts(ca[:], U16[ui][:, t * TT + NPE_TAPS:t * TT + NPE_TAPS + TT], w_dw(c, NPE_TAPS), None, ALU.mult, None,
                           urd, [caa, cab])
                        for j in range(NPE_TAPS + 1, CW):
                            stt(ca[:], U16[ui][:, t * TT + j:t * TT + j + TT], w_dw(c, j), ca[:], ALU.mult, ALU.add,
                                urd + [caa, cab], [caa, cab])
                        stt(R2[:, c, t * TT:(t + 1) * TT], PS[bc][:], b_dw(c), ca[:], ALU.add, ALU.add,
                            [("ps", bc), ("CF",), caa, cab], [("R2", c, t)])
            stages.append((spec, p2a))

        def p2ln():
            dump("cb", R2[:], [128, 8, S], BF16, [x for t in range(NT) for x in r2tok(t)])

            def ln_stats(t):
                bs, bq = nbank(), nbank()
                for c in range(8):
                    cbv = R2[:, c, t * TT:(t + 1) * TT]
                    sq = TH[2][c % 2]
                    sqt = ("T", 2, c % 2)
                    act(sq[:], cbv, AF.Square, [("R2", c, t)], [sqt])
                    P.add("pe", lambda pe, c=c, cbv=cbv, bs=bs: pe.matmul(PS[bs][:], lhsT=ones_b, rhs=cbv, start=(c == 0), stop=(c == 7)),
                          reads=[("R2", c, t), ("CB",)], writes=[("ps", bs)])
                    P.add("pe", lambda pe, c=c, sq=sq, bq=bq: pe.matmul(PS[bq][:], lhsT=ones_b, rhs=sq[:], start=(c == 0), stop=(c == 7)),
                          reads=[sqt, ("CB",)], writes=[("ps", bq)])
                m, ma, mb = tf(3 + 2 * (t % 2))
                q2, qa, qb = tf(4 + 2 * (t % 2))
                ts(m[:], PS[bs][:], 1.0 / D, None, ALU.mult, None, [("ps", bs)], [ma, mb])
                tt(q2[:], m[:], m[:], ALU.mult, [ma, mb], [qa, qb])
                stt(q2[:], PS[bq][:], 1.0 / D, q2[:], ALU.mult, ALU.subtract, [("ps", bq), qa, qb], [qa, qb])
                act(q2[:], q2[:], AF.Ln, [qa, qb, ("LPe",)], [qa, qb], bias=eps_ap(1e-5), scale=1.0)
                act(q2[:], q2[:], AF.Exp, [qa, qb], [qa, qb], scale=-0.5)
                negm = TH[7][t % 2]
                ts(negm[:], PS[bs][:], -1.0 / D, None, ALU.mult, None, [("ps", bs)], [("T", 7, t % 2)])
                return (negm, ("T", 7, t % 2), q2, qa, qb)

            def ln_norm(t, st):
                negm, nmt, q2, qa, qb = st
                for c in range(8):
                    cbv = R2[:, c, t * TT:(t + 1) * TT]
                    t1, t1a, t1b = tf(c % 2)
                    bc_ = nbank()
                    P.add("pe", lambda pe, cbv=cbv, negm=negm, bc_=bc_: (
                        pe.matmul(PS[bc_][:], lhsT=ident_b, rhs=cbv, start=True, stop=False),
                        pe.matmul(PS[bc_][:], lhsT=ident_b, rhs=negm[:], start=False, stop=True))[1],
                        reads=[("R2", c, t), nmt, ("CB",)], writes=[("ps", bc_)])
                    tt(t1[:], PS[bc_][:], q2[:], ALU.mult, [("ps", bc_), qa, qb], [t1a, t1b])
                    act(cbv, t1[:], AF.Silu, [t1a, t1b, ("CF",)], [("R2", c, t)], bias=b_ln(c), scale=g_ln(c))

            st = ln_stats(0)
            for t in range(NT):
                nxt = ln_stats(t + 1) if t + 1 < NT else None
                ln_norm(t, st)
                st = nxt
            dump("cs", R2[:], [128, 8, S], BF16, [x for t in range(NT) for x in r2tok(t)])
        stages.append((None, p2ln))

        for j2 in range(4):
            spec = spec_cols(w_pw_v, [(j2 * 256, 256)]) + [
                (lambda j: WS[j][:, :, 256:512], w_in_v[:, :, C3 + j2 * 256: C3 + (j2 + 1) * 256])]

            def p2b(slot, j2=j2):
                for cc in range(2):
                    c = j2 * 2 + cc
                    for t in range(NT):
                        by, bg = nbank(), nbank()
                        mm(by, [(WS[slot][:, k, cc * 128:(cc + 1) * 128], R2[:, k, t * TT:(t + 1) * TT]) for k in range(8)],
                           [("W", slot)] + r2tok(t))
                        mm(bg, [(WS[slot][:, k, 256 + cc * 128:256 + (cc + 1) * 128], R1[:, k, t * TT:(t + 1) * TT]) for k in range(8)],
                           [("W", slot)] + r1tok(t))
                        sg, sa, sb_ = tf((c * NT + t) % 2)
                        act(sg[:], PS[bg][:], AF.Sigmoid, [("ps", bg), ("CF",)], [sa, sb_], bias=b_gate(c))
                        tt(R3[:, c, t * TT:(t + 1) * TT], PS[by][:], sg[:], ALU.mult, [("ps", by), sa, sb_], [("R3", c, t)])
            stages.append((spec, p2b))

        def dump_m1():
            dump("m1", R3[:], [128, 8, S], BF16, [x for t in range(NT) for x in r3tok(t)])
        stages.append((None, dump_m1))

        for hg in range(2):
            for which in range(2):
                spec = spec_cols(w_in_v, [((C0 if which == 0 else C1) + hg * 512, 512)])

                def p3qk(slot, hg=hg, which=which):
                    if hg == 0 and which == 0:
                        P.barrier()
                    dst = QT if which == 0 else KT
                    dn = "QT" if which == 0 else "KT"
                    for hh in range(4):
                        for t in range(NT):
                            bz = nbank()
                            mm(bz, [(WS[slot][:, k, hh * 128:(hh + 1) * 128], R1[:, k, t * TT:(t + 1) * TT]) for k in range(8)],
                               [("W", slot)] + r1tok(t))
                            i2 = (hh * NT + t) % 2
                            qb_ = TH[0][i2]
                            qbt = ("T", 0, i2)
                            act(qb_[:], PS[bz][:], AF.Copy, [("ps", bz)], [qbt])
                            bp = nbank()
                            P.add("pe", lambda pe, bp=bp, qb_=qb_: pe.matmul(PS[bp][:], lhsT=pm_b, rhs=qb_[:], start=True, stop=True),
                                  reads=[qbt, ("CB",)], writes=[("ps", bp)])
                            tS, tSa, tSb = tf(1 + i2)
                            qf, qfa, qfb = tf(3 + i2)
                            tt(tS[:], PS[bp][:], RS[:, t * TT:(t + 1) * TT], ALU.mult, [("ps", bp), ("ROPE",)], [tSa, tSb])
                            tt(qf[:], PS[bz][:], RC[:, t * TT:(t + 1) * TT], ALU.mult, [("ps", bz), ("ROPE",)], [qfa, qfb])
                            tt(dst[:, hh, t * TT:(t + 1) * TT], qf[:], tS[:], ALU.add, [tSa, tSb, qfa, qfb], [(dn, hh, t)])
                stages.append((spec, p3qk))
            spec = spec_cols(w_in_v, [(C2 + hg * 512, 512)])

            def p3v(slot, hg=hg):
                for tq in range(S // 128):
                    bv = nbank()
                    mm(bv, [(R1[:, k, tq * 128:(tq + 1) * 128], WS[slot][:, k, :]) for k in range(8)],
                       [("W", slot)] + r1tok(tq // 4))
                    P.add("act", lambda e, tq=tq, bv=bv: e.activation(out=VV[:, tq, :], in_=PS[bv][:], func=AF.Copy),
                          reads=[("ps", bv)], writes=[("VV", tq)])
                if hg == 0:
                    dump("qT", QT[:], [128, 4, S], BF16, [("QT", hh, t) for hh in range(4) for t in range(NT)])
                    dump("kT", KT[:], [128, 4, S], BF16, [("KT", hh, t) for hh in range(4) for t in range(NT)])
                    dump("vv", VV[:], [128, S // 128, 512], BF16, [("VV", tq) for tq in range(S // 128)])
            stages.append((spec, p3v))

            def p4(hg=hg):
                its = [(hh, i, j) for hh in range(4) for i in range(NT) for j in range(4 * (i + 1))]
                BO1, BO2, BD1, BD2 = 4, 5, 6, 7

                def geom(n):
                    hh, i, j = its[n]
                    jd = j - 4 * i
                    q0 = jd * 128 if jd > 0 else 0
                    return hh, i, j, jd, q0

                def emit_S(n):
                    hh, i, j, jd, q0 = geom(n)
                    b = 2 * (n % 2)
                    qs = slice(i * TT + q0, (i + 1) * TT)

                    def fn_s(pe, j=j, hh=hh, qs=qs, b=b, q0=q0, diag=(jd >= 0)):
                        pe.matmul(PS[b][:, q0:TT], lhsT=KT[0:64, hh, j * 128:(j + 1) * 128], rhs=QT[0:64, hh, qs],
                                  start=True, stop=not diag)
                        ins = pe.matmul(PS[b + 1][:, q0:TT], lhsT=KT[64:128, hh, j * 128:(j + 1) * 128], rhs=QT[64:128, hh, qs],
                                        start=True, stop=not diag)
                        if diag:
                            for a in range(2):
                                pe.matmul(PS[b][:, q0:q0 + 128], lhsT=LM(0, a), rhs=RM(0, a), start=False, stop=(a == 1))
                                ins = pe.matmul(PS[b + 1][:, q0:q0 + 128], lhsT=LM(1, a), rhs=RM(1, a), start=False, stop=(a == 1))
                        return ins
                    P.add("pe", fn_s, reads=[("KT", hh, j // 4), ("QT", hh, i), ("CB",)], writes=[("ps", b), ("ps", b + 1)])

                def emit_exp(n):
                    hh, i, j, jd, q0 = geom(n)
                    pi = n % 3
                    b = 2 * (n % 2)
                    ptok = [("T", pi, 0), ("T", pi, 1)] if pi < 2 else [("PT2x",)]
                    act(PT2[pi][:, :, q0:TT], PSA[:, b:b + 2, q0:TT], AF.Exp, [("ps", b), ("ps", b + 1)], ptok, scale=0.125)

                def emit_PV(n):
                    hh, i, j, jd, q0 = geom(n)
                    pi = n % 3
                    ptok = [("T", pi, 0), ("T", pi, 1)] if pi < 2 else [("PT2x",)]
                    vv = VV[:, j, hh * 128:(hh + 1) * 128]
                    first, lastj = (j == 0), (j == 4 * (i + 1) - 1)

                    def fn_pv(pe, vv=vv, pi=pi, q0=q0, first=first, lastj=lastj):
                        pe.matmul(PS[BO1][:, q0:TT], lhsT=vv, rhs=PT2[pi][:, 0, q0:TT], start=first, stop=lastj)
                        pe.matmul(PS[BD1][:, q0:TT], lhsT=ones_b, rhs=PT2[pi][:, 0, q0:TT], start=first, stop=lastj)
                        pe.matmul(PS[BO2][:, q0:TT], lhsT=vv, rhs=PT2[pi][:, 1, q0:TT], start=first, stop=lastj)
                        return pe.matmul(PS[BD2][:, q0:TT], lhsT=ones_b, rhs=PT2[pi][:, 1, q0:TT], start=first, stop=lastj)
                    P.add("pe", fn_pv, reads=[("VV", j)] + ptok + [("CB",)],
                          writes=[("ps", BO1), ("ps", BO2), ("ps", BD1), ("ps", BD2)])
                    return lastj

                ltok = [("T", 2, 0), ("T", 2, 1), ("T", 3, 0), ("T", 3, 1)]
                ctok = [("T", 4, 0), ("T", 4, 1), ("T", 5, 0), ("T", 5, 1)]
                C12 = sb(f"C12_{hg}_{s}", [128, 2, TT], F32, o_T + 4 * 2048)

                def finalize(n):
                    hh, i, j, jd, q0 = geom(n)
                    head = hg * 4 + hh
                    oo, ooa, oob = tf(6)
                    sq = TH[7][0]
                    act(LT[:], PSA[:, BD1:BD2 + 1, :], AF.Ln, [("ps", BD1), ("ps", BD2)], ltok)
                    P.add("dve", lambda e: e.tensor_copy(out=C12[:], in_=PSA[:, BO1:BO2 + 1, :]),
                          reads=[("ps", BO1), ("ps", BO2)], writes=ctok)
                    yield
                    act(LT[:], LT[:], AF.Exp, ltok, ltok, scale=-1.0)
                    tt(C12[:], C12[:], LT[:], ALU.mult, ctok + ltok, ctok)
                    stt(oo[:], C12[:, 1, :], neglam, C12[:, 0, :], ALU.mult, ALU.add, ctok + [("LP2",)], [ooa, oob])
                    tt(sq[:], oo[:], oo[:], ALU.mult, [ooa, oob], [("T", 7, 0)])
                    yield
                    yield
                    yield
                    bank = 2 * ((cur[0] + 1) % 2)
                    P.add("pe", lambda pe, bank=bank: pe.matmul(PS[bank][:], lhsT=ones_b, rhs=sq[:], start=True, stop=True),
                          reads=[("T", 7, 0), ("CB",)], writes=[("ps", bank)])
                    rs_, rsa, rsb = RSB, ("RSB", 0), ("RSB", 1)
                    act(rs_[:], PS[bank][:], AF.Ln, [("ps", bank), ("LPe",)], [rsa, rsb], bias=eps_ap(1e-6), scale=1.0 / 128)
                    yield
                    act(rs_[:], rs_[:], AF.Exp, [rsa, rsb], [rsa, rsb], scale=-0.5)
                    stt(R2[:, head, i * TT:(i + 1) * TT], oo[:], gsubs, rs_[:], ALU.mult, ALU.mult,
                        [ooa, oob, rsa, rsb, ("LP3",)], [("R2", head, i)])

                N = len(its)
                cur = [0]
                queue = []
                emit_S(0)
                if N > 1:
                    emit_S(1)
                emit_exp(0)
                def emit_dummy(n, k):
                    hh, i, j, jd, q0 = geom(n)
                    if j == 4 * (i + 1) - 1 or k == 0:
                        return
                    def fn_d(pe, hh=hh, k=k):
                        ins = None
                        for _ in range(k):
                            ins = pe.matmul(PS[BO1][:], lhsT=zeros_b, rhs=KT[:, hh, 0:TT], start=False, stop=False)
                        return ins
                    P.add("pe", fn_d, reads=[("KT", hh, 0), ("CB",)], writes=[("ps", BO1)])

                for n in range(N):
                    cur[0] = n
                    if n + 2 < N:
                        emit_S(n + 2)
                    lastj = emit_PV(n)
                    emit_dummy(n, NDUM_FIN if queue else (NDUM_DIAG if geom(n)[3] >= 0 else 0))
                    if n + 1 < N:
                        emit_exp(n + 1)
                    for g in list(queue):
                        try:
                            next(g)
                        except StopIteration:
                            queue.remove(g)
                    if lastj:
                        g = finalize(n)
                        next(g)
                        queue.append(g)
                while queue:
                    cur[0] += 1
                    for g in list(queue):
                        try:
                            next(g)
                        except StopIteration:
                            queue.remove(g)
            stages.append((None, p4))

        def dump_on():
            dump("on", R2[:], [128, 8, S], BF16, [x for t in range(NT) for x in r2tok(t)])
        stages.append((None, dump_on))

        for j2 in range(4):
            spec = spec_cols(w_ap_v, [(j2 * 256, 256)]) + [
                (lambda j: WS[j][:, :, 256:512], w_in_v[:, :, C3 + D + j2 * 256: C3 + D + (j2 + 1) * 256])]

            def p5(slot, j2=j2):
                for cc in range(2):
                    c = j2 * 2 + cc
                    for t in range(NT):
                        by, bg = nbank(), nbank()
                        mm(by, [(WS[slot][:, k, cc * 128:(cc + 1) * 128], R2[:, k, t * TT:(t + 1) * TT]) for k in range(8)],
                           [("W", slot)] + r2tok(t))
                        mm(bg, [(WS[slot][:, k, 256 + cc * 128:256 + (cc + 1) * 128], R1[:, k, t * TT:(t + 1) * TT]) for k in range(8)],
                           [("W", slot)] + r1tok(t))
                        i2 = (c * NT + t) % 2
                        sg, sa, sb_ = tf(i2)
                        tm, tma, tmb = tf(2 + i2)
                        act(sg[:], PS[bg][:], AF.Sigmoid, [("ps", bg), ("CF",)], [sa, sb_], bias=b_gate(8 + c))
                        tt(tm[:], PS[by][:], sg[:], ALU.mult, [("ps", by), sa, sb_], [tma, tmb])
                        tt(R3[:, c, t * TT:(t + 1) * TT], tm[:], R3[:, c, t * TT:(t + 1) * TT], ALU.add,
                           [tma, tmb, ("R3", c, t)], [("R3", c, t)])
            stages.append((spec, p5))

        def dump_mg():
            dump("mg", R3[:], [128, 8, S], BF16, [x for t in range(NT) for x in r3tok(t)])
        stages.append((None, dump_mg))

        for j2 in range(2):
            spec = spec_cols(w_out_v, [(j2 * 512, 512)])

            def p6(slot, j2=j2):
                if j2 == 0:
                    P.barrier()
                for cc in range(4):
                    c = j2 * 4 + cc
                    for t in range(NT):
                        bz = nbank()
                        mm(bz, [(WS[slot][:, k, cc * 128:(cc + 1) * 128], R3[:, k, t * TT:(t + 1) * TT]) for k in range(8)],
                           [("W", slot)] + r3tok(t))
                        i2 = (c * NT + t) % 2
                        xs, xa, xb = tf(i2)
                        dma("sp", xs[:], xT[c * 128:(c + 1) * 128, tok0 + t * TT: tok0 + (t + 1) * TT], [], [xa, xb])
                        tt(X1[:, c, t * TT:(t + 1) * TT], PS[bz][:], xs[:], ALU.add, [("ps", bz), xa, xb], [("X1", c, t)])
            stages.append((spec, p6))

        def p6n():
            P.barrier()
            dump("x1", X1[:], [128, 8, S], F32, [("X1", c, t) for c in range(8) for t in range(NT)])
            for t in range(NT):
                b = nbank()
                for k in range(8):
                    sq, sqt = TH[2][k % 2], ("T", 2, k % 2)
                    act(sq[:], X1[:, k, t * TT:(t + 1) * TT], AF.Square, [("X1", k, t)], [sqt])
                    P.add("pe", lambda pe, k=k, sq=sq, b=b: pe.matmul(PS[b][:], lhsT=ones_b, rhs=sq[:], start=(k == 0), stop=(k == 7)),
                          reads=[sqt, ("CB",)], writes=[("ps", b)])
                rt, ra, rb = rstd_tile(b, 1.0 / D, 1e-6, 3, "sqrt")
                for k in range(8):
                    stt(R1[:, k, t * TT:(t + 1) * TT], X1[:, k, t * TT:(t + 1) * TT], g_ffn(k), rt[:], ALU.mult, ALU.mult,
                        [("X1", k, t), ra, rb, ("CF",)], [("R1", k, t)])
        stages.append((None, p6n))

        for g in range(NT // TG):
            tiles = [g * TG + x for x in range(TG)]
            for fs in range(11):
                spec = spec_cols(w_fg_v, [(fs * 256, 256)]) + [
                    (lambda j: WS[j][:, :, 256:512], w_fu_v[:, :, fs * 256:(fs + 1) * 256])]

                def p7a(slot, fs=fs, tiles=tiles):
                    for fc in range(2):
                        f = fs * 2 + fc
                        for ti, t in enumerate(tiles):
                            bg, bu = nbank(), nbank()
                            mm(bg, [(WS[slot][:, k, fc * 128:(fc + 1) * 128], R1[:, k, t * TT:(t + 1) * TT]) for k in range(8)],
                               [("W", slot)] + r1tok(t))
                            mm(bu, [(WS[slot][:, k, 256 + fc * 128:256 + (fc + 1) * 128], R1[:, k, t * TT:(t + 1) * TT]) for k in range(8)],
                               [("W", slot)] + r1tok(t))
                            sl, sla, slb = tf((f * TG + ti) % 2)
                            act(sl[:], PS[bg][:], AF.Silu, [("ps", bg)], [sla, slb])
                            tt(FF[:, f, ti * TT:(ti + 1) * TT], PS[bu][:], sl[:], ALU.mult, [("ps", bu), sla, slb], [("FF", f, ti)])
                stages.append((spec, p7a))
            for c in range(8):
                spec = spec_down(c)

                def p7b(slot, c=c, tiles=tiles):
                    for ti, t in enumerate(tiles):
                        bz = nbank()
                        mm(bz, [(WD[slot][:, f, :], FF[:, f, ti * TT:(ti + 1) * TT]) for f in range(FC)],
                           [("W", slot)] + [("FF", f, ti) for f in range(FC)])
                        tt(X1[:, c, t * TT:(t + 1) * TT], PS[bz][:], X1[:, c, t * TT:(t + 1) * TT], ALU.add,
                           [("ps", bz), ("X1", c, t)], [("X1", c, t)])
                stages.append((spec, p7b))

            def p7n(tiles=tiles):
                for t in tiles:
                    b = nbank()
                    for k in range(8):
                        sq, sqt = TH[2][k % 2], ("T", 2, k % 2)
                        act(sq[:], X1[:, k, t * TT:(t + 1) * TT], AF.Square, [("X1", k, t)], [sqt])
                        P.add("pe", lambda pe, k=k, sq=sq, b=b: pe.matmul(PS[b][:], lhsT=ones_b, rhs=sq[:], start=(k == 0), stop=(k == 7)),
                              reads=[sqt, ("CB",)], writes=[("ps", b)])
                    rt, ra, rb = rstd_tile(b, 1.0 / D, 1e-6, 3, "sqrt")
                    for k in range(8):
                        stt(X1[:, k, t * TT:(t + 1) * TT], X1[:, k, t * TT:(t + 1) * TT], g_fin(k), rt[:], ALU.mult, ALU.mult,
                            [("X1", k, t), ra, rb, ("CF",)], [("X1", k, t)])
                    dma("sp", outT_v[:, :, tok0 + t * TT: tok0 + (t + 1) * TT], X1[:, :, t * TT:(t + 1) * TT],
                        [("X1", k, t) for k in range(8)], [("OUT", s, t)])
            stages.append((None, p7n))
        return stages

    all_stages = []
    for s in range(NSEQ):
        all_stages.extend(seq_stages(s))
    widx = [i for i, (sp_, _) in enumerate(all_stages) if sp_ is not None]
    slots = {}
    issued = 0
    for i, (sp_, body) in enumerate(all_stages):
        r = sum(1 for x in widx if x < i)
        while issued < len(widx) and issued <= r + 1:
            wi = widx[issued]
            slots[wi] = fill(all_stages[wi][0])
            issued += 1
        if sp_ is None:
            body()
        else:
            body(slots[i])

    P.analyze()
    P.emit()
    return nc, dbg_out, P


def _host_consts(inp, S):
    cf = np.zeros((128, NCF), np.float32)

    def pk(v):
        return np.ascontiguousarray(np.asarray(v, np.float32).reshape(-1, 128).T)
    cf[:, 0:8] = pk(inp["g_mix"][0])
    cf[:, 8:24] = pk(inp["b_gate"][0])
    wdw = np.asarray(inp["w_dw"][0], np.float32)
    cf[:, 24:272] = wdw.T.reshape(8, 128, CW).transpose(1, 0, 2).reshape(128, 8 * CW)
    cf[:, 272:280] = pk(inp["b_dw"][0])
    cf[:, 280:288] = pk(inp["g_conv_ln"][0])
    cf[:, 288:296] = pk(inp["b_conv_ln"][0])
    cf[:, 296:304] = pk(inp["g_ffn"][0])
    cf[:, 304:312] = pk(inp["g_final"])
    cf[:, 312] = np.asarray(inp["g_subln"][0], np.float32)
    for i, nm in enumerate(("lambda_q1", "lambda_k1", "lambda_q2", "lambda_k2")):
        cf[0:64, 313 + i] = np.asarray(inp[nm][0], np.float32)
    cb = np.zeros((128, 1024), np.float32)
    cb[:, 0:128] = 1.0
    for o in (0, 64):
        for i in range(8):
            cb[o + i + 8, 128 + o + i] = -1.0
            cb[o + i, 128 + o + 8 + i] = 1.0
    kk = np.arange(128)
    cb[:, 256:384] = np.eye(128, dtype=np.float32)
    maskneg = np.where(kk[:, None] > kk[None, :], -30000.0, 0.0).astype(np.float32)
    for h in range(2):
        for p in range(64):
            cb[64 * h + p, 512 + p] = 1.0
            cb[64 * h + p, 512 + 128 + p + 64] = 1.0
        cb[64 * h:64 * h + 64, 768:896] = maskneg[0:64]
        cb[64 * h:64 * h + 64, 896:1024] = maskneg[64:128]
    cb = cb.astype(ml_dtypes.bfloat16)
    inv_freq = (np.float32(500000.0) ** (-np.arange(0, 16, 2, dtype=np.float32) / np.float32(16))).astype(np.float32)
    pos = np.arange(S, dtype=np.float32)
    ang = (pos[:, None] * inv_freq[None, :]).astype(np.float32)
    cs, sn = np.cos(ang).astype(np.float32).T, np.sin(ang).astype(np.float32).T
    rope = np.zeros((128, 2, S), np.float32)
    rope[:, 0, :] = 1.0
    for o in (0, 64):
        rope[o:o + 8, 0] = cs
        rope[o + 8:o + 16, 0] = cs
        rope[o:o + 8, 1] = sn
        rope[o + 8:o + 16, 1] = sn
    return cf, cb, rope


def make_in_maps(inp, S, NSEQ, ncores):
    x = np.asarray(inp["x"], np.float32)
    cf, cb, rope = _host_consts(inp, S)
    shared = {
        "w_in": np.ascontiguousarray(inp["w_in"][0], dtype=np.float32),
        "w_conv_pw": np.ascontiguousarray(inp["w_conv_pw"][0], dtype=np.float32),
        "w_attn_pw": np.ascontiguousarray(inp["w_attn_pw"][0], dtype=np.float32),
        "w_out": np.ascontiguousarray(inp["w_out"][0], dtype=np.float32),
        "w_ffn_gate": np.ascontiguousarray(inp["w_ffn_gate"][0], dtype=np.float32),
        "w_ffn_up": np.ascontiguousarray(inp["w_ffn_up"][0], dtype=np.float32),
        "w_ffn_down": np.ascontiguousarray(inp["w_ffn_down"][0], dtype=np.float32),
        "cf32": cf, "cb16": cb, "rope": rope,
    }
    maps = []
    for c in range(ncores):
        xs = x[c * NSEQ:(c + 1) * NSEQ, :S]
        xT = np.ascontiguousarray(xs.reshape(NSEQ * S, D).T)
        m = dict(shared)
        m["xT"] = xT
        maps.append(m)
    return maps


def kernel(**inputs):
    S, NSEQ, NCORES = 2048, 2, 8
    nc, _, _ = build_nc(S, NSEQ)
    in_maps = make_in_maps(inputs, S, NSEQ, NCORES)
    res = run_bass_kernel_spmd(nc, in_maps, core_ids=list(range(NCORES)))
    outs = []
    for c in range(NCORES):
        oT = np.asarray(res.results[c]["outT"], np.float32)
        outs.append(oT.T.reshape(NSEQ, S, D))
    return np.ascontiguousarray(np.concatenate(outs, axis=0), dtype=np.float32)
```

```python
import math
import numpy as np
import ml_dtypes
import concourse.bass as bass
import concourse.mybir as mybir
from concourse.bass_utils import run_bass_kernel_spmd

F32 = mybir.dt.float32
BF16 = mybir.dt.bfloat16
AF = mybir.ActivationFunctionType
ALU = mybir.AluOpType

D = 1024
KC = 8
NH = 8
DFF = 2816
FC = 22
CW = 31
IN_COLS = 7168
C0, C1, C2, C3 = 2048, 3072, 4096, 5120
LAMBDA_INIT = 0.8 - 0.6 * math.exp(-0.3 * 0)
NCF = 320
TT = 512
NDUM_FIN = 0
NPE_TAPS = 25
NDUM_DIAG = 0
DBG_TRI = False

ENGS = ("pe", "act", "dve", "pool", "sp")
SEM_LIMIT = 30000
NDMA_SEM = 8


class Op:
    __slots__ = ("eng", "fn", "reads", "writes", "dma", "deps", "ord", "gen", "slot", "target", "need_inc", "bar")

    def __init__(self, eng, fn, reads, writes, dma):
        self.eng = eng
        self.fn = fn
        self.reads = reads
        self.writes = writes
        self.dma = dma
        self.deps = ()
        self.ord = 0
        self.gen = 0
        self.slot = 0
        self.target = 0
        self.need_inc = False
        self.bar = False


class Prog:
    def __init__(self, nc):
        self.nc = nc
        self.ops = []

    def add(self, eng, fn, reads=(), writes=(), dma=False):
        self.ops.append(Op(eng, fn, tuple(reads), tuple(writes), dma))

    def barrier(self):
        o = Op("bar", None, (), (), False)
        o.bar = True
        self.ops.append(o)

    def analyze(self):
        ops = self.ops
        tok_w, tok_r = {}, {}
        last_c = {}
        last_d = {}
        bar_deps = {}
        dma_cnt = {"sp": 0, "pool": 0}
        for i, op in enumerate(ops):
            if op.bar:
                deps = set(last_c.values())
                for (q, s), j in last_d.items():
                    if q != "pool":
                        deps.add(j)
                for e in ENGS:
                    if e != "pool":
                        bar_deps.setdefault(e, set()).update(deps)
                continue
            deps = set()
            for t in op.reads:
                deps.update(tok_w.get(t, {}).values())
            for t in op.writes:
                deps.update(tok_w.get(t, {}).values())
                deps.update(tok_r.get(t, {}).values())
            if op.eng in bar_deps:
                deps.update(bar_deps.pop(op.eng))
            if op.dma:
                n = dma_cnt[op.eng]
                dma_cnt[op.eng] = n + 1
                op.slot = n % NDMA_SEM
                prev = last_d.get((op.eng, op.slot))
                if prev is not None:
                    deps.add(prev)
                    op.target = ops[prev].target + 16
                else:
                    op.target = 16
                last_d[(op.eng, op.slot)] = i
                key = ("d", i)
            else:
                last_c[op.eng] = i
                key = op.eng
            for t in op.reads:
                tok_r.setdefault(t, {})[key] = i
            for t in op.writes:
                if tok_r.get(t):
                    tok_w[t] = {key: i}
                    tok_r[t] = {}
                else:
                    tok_w.setdefault(t, {})[key] = i
            deps.discard(i)
            best = {}
            out = []
            for d in deps:
                dop = ops[d]
                if dop.dma:
                    out.append(d)
                else:
                    if dop.eng == "pe" and op.eng == "pe" and not op.dma:
                        continue
                    if d > best.get(dop.eng, -1):
                        best[dop.eng] = d
            out.extend(best.values())
            op.deps = tuple(sorted(out))
            for d in op.deps:
                ops[d].need_inc = True
        cnt = {e: 0 for e in ENGS}
        gen = {e: 0 for e in ENGS}
        for op in ops:
            if op.bar or op.dma or not op.need_inc:
                continue
            if cnt[op.eng] >= SEM_LIMIT:
                cnt[op.eng] = 0
                gen[op.eng] += 1
            cnt[op.eng] += 1
            op.ord = cnt[op.eng]
            op.gen = gen[op.eng]
        self.ngen = {e: gen[e] + 1 for e in ENGS}

    def emit(self):
        nc = self.nc
        engobj = {"pe": nc.tensor, "act": nc.scalar, "dve": nc.vector, "pool": nc.gpsimd, "sp": nc.sync}
        csem = {e: [nc.alloc_semaphore(f"c_{e}_{g}") for g in range(self.ngen[e])] for e in ENGS}
        dsem = {q: [nc.alloc_semaphore(f"d_{q}_{s}") for s in range(NDMA_SEM)] for q in ("sp", "pool")}
        waited = {e: {} for e in ENGS}
        ops = self.ops
        final_d = {}
        nwait = 0
        for op in ops:
            if op.bar:
                continue
            E = engobj[op.eng]
            w = waited[op.eng]
            for d in op.deps:
                dop = ops[d]
                if dop.dma:
                    key = ("d", dop.eng, dop.slot)
                    val = dop.target
                    sem = dsem[dop.eng][dop.slot]
                else:
                    key = ("c", dop.eng, dop.gen)
                    val = dop.ord
                    sem = csem[dop.eng][dop.gen]
                if w.get(key, 0) >= val:
                    continue
                E.wait_ge(sem, val)
                nwait += 1
                w[key] = val
            ins = op.fn(E)
            if op.dma:
                ins.then_inc(dsem[op.eng][op.slot], 16)
                final_d[(op.eng, op.slot)] = op.target
            elif op.need_inc:
                ins.then_inc(csem[op.eng][op.gen], 1)
        for (q, s), tgt in sorted(final_d.items()):
            nc.sync.wait_ge(dsem[q][s], tgt)
        self.nwait = nwait


def build_nc(S=2048, NSEQ=2, dbg=()):
    NT = S // TT
    TG = min(2, NT)
    NTOK = S * NSEQ
    nc = bass.Bass("TRN2", target_bir_lowering=False)
    P = Prog(nc)

    def din(name, shape, dt=F32):
        return nc.dram_tensor(name, shape, dt, kind="ExternalInput").ap()

    xT = din("xT", [D, NTOK])
    w_in = din("w_in", [D, IN_COLS])
    w_pw = din("w_conv_pw", [D, D])
    w_ap = din("w_attn_pw", [D, D])
    w_out = din("w_out", [D, D])
    w_fg = din("w_ffn_gate", [D, DFF])
    w_fu = din("w_ffn_up", [D, DFF])
    w_fd = din("w_ffn_down", [DFF, D])
    cf_d = din("cf32", [128, NCF])
    cb_d = din("cb16", [128, 1024], BF16)
    rope_d = din("rope", [128, 2, S])
    outT = nc.dram_tensor("outT", [D, NTOK], F32, kind="ExternalOutput").ap()
    dbg_out = {}

    w_in_v = w_in.rearrange("(k p) c -> p k c", p=128)
    w_pw_v = w_pw.rearrange("(k p) c -> p k c", p=128)
    w_ap_v = w_ap.rearrange("(k p) c -> p k c", p=128)
    w_out_v = w_out.rearrange("(k p) c -> p k c", p=128)
    w_fg_v = w_fg.rearrange("(k p) c -> p k c", p=128)
    w_fu_v = w_fu.rearrange("(k p) c -> p k c", p=128)
    w_fd_v = w_fd.rearrange("(f p) c -> p f c", p=128)
    xT_v = xT.rearrange("(k p) t -> p k t", p=128)
    outT_v = outT.rearrange("(k p) t -> p k t", p=128)

    def sb(name, shape, dt, off):
        return nc.alloc_sbuf_tensor_at(name, shape, dt, offset=off)

    BASE = 24576 - 256
    SB2 = S * 2 * 8
    o_R1 = BASE
    o_R2 = o_R1 + SB2
    o_A = o_R2 + SB2
    A_SZ = 49152
    o_R3 = o_A + A_SZ
    o_W = o_R3 + SB2
    NW = 3
    o_ROPE = o_W + NW * 8192
    o_T = o_ROPE + 2 * S * 4
    assert o_T + 16384 <= 229376 - 256, o_T
    R1 = sb("R1", [128, 8, S], BF16, o_R1)
    R2 = sb("R2", [128, 8, S], BF16, o_R2)
    R3 = sb("R3", [128, 8, S], BF16, o_R3)
    X1 = sb("X1", [128, 8, S], F32, o_R2)
    assert 8 * S * 4 <= SB2 + 32768
    FF = sb("FF", [128, FC, TG * TT], BF16, o_A + 32768)
    assert FC * TG * TT * 2 <= 16384 + SB2 or S < 2048
    QT = sb("QT", [128, 4, S], BF16, o_A)
    KT = sb("KT", [128, 4, S], BF16, o_A + 16384)
    VV = sb("VV", [128, S // 128, 512], BF16, o_A + 32768)
    XS = [sb(f"XS{i}", [128, 8, TT], F32, o_A + i * 16384) for i in range(2)]
    UL = 30 + S
    ub = ((UL * 2 + 31) // 32) * 32
    U16 = [sb(f"U16_{i}", [128, UL], BF16, o_A + i * ub) for i in range(2)]
    DG = [sb(f"DG{i}", [128, CW, 128], BF16, o_A + 2 * ub + i * CW * 256) for i in range(2)]
    assert 2 * ub + 2 * CW * 256 <= A_SZ
    WS = [sb(f"WS{j}", [128, 8, 512], BF16, o_W + j * 8192) for j in range(NW)]
    WD = [sb(f"WD{j}", [128, FC, 128], BF16, o_W + j * 8192) for j in range(NW)]
    RC = sb("RC", [128, S], F32, o_ROPE)
    RS = sb("RS", [128, S], F32, o_ROPE + S * 4)
    TF = [sb(f"TF{i}", [128, TT], F32, o_T + i * 2048) for i in range(8)]
    TH = [[sb(f"TH{i}_{h}", [128, TT], BF16, o_T + i * 2048 + h * 1024) for h in range(2)] for i in range(8)]

    PT2 = [sb(f"PT2_{i}", [128, 2, TT], BF16, o_T + i * 2048) for i in range(2)]
    LT = sb("LT", [128, 2, TT], F32, o_T + 2 * 2048)

    def tf(i):
        return TF[i], ("T", i, 0), ("T", i, 1)

    AUX = 16768
    CF = sb("CF", [128, NCF], F32, AUX)
    CB = sb("CB", [128, 1024], BF16, AUX + 1280)
    LP = sb("LP", [128, 8], F32, AUX + 1280 + 2048)
    RSB = sb("RSB", [128, TT], F32, AUX + 1280 + 2048 + 64)
    PT2.append(sb("PT2_2", [128, 2, TT], BF16, AUX + 1280 + 2048 + 64 + 2048))
    assert AUX + 1280 + 2048 + 64 + 4096 <= BASE
    ONESF = sb("ONESF", [128, 128], F32, o_T)
    ones_b = CB[:, 0:128]
    pm_b = CB[:, 128:256]
    ident_b = CB[:, 256:384]
    zeros_b = CB[:, 384:512]
    LM = lambda h, a: CB[64 * h:64 * h + 64, 512 + a * 128:512 + (a + 1) * 128]
    RM = lambda h, a: CB[64 * h:64 * h + 64, 768 + a * 128:768 + (a + 1) * 128]
    g_mix = lambda k: CF[:, k:k + 1]
    b_gate = lambda c: CF[:, 8 + c:9 + c]
    w_dw = lambda c, j: CF[:, 24 + c * CW + j:25 + c * CW + j]
    b_dw = lambda c: CF[:, 272 + c:273 + c]
    g_ln = lambda c: CF[:, 280 + c:281 + c]
    b_ln = lambda c: CF[:, 288 + c:289 + c]
    g_ffn = lambda k: CF[:, 296 + k:297 + k]
    g_fin = lambda k: CF[:, 304 + k:305 + k]
    neglam = LP[:, 4:5]
    gsubs = LP[:, 5:6]

    PSA = nc.alloc_psum_tensor("psa", [128, 8, TT], F32)

    class _B:
        def __init__(self, b):
            self.b = b

        def __getitem__(self, idx):
            if not isinstance(idx, tuple):
                idx = (idx,)
            return PSA[(idx[0], self.b) + tuple(idx[1:])]
    PS = [_B(i) for i in range(8)]
    bank_ctr = [0]

    def nbank():
        b = bank_ctr[0] % 8
        bank_ctr[0] += 1
        return b

    def mm(bank, pairs, reads, ncols=None, c0=0):
        out_ap = PS[bank][:, c0:(c0 + ncols)] if ncols is not None else PS[bank][:]

        def fn(pe, pairs=pairs, out_ap=out_ap):
            n = len(pairs)
            ins = None
            for i, (l, r) in enumerate(pairs):
                ins = pe.matmul(out_ap, lhsT=l, rhs=r, start=(i == 0), stop=(i == n - 1))
            return ins
        P.add("pe", fn, reads=reads, writes=[("ps", bank)])

    def act(out, in_, func, reads, writes, bias=None, scale=None):
        kw = {}
        if bias is not None:
            kw["bias"] = bias
        if scale is not None:
            kw["scale"] = scale
        P.add("act", lambda e, kw=kw: e.activation(out=out, in_=in_, func=func, **kw), reads=reads, writes=writes)

    def tt(out, in0, in1, op, reads, writes, eng="dve"):
        P.add(eng, lambda e: e.tensor_tensor(out=out, in0=in0, in1=in1, op=op), reads=reads, writes=writes)

    def stt(out, in0, scalar, in1, op0, op1, reads, writes):
        P.add("dve", lambda e: e.scalar_tensor_tensor(out=out, in0=in0, scalar=scalar, in1=in1, op0=op0, op1=op1),
              reads=reads, writes=writes)

    def ts(out, in0, s1, s2, op0, op1, reads, writes):
        if s2 is None:
            P.add("dve", lambda e: e.tensor_scalar(out=out, in0=in0, scalar1=s1, scalar2=None, op0=op0),
                  reads=reads, writes=writes)
        else:
            P.add("dve", lambda e: e.tensor_scalar(out=out, in0=in0, scalar1=s1, scalar2=s2, op0=op0, op1=op1),
                  reads=reads, writes=writes)

    def recip(out, in_, reads, writes):
        P.add("dve", lambda e: e.reciprocal(out=out, in_=in_), reads=reads, writes=writes)

    def dma(q, out, in_, reads, writes):
        P.add(q, lambda e: e.dma_start(out=out, in_=in_), reads=reads, writes=writes, dma=True)

    def dump(name, ap, shape, dt, reads):
        if name not in dbg:
            return
        t = nc.dram_tensor("dbg_" + name, shape, dt, kind="ExternalOutput").ap()
        dbg_out[name] = t
        dma("sp", t, ap, reads, [("dbg", name)])

    def rstd_tile(bank, scale, eps, slot, mode):
        t, ta, tb = tf(slot)
        act(t[:], PS[bank][:], AF.Ln, [("ps", bank), ("LPe",)], [ta, tb], bias=eps_ap(eps), scale=scale)
        act(t[:], t[:], AF.Exp, [ta, tb], [ta, tb], scale=-0.5)
        return t, ta, tb

    eps_cols = {1e-6: 6, 1e-5: 7}

    def eps_ap(eps):
        c = eps_cols[eps]
        return LP[:, c:c + 1]

    dma("sp", CF[:], cf_d, [], [("CF",)])
    dma("sp", CB[:], cb_d, [], [("CB",)])
    P.add("dve", lambda e: e.memset(ONESF[:], 1.0), writes=[("T", 0, 0)])
    P.add("dve", lambda e: e.memset(LP[:, 6:7], 1e-6), writes=[("LPe",)])
    P.add("dve", lambda e: e.memset(LP[:, 7:8], 1e-5), writes=[("LPe",)])
    tt(LP[:, 0:1], CF[:, 313:314], CF[:, 314:315], ALU.mult, [("CF",)], [("LP0",)])
    tt(LP[:, 1:2], CF[:, 315:316], CF[:, 316:317], ALU.mult, [("CF",)], [("LP0",)])
    b0 = nbank()
    P.add("pe", lambda pe: pe.matmul(PS[b0][:, 0:2], lhsT=ONESF[:], rhs=LP[:, 0:2], start=True, stop=True),
          reads=[("T", 0, 0), ("LP0",)], writes=[("ps", b0)])
    act(LP[:, 2:4], PS[b0][:, 0:2], AF.Exp, [("ps", b0)], [("LP1",)])
    tt(LP[:, 4:5], LP[:, 3:4], LP[:, 2:3], ALU.subtract, [("LP1",)], [("LP2",)])
    ts(LP[:, 4:5], LP[:, 4:5], -LAMBDA_INIT, None, ALU.add, None, [("LP2",)], [("LP2",)])
    ts(LP[:, 5:6], CF[:, 312:313], 1.0 - LAMBDA_INIT, None, ALU.mult, None, [("CF",)], [("LP3",)])
    CONST_R = [("CF",), ("CB",), ("ROPE",), ("LP2",), ("LP3",), ("LPe",)]

    wstate = {"n": 0}

    def fill(spec):
        j = wstate["n"] % NW
        wstate["n"] += 1
        for (dstf, src) in spec:
            dma("pool", dstf(j), src, [], [("W", j)])
        return j

    def spec_cols(view, ranges):
        sp = []
        o = 0
        for (c0, n) in ranges:
            sp.append((lambda j, o=o, n=n: WS[j][:, :, o:o + n], view[:, :, c0:c0 + n]))
            o += n
        return sp

    def spec_down(c):
        return [(lambda j: WD[j][:], w_fd_v[:, :, c * 128:(c + 1) * 128])]

    def r1tok(t):
        return [("R1", k, t) for k in range(8)]

    def r2tok(t):
        return [("R2", k, t) for k in range(8)]

    def r3tok(t):
        return [("R3", k, t) for k in range(8)]

    def seq_stages(s):
        stages = []
        tok0 = s * S

        def p1():
            P.barrier()
            for t in range(NT):
                xs = XS[t % 2]
                xtok = ("XS", t % 2)
                dma("sp", xs[:], xT_v[:, :, tok0 + t * TT: tok0 + (t + 1) * TT], [], [xtok])
                b = nbank()
                for k in range(8):
                    sq = TH[0][k % 2]
                    sqt = ("T", 0, k % 2)
                    act(sq[:], xs[:, k, :], AF.Square, [xtok], [sqt])
                    P.add("pe", lambda pe, k=k, sq=sq, b=b: pe.matmul(PS[b][:], lhsT=ones_b, rhs=sq[:], start=(k == 0), stop=(k == 7)),
                          reads=[sqt, ("CB",)], writes=[("ps", b)])
                rt, ra, rb = rstd_tile(b, 1.0 / D, 1e-6, 1, "sqrt")
                for k in range(8):
                    stt(R1[:, k, t * TT:(t + 1) * TT], xs[:, k, :], g_mix(k), rt[:], ALU.mult, ALU.mult,
                        [xtok, ra, rb, ("CF",)], [("R1", k, t)])
            if s == 0:
                dma("sp", RC[:], rope_d[:, 0, :], [], [("ROPE",)])
                dma("sp", RS[:], rope_d[:, 1, :], [], [("ROPE",)])
            dump("hT", R1[:], [128, 8, S], BF16, [x for t in range(NT) for x in r1tok(t)])
        stages.append((None, p1))

        for j2 in range(4):
            spec = spec_cols(w_in_v, [(j2 * 256, 256), (1024 + j2 * 256, 256)])

            def p2a(slot, j2=j2):
                if j2 == 0:
                    P.barrier()
                    for i in range(2):
                        P.add("dve", lambda e, i=i: e.memset(U16[i][:, 0:30], 0.0), writes=[("U16pad", i)])
                for cc in range(2):
                    c = j2 * 2 + cc
                    ui = c % 2
                    for j in range(NPE_TAPS):
                        P.add("dve", lambda e, j=j, c=c, ui=ui: e.tensor_scalar(out=DG[ui][:, j, :], in0=ident_b, scalar1=w_dw(c, j),
                                                                               scalar2=None, op0=ALU.mult),
                              reads=[("CB",), ("CF",)], writes=[("DG", ui)])
                    for t in range(NT):
                        ba, bg = nbank(), nbank()
                        mm(ba, [(WS[slot][:, k, cc * 128:(cc + 1) * 128], R1[:, k, t * TT:(t + 1) * TT]) for k in range(8)],
                           [("W", slot)] + r1tok(t))
                        mm(bg, [(WS[slot][:, k, 256 + cc * 128:256 + (cc + 1) * 128], R1[:, k, t * TT:(t + 1) * TT]) for k in range(8)],
                           [("W", slot)] + r1tok(t))
                        sg, sa, sb_ = tf(t % 2)
                        act(sg[:], PS[bg][:], AF.Sigmoid, [("ps", bg)], [sa, sb_])
                        tt(U16[ui][:, 30 + t * TT:30 + (t + 1) * TT], PS[ba][:], sg[:], ALU.mult,
                           [("ps", ba), sa, sb_], [("U16", ui, t)])
                    for t in range(NT):
                        bc = nbank()
                        rd = [("DG", ui), ("U16", ui, t), ("U16pad", ui)] + ([("U16", ui, t - 1)] if t > 0 else [])
                        mm(bc, [(DG[ui][:, j, :], U16[ui][:, t * TT + j:t * TT + j + TT]) for j in range(NPE_TAPS)], rd)
                        ca, caa, cab = tf(4 + (t % 2))
                        urd = [x for x in rd if x[0] != "DG"] + [("CF",)]
                        ts(ca[:], U16[ui][:, t * TT + NPE_TAPS:t * TT + NPE_TAPS + TT], w_dw(c, NPE_TAPS), None, ALU.mult, None,
                           urd, [caa, cab])
                        for j in range(NPE_TAPS + 1, CW):
                            stt(ca[:], U16[ui][:, t * TT + j:t * TT + j + TT], w_dw(c, j), ca[:], ALU.mult, ALU.add,
                                urd + [caa, cab], [caa, cab])
                        stt(R2[:, c, t * TT:(t + 1) * TT], PS[bc][:], b_dw(c), ca[:], ALU.add, ALU.add,
                            [("ps", bc), ("CF",), caa, cab], [("R2", c, t)])
            stages.append((spec, p2a))

        def p2ln():
            dump("cb", R2[:], [128, 8, S], BF16, [x for t in range(NT) for x in r2tok(t)])

            def ln_stats(t):
                bs, bq = nbank(), nbank()
                for c in range(8):
                    cbv = R2[:, c, t * TT:(t + 1) * TT]
                    sq = TH[2][c % 2]
                    sqt = ("T", 2, c % 2)
                    act(sq[:], cbv, AF.Square, [("R2", c, t)], [sqt])
                    P.add("pe", lambda pe, c=c, cbv=cbv, bs=bs: pe.matmul(PS[bs][:], lhsT=ones_b, rhs=cbv, start=(c == 0), stop=(c == 7)),
                          reads=[("R2", c, t), ("CB",)], writes=[("ps", bs)])
                    P.add("pe", lambda pe, c=c, sq=sq, bq=bq: pe.matmul(PS[bq][:], lhsT=ones_b, rhs=sq[:], start=(c == 0), stop=(c == 7)),
                          reads=[sqt, ("CB",)], writes=[("ps", bq)])
                m, ma, mb = tf(3 + 2 * (t % 2))
                q2, qa, qb = tf(4 + 2 * (t % 2))
                ts(m[:], PS[bs][:], 1.0 / D, None, ALU.mult, None, [("ps", bs)], [ma, mb])
                tt(q2[:], m[:], m[:], ALU.mult, [ma, mb], [qa, qb])
                stt(q2[:], PS[bq][:], 1.0 / D, q2[:], ALU.mult, ALU.subtract, [("ps", bq), qa, qb], [qa, qb])
                act(q2[:], q2[:], AF.Ln, [qa, qb, ("LPe",)], [qa, qb], bias=eps_ap(1e-5), scale=1.0)
                act(q2[:], q2[:], AF.Exp, [qa, qb], [qa, qb], scale=-0.5)
                negm = TH[7][t % 2]
                ts(negm[:], PS[bs][:], -1.0 / D, None, ALU.mult, None, [("ps", bs)], [("T", 7, t % 2)])
                return (negm, ("T", 7, t % 2), q2, qa, qb)

            def ln_norm(t, st):
                negm, nmt, q2, qa, qb = st
                for c in range(8):
                    cbv = R2[:, c, t * TT:(t + 1) * TT]
                    t1, t1a, t1b = tf(c % 2)
                    bc_ = nbank()
                    P.add("pe", lambda pe, cbv=cbv, negm=negm, bc_=bc_: (
                        pe.matmul(PS[bc_][:], lhsT=ident_b, rhs=cbv, start=True, stop=False),
                        pe.matmul(PS[bc_][:], lhsT=ident_b, rhs=negm[:], start=False, stop=True))[1],
                        reads=[("R2", c, t), nmt, ("CB",)], writes=[("ps", bc_)])
                    tt(t1[:], PS[bc_][:], q2[:], ALU.mult, [("ps", bc_), qa, qb], [t1a, t1b])
                    act(cbv, t1[:], AF.Silu, [t1a, t1b, ("CF",)], [("R2", c, t)], bias=b_ln(c), scale=g_ln(c))

            st = ln_stats(0)
            for t in range(NT):
                nxt = ln_stats(t + 1) if t + 1 < NT else None
                ln_norm(t, st)
                st = nxt
            dump("cs", R2[:], [128, 8, S], BF16, [x for t in range(NT) for x in r2tok(t)])
        stages.append((None, p2ln))

        for j2 in range(4):
            spec = spec_cols(w_pw_v, [(j2 * 256, 256)]) + [
                (lambda j: WS[j][:, :, 256:512], w_in_v[:, :, C3 + j2 * 256: C3 + (j2 + 1) * 256])]

            def p2b(slot, j2=j2):
                for cc in range(2):
                    c = j2 * 2 + cc
                    for t in range(NT):
                        by, bg = nbank(), nbank()
                        mm(by, [(WS[slot][:, k, cc * 128:(cc + 1) * 128], R2[:, k, t * TT:(t + 1) * TT]) for k in range(8)],
                           [("W", slot)] + r2tok(t))
                        mm(bg, [(WS[slot][:, k, 256 + cc * 128:256 + (cc + 1) * 128], R1[:, k, t * TT:(t + 1) * TT]) for k in range(8)],
                           [("W", slot)] + r1tok(t))
                        sg, sa, sb_ = tf((c * NT + t) % 2)
                        act(sg[:], PS[bg][:], AF.Sigmoid, [("ps", bg), ("CF",)], [sa, sb_], bias=b_gate(c))
                        tt(R3[:, c, t * TT:(t + 1) * TT], PS[by][:], sg[:], ALU.mult, [("ps", by), sa, sb_], [("R3", c, t)])
            stages.append((spec, p2b))

        def dump_m1():
            dump("m1", R3[:], [128, 8, S], BF16, [x for t in range(NT) for x in r3tok(t)])
        stages.append((None, dump_m1))

        for hg in range(2):
            for which in range(2):
                spec = spec_cols(w_in_v, [((C0 if which == 0 else C1) + hg * 512, 512)])

                def p3qk(slot, hg=hg, which=which):
                    if hg == 0 and which == 0:
                        P.barrier()
                    dst = QT if which == 0 else KT
                    dn = "QT" if which == 0 else "KT"
                    for hh in range(4):
                        for t in range(NT):
                            bz = nbank()
                            mm(bz, [(WS[slot][:, k, hh * 128:(hh + 1) * 128], R1[:, k, t * TT:(t + 1) * TT]) for k in range(8)],
                               [("W", slot)] + r1tok(t))
                            i2 = (hh * NT + t) % 2
                            qb_ = TH[0][i2]
                            qbt = ("T", 0, i2)
                            act(qb_[:], PS[bz][:], AF.Copy, [("ps", bz)], [qbt])
                            bp = nbank()
                            P.add("pe", lambda pe, bp=bp, qb_=qb_: pe.matmul(PS[bp][:], lhsT=pm_b, rhs=qb_[:], start=True, stop=True),
                                  reads=[qbt, ("CB",)], writes=[("ps", bp)])
                            tS, tSa, tSb = tf(1 + i2)
                            qf, qfa, qfb = tf(3 + i2)
                            tt(tS[:], PS[bp][:], RS[:, t * TT:(t + 1) * TT], ALU.mult, [("ps", bp), ("ROPE",)], [tSa, tSb])
                            tt(qf[:], PS[bz][:], RC[:, t * TT:(t + 1) * TT], ALU.mult, [("ps", bz), ("ROPE",)], [qfa, qfb])
                            tt(dst[:, hh, t * TT:(t + 1) * TT], qf[:], tS[:], ALU.add, [tSa, tSb, qfa, qfb], [(dn, hh, t)])
                stages.append((spec, p3qk))
            spec = spec_cols(w_in_v, [(C2 + hg * 512, 512)])

            def p3v(slot, hg=hg):
                for tq in range(S // 128):
                    bv = nbank()
                    mm(bv, [(R1[:, k, tq * 128:(tq + 1) * 128], WS[slot][:, k, :]) for k in range(8)],
                       [("W", slot)] + r1tok(tq // 4))
                    P.add("act", lambda e, tq=tq, bv=bv: e.activation(out=VV[:, tq, :], in_=PS[bv][:], func=AF.Copy),
                          reads=[("ps", bv)], writes=[("VV", tq)])
                if hg == 0:
                    dump("qT", QT[:], [128, 4, S], BF16, [("QT", hh, t) for hh in range(4) for t in range(NT)])
                    dump("kT", KT[:], [128, 4, S], BF16, [("KT", hh, t) for hh in range(4) for t in range(NT)])
                    dump("vv", VV[:], [128, S // 128, 512], BF16, [("VV", tq) for tq in range(S // 128)])
            stages.append((spec, p3v))

            def p4(hg=hg):
                its = [(hh, i, j) for hh in range(4) for i in range(NT) for j in range(4 * (i + 1))]
                BO1, BO2, BD1, BD2 = 4, 5, 6, 7

                def geom(n):
                    hh, i, j = its[n]
                    jd = j - 4 * i
                    q0 = jd * 128 if jd > 0 else 0
                    return hh, i, j, jd, q0

                def emit_S(n):
                    hh, i, j, jd, q0 = geom(n)
                    b = 2 * (n % 2)
                    qs = slice(i * TT + q0, (i + 1) * TT)

                    def fn_s(pe, j=j, hh=hh, qs=qs, b=b, q0=q0, diag=(jd >= 0)):
                        pe.matmul(PS[b][:, q0:TT], lhsT=KT[0:64, hh, j * 128:(j + 1) * 128], rhs=QT[0:64, hh, qs],
                                  start=True, stop=not diag)
                        ins = pe.matmul(PS[b + 1][:, q0:TT], lhsT=KT[64:128, hh, j * 128:(j + 1) * 128], rhs=QT[64:128, hh, qs],
                                        start=True, stop=not diag)
                        if diag:
                            for a in range(2):
                                pe.matmul(PS[b][:, q0:q0 + 128], lhsT=LM(0, a), rhs=RM(0, a), start=False, stop=(a == 1))
                                ins = pe.matmul(PS[b + 1][:, q0:q0 + 128], lhsT=LM(1, a), rhs=RM(1, a), start=False, stop=(a == 1))
                        return ins
                    P.add("pe", fn_s, reads=[("KT", hh, j // 4), ("QT", hh, i), ("CB",)], writes=[("ps", b), ("ps", b + 1)])

                def emit_exp(n):
                    hh, i, j, jd, q0 = geom(n)
                    pi = n % 3
                    b = 2 * (n % 2)
                    ptok = [("T", pi, 0), ("T", pi, 1)] if pi < 2 else [("PT2x",)]
                    act(PT2[pi][:, :, q0:TT], PSA[:, b:b + 2, q0:TT], AF.Exp, [("ps", b), ("ps", b + 1)], ptok, scale=0.125)

                def emit_PV(n):
                    hh, i, j, jd, q0 = geom(n)
                    pi = n % 3
                    ptok = [("T", pi, 0), ("T", pi, 1)] if pi < 2 else [("PT2x",)]
                    vv = VV[:, j, hh * 128:(hh + 1) * 128]
                    first, lastj = (j == 0), (j == 4 * (i + 1) - 1)

                    def fn_pv(pe, vv=vv, pi=pi, q0=q0, first=first, lastj=lastj):
                        pe.matmul(PS[BO1][:, q0:TT], lhsT=vv, rhs=PT2[pi][:, 0, q0:TT], start=first, stop=lastj)
                        pe.matmul(PS[BD1][:, q0:TT], lhsT=ones_b, rhs=PT2[pi][:, 0, q0:TT], start=first, stop=lastj)
                        pe.matmul(PS[BO2][:, q0:TT], lhsT=vv, rhs=PT2[pi][:, 1, q0:TT], start=first, stop=lastj)
                        return pe.matmul(PS[BD2][:, q0:TT], lhsT=ones_b, rhs=PT2[pi][:, 1, q0:TT], start=first, stop=lastj)
                    P.add("pe", fn_pv, reads=[("VV", j)] + ptok + [("CB",)],
                          writes=[("ps", BO1), ("ps", BO2), ("ps", BD1), ("ps", BD2)])
                    return lastj

                ltok = [("T", 2, 0), ("T", 2, 1), ("T", 3, 0), ("T", 3, 1)]
                ctok = [("T", 4, 0), ("T", 4, 1), ("T", 5, 0), ("T", 5, 1)]
                C12 = sb(f"C12_{hg}_{s}", [128, 2, TT], F32, o_T + 4 * 2048)

                def finalize(n):
                    hh, i, j, jd, q0 = geom(n)
                    head = hg * 4 + hh
                    oo, ooa, oob = tf(6)
                    sq = TH[7][0]
                    act(LT[:], PSA[:, BD1:BD2 + 1, :], AF.Ln, [("ps", BD1), ("ps", BD2)], ltok)
                    P.add("dve", lambda e: e.tensor_copy(out=C12[:], in_=PSA[:, BO1:BO2 + 1, :]),
                          reads=[("ps", BO1), ("ps", BO2)], writes=ctok)
                    yield
                    act(LT[:], LT[:], AF.Exp, ltok, ltok, scale=-1.0)
                    tt(C12[:], C12[:], LT[:], ALU.mult, ctok + ltok, ctok)
                    stt(oo[:], C12[:, 1, :], neglam, C12[:, 0, :], ALU.mult, ALU.add, ctok + [("LP2",)], [ooa, oob])
                    tt(sq[:], oo[:], oo[:], ALU.mult, [ooa, oob], [("T", 7, 0)])
                    yield
                    yield
                    yield
                    bank = 2 * ((cur[0] + 1) % 2)
                    P.add("pe", lambda pe, bank=bank: pe.matmul(PS[bank][:], lhsT=ones_b, rhs=sq[:], start=True, stop=True),
                          reads=[("T", 7, 0), ("CB",)], writes=[("ps", bank)])
                    rs_, rsa, rsb = RSB, ("RSB", 0), ("RSB", 1)
                    act(rs_[:], PS[bank][:], AF.Ln, [("ps", bank), ("LPe",)], [rsa, rsb], bias=eps_ap(1e-6), scale=1.0 / 128)
                    yield
                    act(rs_[:], rs_[:], AF.Exp, [rsa, rsb], [rsa, rsb], scale=-0.5)
                    stt(R2[:, head, i * TT:(i + 1) * TT], oo[:], gsubs, rs_[:], ALU.mult, ALU.mult,
                        [ooa, oob, rsa, rsb, ("LP3",)], [("R2", head, i)])

                N = len(its)
                cur = [0]
                queue = []
                emit_S(0)
                if N > 1:
                    emit_S(1)
                emit_exp(0)
                def emit_dummy(n, k):
                    hh, i, j, jd, q0 = geom(n)
                    if j == 4 * (i + 1) - 1 or k == 0:
                        return
                    def fn_d(pe, hh=hh, k=k):
                        ins = None
                        for _ in range(k):
                            ins = pe.matmul(PS[BO1][:], lhsT=zeros_b, rhs=KT[:, hh, 0:TT], start=False, stop=False)
                        return ins
                    P.add("pe", fn_d, reads=[("KT", hh, 0), ("CB",)], writes=[("ps", BO1)])

                for n in range(N):
                    cur[0] = n
                    if n + 2 < N:
                        emit_S(n + 2)
                    lastj = emit_PV(n)
                    emit_dummy(n, NDUM_FIN if queue else (NDUM_DIAG if geom(n)[3] >= 0 else 0))
                    if n + 1 < N:
                        emit_exp(n + 1)
                    for g in list(queue):
                        try:
                            next(g)
                        except StopIteration:
                            queue.remove(g)
                    if lastj:
                        g = finalize(n)
                        next(g)
                        queue.append(g)
                while queue:
                    cur[0] += 1
                    for g in list(queue):
                        try:
                            next(g)
                        except StopIteration:
                            queue.remove(g)
            stages.append((None, p4))

        def dump_on():
            dump("on", R2[:], [128, 8, S], BF16, [x for t in range(NT) for x in r2tok(t)])
        stages.append((None, dump_on))

        for j2 in range(4):
            spec = spec_cols(w_ap_v, [(j2 * 256, 256)]) + [
                (lambda j: WS[j][:, :, 256:512], w_in_v[:, :, C3 + D + j2 * 256: C3 + D + (j2 + 1) * 256])]

            def p5(slot, j2=j2):
                for cc in range(2):
                    c = j2 * 2 + cc
                    for t in range(NT):
                        by, bg = nbank(), nbank()
                        mm(by, [(WS[slot][:, k, cc * 128:(cc + 1) * 128], R2[:, k, t * TT:(t + 1) * TT]) for k in range(8)],
                           [("W", slot)] + r2tok(t))
                        mm(bg, [(WS[slot][:, k, 256 + cc * 128:256 + (cc + 1) * 128], R1[:, k, t * TT:(t + 1) * TT]) for k in range(8)],
                           [("W", slot)] + r1tok(t))
                        i2 = (c * NT + t) % 2
                        sg, sa, sb_ = tf(i2)
                        tm, tma, tmb = tf(2 + i2)
                        act(sg[:], PS[bg][:], AF.Sigmoid, [("ps", bg), ("CF",)], [sa, sb_], bias=b_gate(8 + c))
                        tt(tm[:], PS[by][:], sg[:], ALU.mult, [("ps", by), sa, sb_], [tma, tmb])
                        tt(R3[:, c, t * TT:(t + 1) * TT], tm[:], R3[:, c, t * TT:(t + 1) * TT], ALU.add,
                           [tma, tmb, ("R3", c, t)], [("R3", c, t)])
            stages.append((spec, p5))

        def dump_mg():
            dump("mg", R3[:], [128, 8, S], BF16, [x for t in range(NT) for x in r3tok(t)])
        stages.append((None, dump_mg))

        for j2 in range(2):
            spec = spec_cols(w_out_v, [(j2 * 512, 512)])

            def p6(slot, j2=j2):
                if j2 == 0:
                    P.barrier()
                for cc in range(4):
                    c = j2 * 4 + cc
                    for t in range(NT):
                        bz = nbank()
                        mm(bz, [(WS[slot][:, k, cc * 128:(cc + 1) * 128], R3[:, k, t * TT:(t + 1) * TT]) for k in range(8)],
                           [("W", slot)] + r3tok(t))
                        i2 = (c * NT + t) % 4
                        xs, xa, xb = tf(i2)
                        dma("sp", xs[:], xT[c * 128:(c + 1) * 128, tok0 + t * TT: tok0 + (t + 1) * TT], [], [xa, xb])
                        tt(X1[:, c, t * TT:(t + 1) * TT], PS[bz][:], xs[:], ALU.add, [("ps", bz), xa, xb], [("X1", c, t)])
            stages.append((spec, p6))

        def p6n():
            P.barrier()
            dump("x1", X1[:], [128, 8, S], F32, [("X1", c, t) for c in range(8) for t in range(NT)])
            for t in range(NT):
                b = nbank()
                for k in range(8):
                    sq, sqt = TH[2][k % 2], ("T", 2, k % 2)
                    act(sq[:], X1[:, k, t * TT:(t + 1) * TT], AF.Square, [("X1", k, t)], [sqt])
                    P.add("pe", lambda pe, k=k, sq=sq, b=b: pe.matmul(PS[b][:], lhsT=ones_b, rhs=sq[:], start=(k == 0), stop=(k == 7)),
                          reads=[sqt, ("CB",)], writes=[("ps", b)])
                rt, ra, rb = rstd_tile(b, 1.0 / D, 1e-6, 3, "sqrt")
                for k in range(8):
                    stt(R1[:, k, t * TT:(t + 1) * TT], X1[:, k, t * TT:(t + 1) * TT], g_ffn(k), rt[:], ALU.mult, ALU.mult,
                        [("X1", k, t), ra, rb, ("CF",)], [("R1", k, t)])
        stages.append((None, p6n))

        for g in range(NT // TG):
            tiles = [g * TG + x for x in range(TG)]
            for fs in range(11):
                spec = spec_cols(w_fg_v, [(fs * 256, 256)]) + [
                    (lambda j: WS[j][:, :, 256:512], w_fu_v[:, :, fs * 256:(fs + 1) * 256])]

                def p7a(slot, fs=fs, tiles=tiles):
                    for fc in range(2):
                        f = fs * 2 + fc
                        for ti, t in enumerate(tiles):
                            bg, bu = nbank(), nbank()
                            mm(bg, [(WS[slot][:, k, fc * 128:(fc + 1) * 128], R1[:, k, t * TT:(t + 1) * TT]) for k in range(8)],
                               [("W", slot)] + r1tok(t))
                            mm(bu, [(WS[slot][:, k, 256 + fc * 128:256 + (fc + 1) * 128], R1[:, k, t * TT:(t + 1) * TT]) for k in range(8)],
                               [("W", slot)] + r1tok(t))
                            sl, sla, slb = tf((f * TG + ti) % 2)
                            act(sl[:], PS[bg][:], AF.Silu, [("ps", bg)], [sla, slb])
                            tt(FF[:, f, ti * TT:(ti + 1) * TT], PS[bu][:], sl[:], ALU.mult, [("ps", bu), sla, slb], [("FF", f, ti)])
                stages.append((spec, p7a))
            for c in range(8):
                spec = spec_down(c)

                def p7b(slot, c=c, tiles=tiles):
                    for ti, t in enumerate(tiles):
                        bz = nbank()
                        mm(bz, [(WD[slot][:, f, :], FF[:, f, ti * TT:(ti + 1) * TT]) for f in range(FC)],
                           [("W", slot)] + [("FF", f, ti) for f in range(FC)])
                        tt(X1[:, c, t * TT:(t + 1) * TT], PS[bz][:], X1[:, c, t * TT:(t + 1) * TT], ALU.add,
                           [("ps", bz), ("X1", c, t)], [("X1", c, t)])
                stages.append((spec, p7b))

            def p7n(tiles=tiles):
                for t in tiles:
                    b = nbank()
                    for k in range(8):
                        sq, sqt = TH[2][k % 2], ("T", 2, k % 2)
                        act(sq[:], X1[:, k, t * TT:(t + 1) * TT], AF.Square, [("X1", k, t)], [sqt])
                        P.add("pe", lambda pe, k=k, sq=sq, b=b: pe.matmul(PS[b][:], lhsT=ones_b, rhs=sq[:], start=(k == 0), stop=(k == 7)),
                              reads=[sqt, ("CB",)], writes=[("ps", b)])
                    rt, ra, rb = rstd_tile(b, 1.0 / D, 1e-6, 3, "sqrt")
                    for k in range(8):
                        stt(X1[:, k, t * TT:(t + 1) * TT], X1[:, k, t * TT:(t + 1) * TT], g_fin(k), rt[:], ALU.mult, ALU.mult,
                            [("X1", k, t), ra, rb, ("CF",)], [("X1", k, t)])
                    dma("sp", outT_v[:, :, tok0 + t * TT: tok0 + (t + 1) * TT], X1[:, :, t * TT:(t + 1) * TT],
                        [("X1", k, t) for k in range(8)], [("OUT", s, t)])
            stages.append((None, p7n))
        return stages

    all_stages = []
    for s in range(NSEQ):
        all_stages.extend(seq_stages(s))
    widx = [i for i, (sp_, _) in enumerate(all_stages) if sp_ is not None]
    slots = {}
    issued = 0
    for i, (sp_, body) in enumerate(all_stages):
        r = sum(1 for x in widx if x < i)
        while issued < len(widx) and issued <= r + 1:
            wi = widx[issued]
            slots[wi] = fill(all_stages[wi][0])
            issued += 1
        if sp_ is None:
            body()
        else:
            body(slots[i])

    P.analyze()
    P.emit()
    return nc, dbg_out, P


def _host_consts(inp, S):
    cf = np.zeros((128, NCF), np.float32)

    def pk(v):
        return np.ascontiguousarray(np.asarray(v, np.float32).reshape(-1, 128).T)
    cf[:, 0:8] = pk(inp["g_mix"][0])
    cf[:, 8:24] = pk(inp["b_gate"][0])
    wdw = np.asarray(inp["w_dw"][0], np.float32)
    cf[:, 24:272] = wdw.T.reshape(8, 128, CW).transpose(1, 0, 2).reshape(128, 8 * CW)
    cf[:, 272:280] = pk(inp["b_dw"][0])
    cf[:, 280:288] = pk(inp["g_conv_ln"][0])
    cf[:, 288:296] = pk(inp["b_conv_ln"][0])
    cf[:, 296:304] = pk(inp["g_ffn"][0])
    cf[:, 304:312] = pk(inp["g_final"])
    cf[:, 312] = np.asarray(inp["g_subln"][0], np.float32)
    for i, nm in enumerate(("lambda_q1", "lambda_k1", "lambda_q2", "lambda_k2")):
        cf[0:64, 313 + i] = np.asarray(inp[nm][0], np.float32)
    cb = np.zeros((128, 1024), np.float32)
    cb[:, 0:128] = 1.0
    for o in (0, 64):
        for i in range(8):
            cb[o + i + 8, 128 + o + i] = -1.0
            cb[o + i, 128 + o + 8 + i] = 1.0
    kk = np.arange(128)
    cb[:, 256:384] = np.eye(128, dtype=np.float32)
    maskneg = np.where(kk[:, None] > kk[None, :], -30000.0, 0.0).astype(np.float32)
    for h in range(2):
        for p in range(64):
            cb[64 * h + p, 512 + p] = 1.0
            cb[64 * h + p, 512 + 128 + p + 64] = 1.0
        cb[64 * h:64 * h + 64, 768:896] = maskneg[0:64]
        cb[64 * h:64 * h + 64, 896:1024] = maskneg[64:128]
    cb = cb.astype(ml_dtypes.bfloat16)
    inv_freq = (np.float32(500000.0) ** (-np.arange(0, 16, 2, dtype=np.float32) / np.float32(16))).astype(np.float32)
    pos = np.arange(S, dtype=np.float32)
    ang = (pos[:, None] * inv_freq[None, :]).astype(np.float32)
    cs, sn = np.cos(ang).astype(np.float32).T, np.sin(ang).astype(np.float32).T
    rope = np.zeros((128, 2, S), np.float32)
    rope[:, 0, :] = 1.0
    for o in (0, 64):
        rope[o:o + 8, 0] = cs
        rope[o + 8:o + 16, 0] = cs
        rope[o:o + 8, 1] = sn
        rope[o + 8:o + 16, 1] = sn
    return cf, cb, rope


def make_in_maps(inp, S, NSEQ, ncores):
    x = np.asarray(inp["x"], np.float32)
    cf, cb, rope = _host_consts(inp, S)
    shared = {
        "w_in": np.ascontiguousarray(inp["w_in"][0], dtype=np.float32),
        "w_conv_pw": np.ascontiguousarray(inp["w_conv_pw"][0], dtype=np.float32),
        "w_attn_pw": np.ascontiguousarray(inp["w_attn_pw"][0], dtype=np.float32),
        "w_out": np.ascontiguousarray(inp["w_out"][0], dtype=np.float32),
        "w_ffn_gate": np.ascontiguousarray(inp["w_ffn_gate"][0], dtype=np.float32),
        "w_ffn_up": np.ascontiguousarray(inp["w_ffn_up"][0], dtype=np.float32),
        "w_ffn_down": np.ascontiguousarray(inp["w_ffn_down"][0], dtype=np.float32),
        "cf32": cf, "cb16": cb, "rope": rope,
    }
    maps = []
    for c in range(ncores):
        xs = x[c * NSEQ:(c + 1) * NSEQ, :S]
        xT = np.ascontiguousarray(xs.reshape(NSEQ * S, D).T)
        m = dict(shared)
        m["xT"] = xT
        maps.append(m)
    return maps


def kernel(**inputs):
    S, NSEQ, NCORES = 2048, 2, 8
    nc, _, _ = build_nc(S, NSEQ)
    in_maps = make_in_maps(inputs, S, NSEQ, NCORES)
    res = run_bass_kernel_spmd(nc, in_maps, core_ids=list(range(NCORES)))
    outs = []
    for c in range(NCORES):
        oT = np.asarray(res.results[c]["outT"], np.float32)
        outs.append(oT.T.reshape(NSEQ, S, D))
    return np.ascontiguousarray(np.concatenate(outs, axis=0), dtype=np.float32)
```
